# Optimizing a Trainium2 kernel written in Bass

```python
import jax, jax.numpy as jnp
from jax import lax
import numpy as np

D_MODEL = 1024
BATCH = 2
SEQ = 8192
DEPTH = 1

D_MIX = D_MODEL
ATT_HEADS = 8
ATT_KV_HEADS = 2
ATT_HEAD_DIM = 64
WINDOW = 128
ATT_BLOCK = 128
GLA_HEADS = 4
GLA_DK = 64
GLA_DV = 128
GLA_GATE_RANK = 16
GLA_TAU = 16.0
GLA_CHUNK = 64
PEER_HEADS = 8
PEER_QDIM = 256
N_KEYS = 128
N_EXPERTS = N_KEYS * N_KEYS
PEER_TOPK = 16
PEER_BLOCK = 128
NORM_EPS = 1e-6

ATT_Q_W = ATT_HEADS * ATT_HEAD_DIM
ATT_KV_W = ATT_KV_HEADS * ATT_HEAD_DIM
GLA_QK_W = GLA_HEADS * GLA_DK
GLA_V_W = GLA_HEADS * GLA_DV
IN_SIZES = (ATT_Q_W, ATT_KV_W, ATT_KV_W, GLA_QK_W, GLA_QK_W, GLA_V_W, GLA_V_W, GLA_GATE_RANK)
IN_WIDTH = sum(IN_SIZES)

kernel_name = "hymba_swa_sink_gla_peer_adaln"


def rms_norm(x, w):
    xf = x.astype(jnp.float32)
    y = xf * lax.rsqrt(jnp.mean(xf * xf, axis=-1, keepdims=True) + NORM_EPS)
    return (y * w.astype(jnp.float32)).astype(x.dtype)


def sliding_window_attention(q, k, v, sinks):
    B, S = q.shape[0], q.shape[1]
    nb = S // ATT_BLOCK
    G = ATT_HEADS // ATT_KV_HEADS
    qb = q.reshape(B, nb, ATT_BLOCK, ATT_KV_HEADS, G, ATT_HEAD_DIM)

    def band(t):
        tb = t.reshape(B, nb, ATT_BLOCK, ATT_KV_HEADS, ATT_HEAD_DIM)
        prev = jnp.pad(tb, ((0, 0), (1, 0), (0, 0), (0, 0), (0, 0)))[:, :-1]
        return jnp.concatenate([prev, tb], axis=2)

    kb, vb = band(k), band(v)
    scores = jnp.einsum('bnqhgd,bnkhd->bnhgqk', qb, kb).astype(jnp.float32) * (ATT_HEAD_DIM ** -0.5)
    blk = jnp.arange(nb)[:, None, None]
    qpos = blk * ATT_BLOCK + jnp.arange(ATT_BLOCK)[None, :, None]
    kpos = (blk - 1) * ATT_BLOCK + jnp.arange(2 * ATT_BLOCK)[None, None, :]
    rel = qpos - kpos
    mask = (rel >= 0) & (rel < WINDOW) & (kpos >= 0)
    scores = jnp.where(mask[None, :, None, None], scores, -jnp.inf)
    sink = sinks.astype(jnp.float32).reshape(ATT_KV_HEADS, G)[None, None, :, :, None, None]
    sink = jnp.broadcast_to(sink, scores.shape[:-1] + (1,))
    probs = jax.nn.softmax(jnp.concatenate([scores, sink], axis=-1), axis=-1)[..., :-1]
    out = jnp.einsum('bnhgqk,bnkhd->bnqhgd', probs.astype(v.dtype), vb)
    return out.reshape(B, S, ATT_Q_W)


def gla_chunked(q, k, v, log_a):
    B, S = q.shape[0], q.shape[1]
    nc = S // GLA_CHUNK

    def chunks(t):
        return t.astype(jnp.float32).reshape(B, nc, GLA_CHUNK, GLA_HEADS, t.shape[-1]).transpose(1, 0, 3, 2, 4)

    qc = chunks(q) * (GLA_DK ** -0.5)
    kc, vc, gc = chunks(k), chunks(v), chunks(log_a)
    causal = jnp.tril(jnp.ones((GLA_CHUNK, GLA_CHUNK), dtype=bool))

    def step(state, inp):
        qt, kt, vt, gt = inp
        b = jnp.cumsum(gt, axis=2)
        inter = jnp.einsum('bhtd,bhde->bhte', qt * jnp.exp(b), state)
        diff = b[:, :, :, None, :] - b[:, :, None, :, :]
        decay = jnp.exp(jnp.where(causal[:, :, None], diff, -jnp.inf))
        att = jnp.einsum('bhtd,bhsd,bhtsd->bhts', qt, kt, decay)
        intra = jnp.einsum('bhts,bhse->bhte', att, vt)
        b_last = b[:, :, -1:, :]
        new_state = jnp.exp(b_last[:, :, 0, :, None]) * state + jnp.einsum(
            'bhsd,bhse->bhde', kt * jnp.exp(b_last - b), vt)
        return new_state, inter + intra

    state0 = jnp.zeros((B, GLA_HEADS, GLA_DK, GLA_DV), jnp.float32)
    _, out = lax.scan(step, state0, (qc, kc, vc, gc))
    return out.transpose(1, 0, 3, 2, 4).reshape(B, S, GLA_HEADS, GLA_DV)


def peer_ffn(h, wq, subkeys, u, v):
    B, S, D = h.shape
    q = (h @ wq).reshape(B, S, PEER_HEADS, 2, PEER_QDIM // 2)
    sub_scores = jnp.einsum('bshpk,hpnk->bshpn', q, subkeys).astype(jnp.float32)
    top_v, top_i = lax.top_k(sub_scores, PEER_TOPK)
    cand = (top_v[..., 0, :, None] + top_v[..., 1, None, :]).reshape(B, S, PEER_HEADS, PEER_TOPK * PEER_TOPK)
    cand_i = (top_i[..., 0, :, None] * N_KEYS + top_i[..., 1, None, :]).reshape(B, S, PEER_HEADS, PEER_TOPK * PEER_TOPK)
    best_v, best_pos = lax.top_k(cand, PEER_TOPK)
    expert = jnp.take_along_axis(cand_i, best_pos, axis=-1)
    gates = jax.nn.softmax(best_v, axis=-1).astype(h.dtype)
    nblk = (B * S) // PEER_BLOCK
    xs = h.reshape(nblk, PEER_BLOCK, D)
    es = expert.reshape(nblk, PEER_BLOCK, PEER_HEADS * PEER_TOPK)
    gs = gates.reshape(nblk, PEER_BLOCK, PEER_HEADS * PEER_TOPK)

    def block(args):
        xb, eb, gb = args
        ub = jnp.take(u, eb, axis=0)
        hid = jax.nn.gelu(jnp.einsum('tkd,td->tk', ub, xb), approximate=False) * gb
        vb = jnp.take(v, eb, axis=0)
        return jnp.einsum('tk,tkd->td', hid, vb)

    return lax.map(block, (xs, es, gs)).reshape(B, S, D)


def setup_inputs(seed: int = 0) -> dict:
    key = jax.random.key(seed)
    ks = jax.random.split(key, 20)
    f32 = jnp.float32
    nrm = lambda k, shape, s: jax.random.normal(k, shape, f32) * s
    L = DEPTH
    return {
        "x": nrm(ks[0], (BATCH, SEQ, D_MODEL), 1.0),
        "c": nrm(ks[1], (BATCH, D_MODEL), 1.0),
        "w_ada": nrm(ks[2], (L, D_MODEL, 6 * D_MODEL), 0.5 * D_MODEL ** -0.5),
        "b_ada": nrm(ks[3], (L, 6 * D_MODEL), 0.02),
        "norm1_w": 1.0 + nrm(ks[4], (L, D_MODEL), 0.02),
        "w_in": nrm(ks[5], (L, D_MODEL, IN_WIDTH), D_MODEL ** -0.5),
        "attn_sinks": nrm(ks[6], (L, ATT_HEADS), 0.5),
        "gla_gate_up": nrm(ks[7], (L, GLA_GATE_RANK, GLA_QK_W), GLA_GATE_RANK ** -0.5),
        "gla_gate_bias": nrm(ks[8], (L, GLA_QK_W), 0.1),
        "gla_norm_w": 1.0 + nrm(ks[9], (L, GLA_DV), 0.02),
        "w_out": nrm(ks[10], (L, D_MIX, D_MODEL), D_MIX ** -0.5),
        "norm2_w": 1.0 + nrm(ks[11], (L, D_MODEL), 0.02),
        "peer_wq": nrm(ks[12], (L, D_MODEL, PEER_HEADS * PEER_QDIM), D_MODEL ** -0.5),
        "peer_subkeys": nrm(ks[13], (L, PEER_HEADS, 2, N_KEYS, PEER_QDIM // 2), (PEER_QDIM // 2) ** -0.5),
        "peer_u": nrm(ks[14], (L, N_EXPERTS, D_MODEL), D_MODEL ** -0.5),
        "peer_v": nrm(ks[15], (L, N_EXPERTS, D_MODEL), (PEER_HEADS * PEER_TOPK) ** -0.5),
        "final_norm_w": 1.0 + nrm(ks[16], (D_MODEL,), 0.02),
    }


def reference(x, c, w_ada, b_ada, norm1_w, w_in, attn_sinks, gla_gate_up, gla_gate_bias, gla_norm_w,
              w_out, norm2_w, peer_wq, peer_subkeys, peer_u, peer_v, final_norm_w):
    B, S, _ = x.shape
    split_points = np.cumsum(np.array(IN_SIZES))[:-1].tolist()
    for l in range(DEPTH):
        mod = jax.nn.silu(c) @ w_ada[l] + b_ada[l]
        shift1, scale1, gate1, shift2, scale2, gate2 = jnp.split(mod[:, None, :], 6, axis=-1)

        h = rms_norm(x, norm1_w[l]) * (1.0 + scale1) + shift1
        proj = h @ w_in[l]
        aq, ak, av, gq, gk, gv, gg, glr = jnp.split(proj, split_points, axis=-1)
        att = sliding_window_attention(
            aq.reshape(B, S, ATT_HEADS, ATT_HEAD_DIM),
            ak.reshape(B, S, ATT_KV_HEADS, ATT_HEAD_DIM),
            av.reshape(B, S, ATT_KV_HEADS, ATT_HEAD_DIM),
            attn_sinks[l])
        log_a = jax.nn.log_sigmoid((glr @ gla_gate_up[l] + gla_gate_bias[l]).astype(jnp.float32)) / GLA_TAU
        go = gla_chunked(
            gq.reshape(B, S, GLA_HEADS, GLA_DK),
            gk.reshape(B, S, GLA_HEADS, GLA_DK),
            gv.reshape(B, S, GLA_HEADS, GLA_DV),
            log_a.reshape(B, S, GLA_HEADS, GLA_DK))
        go = rms_norm(go, gla_norm_w[l]).reshape(B, S, GLA_V_W).astype(x.dtype) * jax.nn.silu(gg)
        mixed = jnp.concatenate([att, go], axis=-1) @ w_out[l]
        x = x + gate1 * mixed

        h2 = rms_norm(x, norm2_w[l]) * (1.0 + scale2) + shift2
        x = x + gate2 * peer_ffn(h2, peer_wq[l], peer_subkeys[l], peer_u[l], peer_v[l])
    return rms_norm(x, final_norm_w)
```

```python
from contextlib import ExitStack
import numpy as np
import concourse.bass as bass
import concourse.mybir as mybir
from concourse.bass_utils import run_bass_kernel_spmd

F32 = mybir.dt.float32
F32R = mybir.dt.float32r
I32 = mybir.dt.int32
BF16 = mybir.dt.bfloat16
U32 = mybir.dt.uint32
AF = mybir.ActivationFunctionType
ALU = mybir.AluOpType
AX = mybir.AxisListType

NCORES = 8
TOK = 2048
NT = TOK // 128
NPRE = 48
D = 1024
EPS = 1e-6
DEBUG = False
STAGE = 99
SUB = 99
XSRC_PRE = False
NT_RUN = 16
NPRE_RUN = 48
RUN_CORES = NCORES


class _Stop(Exception):
    pass
_dbg_out = {}

STREAMS = ("pe", "act", "dve", "pool", "sp")
DMA_RING = {"sp": 8, "act": 4, "pool": 16}


class Buf:
    def __init__(self, name, t, psum=False):
        self.name = name
        self.t = t
        self.regs = {}
        self.psum = psum

    def __getitem__(self, idx):
        return self.t[idx]


class Prog:
    def __init__(self, nc):
        self.nc = nc
        self.ops = {s: [] for s in STREAMS}
        self.seq = {s: 0 for s in STREAMS}
        self.dcount = {q: 0 for q in DMA_RING}
        self.waited = {s: {} for s in STREAMS}
        self.last_dma_tok = {}

    def _need(self, stream, tok, hazard, waits):
        if tok is None:
            return
        semkey, val, pstream, kind = tok
        if kind == "c" and pstream == stream:
            if stream == "pe":
                return
        w = self.waited[stream]
        if w.get(semkey, -1) >= val:
            return
        w[semkey] = val
        waits.append((semkey, val))

    def _collect(self, stream, reads, writes):
        waits = []
        for b, k in reads:
            regs = b.regs
            if k is None:
                for e in regs.values():
                    self._need(stream, e["w"], "RAW", waits)
            elif k in regs:
                self._need(stream, regs[k]["w"], "RAW", waits)
            elif None in regs:
                self._need(stream, regs[None]["w"], "RAW", waits)
        for b, k in writes:
            regs = b.regs
            if k is None:
                for e in regs.values():
                    self._need(stream, e["w"], "WAW", waits)
                    for t in e["r"].values():
                        self._need(stream, t, "WAR", waits)
            else:
                for kk in (k, None):
                    if kk in regs:
                        e = regs[kk]
                        self._need(stream, e["w"], "WAW", waits)
                        for t in e["r"].values():
                            self._need(stream, t, "WAR", waits)
        return waits

    def _record(self, tok, reads, writes):
        for b, k in reads:
            regs = b.regs
            if k is None:
                e = regs.setdefault(None, {"w": None, "r": {}})
            else:
                if k not in regs:
                    regs[k] = {"w": regs[None]["w"] if None in regs else None, "r": {}}
                e = regs[k]
            e["r"][tok[0]] = tok
        for b, k in writes:
            if k is None:
                b.regs = {None: {"w": tok, "r": {}}}
            else:
                b.regs[k] = {"w": tok, "r": {}}

    @staticmethod
    def _norm(lst):
        return [x if isinstance(x, tuple) else (x, None) for x in (lst or [])]

    def op(self, stream, emit, reads=None, writes=None):
        reads = self._norm(reads)
        writes = self._norm(writes)
        writes = writes + [(b, None) for b, k in reads if b.psum]
        reads = [(b, k) for b, k in reads if not b.psum]
        waits = self._collect(stream, reads, writes)
        self.seq[stream] += 1
        tok = (("c", stream), self.seq[stream], stream, "c")
        self._record(tok, reads, writes)
        self.ops[stream].append((waits, emit, (("c", stream), 1)))
        return tok

    def dma(self, queue, emit, reads=None, writes=None):
        reads = self._norm(reads)
        writes = self._norm(writes)
        waits = self._collect(queue, reads, writes)
        i = self.dcount[queue]
        self.dcount[queue] += 1
        K = DMA_RING[queue]
        slot = i % K
        semkey = ("d", queue, slot)
        if i >= K:
            self._need(queue, (semkey, 16 * (i // K), queue, "d"), "WAW", waits)
        val = 16 * (i // K + 1)
        tok = (semkey, val, queue, "d")
        self.last_dma_tok[semkey] = val
        self._record(tok, reads, writes)
        self.ops[queue].append((waits, emit, (semkey, 16)))
        return tok

    def barrier(self, skip_queue=None):
        for s in STREAMS:
            waits = []
            for s2 in STREAMS:
                if s2 != s and self.seq[s2] > 0:
                    self._need(s, (("c", s2), self.seq[s2], s2, "c"), "RAW", waits)
            for semkey, val in self.last_dma_tok.items():
                if skip_queue is not None and semkey[1] == skip_queue:
                    continue
                self._need(s, (semkey, val, semkey[1], "d"), "RAW", waits)
            if waits:
                self.ops[s].append((waits, None, None))

    def wait_all_dma(self, stream="sp"):
        waits = []
        for semkey, val in self.last_dma_tok.items():
            self._need(stream, (semkey, val, semkey[1], "d"), "RAW", waits)
        if waits:
            self.ops[stream].append((waits, None, None))

    def setup(self, stack):
        nc = self.nc
        self.sems = {}
        for s in STREAMS:
            self.sems[("c", s)] = stack.enter_context(nc.semaphore(f"c_{s}"))
        for q, K in DMA_RING.items():
            for j in range(K):
                self.sems[("d", q, j)] = stack.enter_context(nc.semaphore(f"d_{q}_{j}"))

    def flush(self):
        nc = self.nc
        sems = self.sems
        ops = self.ops
        self.ops = {s: [] for s in STREAMS}

        def run(stream):
            def f(eng):
                for waits, emit, inc in ops[stream]:
                    for semkey, val in waits:
                        eng.wait_ge(sems[semkey], val)
                    if emit is not None:
                        emit(eng).then_inc(sems[inc[0]], inc[1])
            return f

        with nc.Block() as block:
            block.tensor(run("pe"))
            block.scalar(run("act"))
            block.vector(run("dve"))
            block.gpsimd(run("pool"))
            block.sync(run("sp"))


def build_program():
    nc = bass.Bass("TRN2", target_bir_lowering=False)

    def din(name, shape, dt=F32):
        return nc.dram_tensor(name, shape, dt, kind="ExternalInput")

    d_xown = din("x_own", [TOK, D]).ap()
    d_xpre = din("x_pre", [NPRE * 128, D]).ap()
    d_pvalid = din("pvalid", [128, NPRE]).ap()
    d_mask0 = din("mask0", [128, 256]).ap()
    d_maskr = din("maskr", [128, 256]).ap()
    d_ct = din("c_t", [128, 8]).ap()
    h_wada = din("w_ada", [D, 6 * D])
    h_bada = din("b_ada", [1, 6 * D])
    h_n1w = din("norm1_w", [1, D])
    h_win = din("w_in", [D, 2320])
    h_sinks = din("sinks", [1, 8])
    d_gup = din("gate_up", [16, 256]).ap()
    d_gbias = din("gate_bias", [1, 256]).ap()
    h_gnw = din("gla_norm_w", [1, 128])
    h_wout = din("w_out", [D, D])
    h_n2w = din("norm2_w", [1, D])
    h_wq = din("peer_wq", [D, 2048])
    d_skT = din("skT", [128, 16, 128]).ap()
    d_puv = din("peer_uv", [16384, 2 * D]).ap()
    d_t16 = nc.dram_tensor("tab16", [16384, 2 * D], BF16, kind="Internal").ap()
    h_fnw = din("final_norm_w", [1, D])
    d_ident = din("ident", [128, 128]).ap()
    d_tri = din("tri", [128, 128]).ap()
    d_triu = din("triu", [128, 128]).ap()
    d_iota = din("iota16", [128, 16]).ap()
    d_out = nc.dram_tensor("out", [TOK, D], F32, kind="ExternalOutput").ap()
    d_bc = nc.dram_tensor("bc_scratch", [6, 128, D], F32, kind="Internal").ap()
    dbg = {}
    if DEBUG:
        for nm, shp, dt in (("dbg_mix", [TOK, D], F32), ("dbg_x1", [TOK, D], F32),
                            ("dbg_idx", [TOK, 128], I32), ("dbg_gate", [TOK, 128], F32),
                            ("dbg_bc", [6, 128, D], F32), ("dbg_hid", [TOK, 128], F32), ("dbg_hraw", [TOK, 128], F32), ("dbg_y", [TOK, D], F32)):
            dbg[nm] = nc.dram_tensor(nm, shp, dt, kind="ExternalOutput").ap()

    def bcast_row(handle, n, off=0):
        return bass.AP(handle, off, [[0, 128], [1, n]])

    wada_r = h_wada.ap().rearrange("(kc p) n -> p kc n", p=128)
    win_r = h_win.ap().rearrange("(kc p) n -> p kc n", p=128)
    wout_r = h_wout.ap().rearrange("(kc p) n -> p kc n", p=128)
    wq_r = h_wq.ap().rearrange("(kc p) n -> p kc n", p=128)

    P = Prog(nc)
    top = ExitStack()
    P.setup(top)

    def maybe_stop(k):
        return

    try:

        def sb(st, name, shape, dt=F32):
            return Buf(name, st.enter_context(nc.sbuf_tensor("s_" + name, shape, dt)))

        banks = [Buf(f"ps{j}", top.enter_context(nc.psum_tensor(f"ps{j}", [128, 512], F32)), psum=True) for j in range(8)]
        pcnt = [0]

        def psum():
            b = banks[pcnt[0] % 6]
            pcnt[0] += 1
            return b

        ident = sb(top, "ident", [128, 128])
        tri = sb(top, "tri", [128, 128])
        triu = sb(top, "triu", [128, 128])
        iota16 = sb(top, "iota16", [128, 16])
        ones = sb(top, "ones", [128, 128])
        for t_, d_ in ((ident, d_ident), (tri, d_tri), (triu, d_triu), (iota16, d_iota)):
            P.dma("sp", (lambda t_, d_: lambda e: e.dma_start(out=t_[:], in_=d_))(t_, d_), writes=[t_])
        P.op("dve", lambda e: e.memset(ones[:], 1.0), writes=[ones])

        t16b = Buf("tab16", None)
        NCAST = 64
        castn = [0]

        def issue_cast():
            j = castn[0]
            if j >= NCAST:
                return
            castn[0] += 1
            rows = 16384 // NCAST
            P.dma("pool", lambda e: e.dma_start(out=d_t16[j * rows:(j + 1) * rows, :], in_=d_puv[j * rows:(j + 1) * rows, :]), writes=[(t16b, j)])


        def mm(out_b, out_ap, lhsT, rhs, start, stop, reads):
            return P.op("pe", lambda e: e.matmul(out_ap, lhsT=lhsT, rhs=rhs, start=start, stop=stop),
                        reads=reads, writes=[out_b])

        def tr(out_b, out_ap, in_ap, reads):
            return P.op("pe", lambda e: e.transpose(out_ap, in_ap, ident[:]), reads=reads + [ident], writes=[out_b])

        def rstd_from_ss(ss_b, ss_ap, n, keys=None):
            P.op("act", lambda e: e.activation(out=ss_ap, in_=ss_ap, func=AF.Ln, scale=1.0 / n, bias=EPS), reads=[ss_b], writes=[ss_b])
            P.op("act", lambda e: e.activation(out=ss_ap, in_=ss_ap, func=AF.Exp, scale=-0.5), reads=[ss_b], writes=[ss_b])

        def load_round(dst, src_r, n, tag, perm_aq=False):
            with ExitStack() as stg:
                half = n // 2
                stage = [sb(stg, f"wst_{tag}{j}", [128, half]) for j in range(2)]
                q = 0
                for kc in range(8):
                    for hf in range(2):
                        st_ = stage[q % 2]
                        P.dma("sp", (lambda st_, kc, hf: lambda e: e.dma_start(out=st_[:], in_=src_r[:, kc, hf * half:(hf + 1) * half]))(st_, kc, hf), writes=[st_])
                        if perm_aq and hf == 0:
                            P.op("act", (lambda st_, kc: lambda e: e.copy(
                                out=dst[:, kc, 0:512].rearrange("p (c two d) -> p two c d", c=4, two=2, d=64),
                                in_=st_[:, 0:512].rearrange("p (two c d) -> p two c d", two=2, c=4, d=64)))(st_, kc), reads=[st_], writes=[(dst, kc)])
                            P.op("dve", (lambda st_, kc: lambda e: e.tensor_copy(out=dst[:, kc, 512:half], in_=st_[:, 512:half]))(st_, kc),
                                 reads=[st_], writes=[(dst, kc)])
                        elif q % 2 == 0:
                            P.op("act", (lambda st_, kc, hf: lambda e: e.copy(out=dst[:, kc, hf * half:(hf + 1) * half], in_=st_[:]))(st_, kc, hf),
                                 reads=[st_], writes=[(dst, kc)])
                        else:
                            P.op("dve", (lambda st_, kc, hf: lambda e: e.tensor_copy(out=dst[:, kc, hf * half:(hf + 1) * half], in_=st_[:]))(st_, kc, hf),
                                 reads=[st_], writes=[(dst, kc)])
                        q += 1
                P.barrier(skip_queue="pool")
                P.flush()

        with ExitStack() as ph:
            wada = [sb(ph, f"wada{j}", [128, 8, 512]) for j in range(4)]
            stage = [sb(ph, f"stg{j}", [128, 512]) for j in range(2)]
            n1w = sb(ph, "n1w", [128, D])
            n2w = sb(ph, "n2w", [128, D])
            bada = sb(ph, "bada", [1, 6 * D])
            ct = sb(ph, "ct", [128, 8])
            silc = sb(ph, "silc", [128, 8])
            silc_bc = sb(ph, "silc_bc", [128, 8, 128], F32R)
            wadar = [sb(ph, f"wadar{j}", [128, 8, 512], F32R) for j in range(2)]
            P.dma("sp", lambda e: e.dma_start(out=n1w[:], in_=bcast_row(h_n1w, D)), writes=[n1w])
            P.dma("sp", lambda e: e.dma_start(out=n2w[:], in_=bcast_row(h_n2w, D)), writes=[n2w])
            P.dma("sp", lambda e: e.dma_start(out=bada[:], in_=h_bada.ap()), writes=[bada])
            P.dma("sp", lambda e: e.dma_start(out=ct[:], in_=d_ct), writes=[ct])
            P.op("act", lambda e: e.activation(out=silc[:], in_=ct[:], func=AF.Silu), reads=[ct], writes=[silc])
            for kc in range(8):
                P.op("act", (lambda kc: lambda e: e.copy(out=silc_bc[:, kc, :], in_=silc[:, kc:kc + 1].to_broadcast([128, 128])))(kc),
                     reads=[silc], writes=[(silc_bc, kc)])
            for n in range(12):
                wb = wada[n % 4]
                wr = wadar[n % 2]
                P.dma("sp", (lambda wb, n: lambda e: e.dma_start(out=wb[:], in_=wada_r[:, :, n * 512:(n + 1) * 512]))(wb, n), writes=[wb])
                for kc in range(8):
                    if kc % 2 == 0:
                        P.op("act", (lambda wr, wb, kc: lambda e: e.copy(out=wr[:, kc, :], in_=wb[:, kc, :]))(wr, wb, kc), reads=[wb], writes=[(wr, kc)])
                    else:
                        P.op("dve", (lambda wr, wb, kc: lambda e: e.tensor_copy(out=wr[:, kc, :], in_=wb[:, kc, :]))(wr, wb, kc), reads=[wb], writes=[(wr, kc)])
                bk = psum()
                for kc in range(8):
                    mm(bk, bk[:, :], silc_bc[:, kc, :], wr[:, kc, :], kc == 0, False, [(silc_bc, kc), (wr, kc)])
                mm(bk, bk[:, :], ones[0:1, :], bada[0:1, n * 512:(n + 1) * 512], False, True, [ones, bada])
                sec, half = n // 2, n % 2
                sg = stage[n % 2]
                if sec in (1, 4):
                    nw = n1w if sec == 1 else n2w
                    P.op("dve", (lambda sg, bk, nw, half: lambda e: e.scalar_tensor_tensor(
                        out=sg[:], in0=bk[:, :], scalar=1.0, in1=nw[:, half * 512:(half + 1) * 512], op0=ALU.add, op1=ALU.mult))(sg, bk, nw, half),
                        reads=[bk, nw], writes=[sg])
                else:
                    P.op("act", (lambda sg, bk: lambda e: e.copy(out=sg[:], in_=bk[:, :]))(sg, bk), reads=[bk], writes=[sg])
                P.dma("sp", (lambda sg, sec, half: lambda e: e.dma_start(out=d_bc[sec, :, half * 512:(half + 1) * 512], in_=sg[:]))(sg, sec, half), reads=[sg])
            P.barrier(skip_queue="pool")
            P.flush()
            maybe_stop(0)
        bcbuf = Buf("bc_dram", None)

        def load_bc(t, sec):
            P.dma("sp", lambda e: e.dma_start(out=t[:], in_=d_bc[sec]), writes=[t])

        if DEBUG:
            with ExitStack() as ph:
                tt = sb(ph, "dbgt", [128, D])
                for sec in range(6):
                    load_bc(tt, sec)
                    P.dma("sp", (lambda sec: lambda e: e.dma_start(out=dbg["dbg_bc"][sec], in_=tt[:]))(sec), reads=[tt])
                P.barrier(skip_queue="pool")
                P.flush()

        xstore = [sb(top, f"xs{i}", [128, D]) for i in range(NT)]

        with ExitStack() as ph:
            w_in = sb(ph, "w_in", [128, 8, 2320], F32R)
            load_round(w_in, win_r, 2320, "a", perm_aq=True)
            A1t = sb(ph, "A1t", [128, D])
            B1t = sb(ph, "B1t", [128, D])
            gup = sb(ph, "gup", [128, 256])
            gbias = sb(ph, "gbias", [1, 256])
            sink_bc = sb(ph, "sink_bc", [128, 8])
            gnw_bc = sb(ph, "gnw_bc", [128, 128])
            pvalid = sb(ph, "pvalid", [128, NPRE])
            maskr = sb(ph, "maskr", [128, 256])
            mask0 = sb(ph, "mask0", [128, 256])
            xt0 = sb(ph, "xt0", [128, D])
            h = sb(ph, "h", [128, D])
            hT = sb(ph, "hT", [128, 8, 128], F32R)
            hTb = sb(ph, "hTb", [128, 8, 128], F32R)
            decay2 = sb(ph, "decay2", [128, 2])
            ss1b = sb(ph, "ss1b", [128, 1])
            aqTp = sb(ph, "aqTp", [128, 8, 128], F32R)
            akT = [sb(ph, f"akT{j}", [128, 128], F32R) for j in range(2)]
            av = [sb(ph, f"av{j}", [128, 128], F32R) for j in range(2)]
            gqTp = sb(ph, "gqTp", [128, 4, 128])
            raw = sb(ph, "raw", [128, 4, 128])
            glrTL = [sb(ph, f"glrT{j}", [128, 128]) for j in range(2)]
            k_tmL = [sb(ph, f"k_tm{j}", [128, 256]) for j in range(2)]
            v_tmL = [sb(ph, f"v_tm{j}", [128, 512]) for j in range(2)]
            sg_ = sb(ph, "sgg", [128, 512])
            e1 = sb(ph, "e1", [128, 256])
            sp_ = sb(ph, "sp", [128, 256])
            eq = sb(ph, "eq", [128, 2, 128])
            ek = sb(ph, "ek", [128, 2, 128])
            kt = sb(ph, "kt", [128, 2, 128])
            er = sb(ph, "er", [128, 256])
            khat = sb(ph, "khat", [128, 256])
            attTm = sb(ph, "attTm", [128, 4, 128])
            S_sb = sb(ph, "S_sb", [128, 2, 128])
            decay = sb(ph, "decay", [128, 2])
            sc = sb(ph, "sc", [128, 4, 256])
            PT = sb(ph, "PT", [128, 8, 128], F32R)
            st8 = sb(ph, "st8", [128, 5, 8])
            ss1 = sb(ph, "ss1", [128, 1])
            ss4 = sb(ph, "ss4", [128, 4])
            gtmp = sb(ph, "gtmp", [128, 512])
            xt = [xt0, xstore[15]]
            h_alt = xstore[14]
            e1L = [e1, Buf("e1v", gtmp.t[:, 0:256])]
            spL = [sp_, Buf("spv", gtmp.t[:, 256:512])]
            erL = [er, Buf("erv", attTm.t[:, 0:2, :].rearrange("p a b -> p (a b)"))]
            khatL = [khat, Buf("khatv", attTm.t[:, 2:4, :].rearrange("p a b -> p (a b)"))]
            k_tm3 = k_tmL + [Buf("ktm3v", raw.t[:, 0:2, :].rearrange("p a b -> p (a b)"))]
            v_tm3 = v_tmL + [Buf("vtm3v", sg_.t[:, :])]

            load_bc(A1t, 1)
            load_bc(B1t, 0)
            P.op("dve", lambda e: e.memset(gup[:], 0.0), writes=[gup])
            P.dma("sp", lambda e: e.dma_start(out=gup[112:128, :], in_=d_gup), writes=[gup])
            P.dma("sp", lambda e: e.dma_start(out=gbias[:], in_=d_gbias), writes=[gbias])
            P.dma("sp", lambda e: e.dma_start(out=sink_bc[:], in_=bcast_row(h_sinks, 8)), writes=[sink_bc])
            P.dma("sp", lambda e: e.dma_start(out=gnw_bc[:], in_=bcast_row(h_gnw, 128)), writes=[gnw_bc])
            P.dma("sp", lambda e: e.dma_start(out=pvalid[:], in_=d_pvalid), writes=[pvalid])
            P.dma("sp", lambda e: e.dma_start(out=maskr[:], in_=d_maskr), writes=[maskr])
            P.dma("sp", lambda e: e.dma_start(out=mask0[:], in_=d_mask0), writes=[mask0])
            w_in_r = w_in.t[:]
            w_in_f = w_in.t[:].bitcast(F32)
            zsrc = xstore[13]
            P.op("dve", lambda e: e.memset(zsrc[:], 0.0), writes=[zsrc])
            P.op("act", lambda e: e.copy(out=aqTp[:], in_=zsrc[:].rearrange("p (a b) -> p a b", a=8)), reads=[zsrc], writes=[aqTp])
            P.op("dve", lambda e: e.memset(gqTp[:], 0.0), writes=[gqTp])
            P.op("dve", lambda e: e.memset(S_sb[:], 0.0), writes=[S_sb])
            P.op("act", lambda e: e.copy(out=akT[1][:], in_=zsrc[:, 0:128]), reads=[zsrc], writes=[akT[1]])
            P.op("act", lambda e: e.copy(out=av[1][:], in_=zsrc[:, 0:128]), reads=[zsrc], writes=[av[1]])

            cnt = [0]

            def tile_front(xsrc, own, pj):
                x_ = xt0
                cnt[0] += 1
                P.dma("sp", lambda e: e.dma_start(out=x_[:], in_=xsrc), writes=[x_])
                P.op("act", lambda e: e.activation(out=h[:], in_=x_[:], func=AF.Square, accum_out=ss1[:, 0:1]), reads=[x_], writes=[h, ss1])
                rstd_from_ss(ss1, ss1[:, 0:1], D)
                P.op("dve", lambda e: e.scalar_tensor_tensor(out=h[:], in0=x_[:], scalar=ss1[:, 0:1], in1=A1t[:], op0=ALU.mult, op1=ALU.mult),
                     reads=[x_, ss1, A1t], writes=[h])
                P.op("dve", lambda e: e.tensor_tensor(out=h[:], in0=h[:], in1=B1t[:], op=ALU.add), reads=[h, B1t], writes=[h])
                ba, bb = psum(), psum()
                for j in range(8):
                    bk = ba if j < 4 else bb
                    tr(bk, bk[:, (j % 4) * 128:(j % 4 + 1) * 128], h[:, j * 128:(j + 1) * 128], [h])
                P.op("act", lambda e: e.copy(out=hT[:, 0:4, :], in_=ba[:, :].rearrange("p (a b) -> p a b", a=4)), reads=[ba], writes=[(hT, 0)])
                P.op("dve", lambda e: e.tensor_copy(out=hT[:, 4:8, :], in_=bb[:, :].rearrange("p (a b) -> p a b", a=4)), reads=[bb], writes=[(hT, 1)])
                hTk = lambda kc: (hT, 0 if kc < 4 else 1)
                last_pre = (not own) and pj == NPRE - 1
                cur = (cnt[0] - 1) % 2 if own else 1
                return x_, last_pre

            def proj_tm(cols, bk, ncol, hT=hT):
                for kc in range(8):
                    mm(bk, bk[:, 0:ncol], hT[:, kc, :], w_in_r[:, kc, cols[0]:cols[1]], kc == 0, kc == 7,
                       [(hT, 0 if kc < 4 else 1), (w_in, kc)])

            def proj_fm(lhs_fn, bk, col0, m=128, f32=False, hT=hT):
                for kc in range(8):
                    lhsT = lhs_fn(kc)
                    rhs = hT[:, kc, :]
                    if f32:
                        rhs = rhs.bitcast(F32)
                    mm(bk, bk[0:m, col0:col0 + 128], lhsT, rhs, kc == 0, kc == 7, [(hT, 0 if kc < 4 else 1), (w_in, kc)])

            def gla_common(own, pj, bs):
                glrT, k_tm = glrTL[bs], k_tmL[bs]
                bz = psum()
                mm(bz, bz[:, 0:256], glrT[:, :], gup[:, :], True, False, [glrT, gup])
                mm(bz, bz[:, 0:256], ones[0:1, :], gbias[0:1, :], False, True, [ones, gbias])
                P.op("act", lambda e: e.activation(out=e1[:], in_=bz[:, 0:256], func=AF.Exp, scale=-1.0), reads=[bz], writes=[e1])
                P.op("act", lambda e: e.activation(out=sp_[:], in_=e1[:], func=AF.Ln, bias=1.0), reads=[e1], writes=[sp_])
                br = psum()
                mm(br, br[:, 0:256], triu[:, :], sp_[:, :], True, True, [triu, sp_])
                bt = psum()
                if own:
                    for hc in range(2):
                        mm(bt, bt[:, hc * 128:(hc + 1) * 128], sp_[:, hc * 128:(hc + 1) * 128], tri[:, :], True, True, [sp_, tri])
                for hc in range(2):
                    mm(bt, bt[:, 256 + 2 * hc:258 + 2 * hc], sp_[:, hc * 128:(hc + 1) * 128], ones[:, 0:2], True, True, [sp_, ones])
                P.op("act", lambda e: e.activation(out=er[:], in_=br[:, 0:256], func=AF.Exp, scale=-1.0 / 16), reads=[br], writes=[er])
                P.op("dve", lambda e: e.tensor_tensor(out=khat[:], in0=k_tm[:], in1=er[:], op=ALU.mult), reads=[k_tm, er], writes=[khat])
                P.op("act", lambda e: e.activation(out=decay[:], in_=bt[:, 256:260].rearrange("p (a b) -> p a b", a=2)[:, :, 0],
                                                   func=AF.Exp, scale=-1.0 / 16), reads=[bt], writes=[decay])
                if own:
                    btv = bt[:, 0:256].rearrange("p (a b) -> p a b", a=2)
                    P.op("act", lambda e: e.activation(out=eq[:], in_=btv, func=AF.Exp, scale=-1.0 / 16), reads=[bt], writes=[eq])
                    P.op("act", lambda e: e.activation(out=ek[:], in_=btv, func=AF.Exp, scale=1.0 / 16), reads=[bt], writes=[ek])
                    P.op("dve", lambda e: e.scalar_tensor_tensor(out=gqTp[0:64, 0:4:2, :], in0=eq[0:64, :, :], scalar=0.125, in1=raw[0:64, 0:2, :],
                                                                 op0=ALU.mult, op1=ALU.mult), reads=[eq, raw], writes=[(gqTp, 0)])
                    P.op("dve", lambda e: e.scalar_tensor_tensor(out=gqTp[64:128, 1:4:2, :], in0=eq[64:128, :, :], scalar=0.125, in1=raw[64:128, 0:2, :],
                                                                 op0=ALU.mult, op1=ALU.mult), reads=[eq, raw], writes=[(gqTp, 1)])
                    P.op("dve", lambda e: e.tensor_tensor(out=kt[:], in0=ek[:], in1=raw[:, 2:4, :], op=ALU.mult), reads=[ek, raw], writes=[kt])

            def state_update(bs):
                v_tm = v_tmL[bs]
                bd = psum()
                for hc in range(2):
                    mm(bd, bd[:, hc * 256:(hc + 1) * 256], khat[:, hc * 128:(hc + 1) * 128], v_tm[:, hc * 256:(hc + 1) * 256], True, True, [khat, v_tm])
                for hh in range(4):
                    p0, c = (hh % 2) * 64, hh // 2
                    P.op("dve", (lambda p0, c, hh: lambda e: e.scalar_tensor_tensor(
                        out=S_sb[p0:p0 + 64, c, :], in0=S_sb[p0:p0 + 64, c, :], scalar=decay[p0:p0 + 64, c:c + 1],
                        in1=bd[p0:p0 + 64, c * 256 + (hh % 2) * 128:c * 256 + (hh % 2) * 128 + 128], op0=ALU.mult, op1=ALU.add))(p0, c, hh),
                        reads=[(S_sb, hh), decay, bd], writes=[(S_sb, hh)])

            HH, HT, SS1, DEC = [h, h_alt], [hT, hTb], [ss1, ss1b], [decay, decay2]

            def pS1a(pj):
                b_ = pj % 2
                x_, h_, hT_, ss_ = xt[b_], HH[b_], HT[b_], SS1[b_]
                P.dma("sp", lambda e: e.dma_start(out=x_[:], in_=d_xpre[pj * 128:(pj + 1) * 128, :]), writes=[x_])
                P.op("act", lambda e: e.activation(out=h_[:], in_=x_[:], func=AF.Square, accum_out=ss_[:, 0:1]), reads=[x_], writes=[h_, ss_])
                rstd_from_ss(ss_, ss_[:, 0:1], D)
                P.op("dve", lambda e: e.scalar_tensor_tensor(out=h_[:], in0=x_[:], scalar=ss_[:, 0:1], in1=A1t[:], op0=ALU.mult, op1=ALU.mult),
                     reads=[x_, ss_, A1t], writes=[h_])
                P.op("dve", lambda e: e.tensor_tensor(out=h_[:], in0=h_[:], in1=B1t[:], op=ALU.add), reads=[h_, B1t], writes=[h_])

            def pS1b(pj):
                b_ = pj % 2
                h_, hT_ = HH[b_], HT[b_]
                ba, bb = banks[6], banks[7]
                for j in range(8):
                    bk = ba if j < 4 else bb
                    tr(bk, bk[:, (j % 4) * 128:(j % 4 + 1) * 128], h_[:, j * 128:(j + 1) * 128], [h_])
                P.op("act", lambda e: e.copy(out=hT_[:, 0:4, :], in_=ba[:, :].rearrange("p (a b) -> p a b", a=4)), reads=[ba], writes=[(hT_, 0)])
                P.op("dve", lambda e: e.tensor_copy(out=hT_[:, 4:8, :], in_=bb[:, :].rearrange("p (a b) -> p a b", a=4)), reads=[bb], writes=[(hT_, 1)])

            def pS2(pj):
                b_ = pj % 2
                hT_ = HT[b_]
                glrT, k_tm, v_tm = glrTL[b_], k_tm3[pj % 3], v_tm3[pj % 3]
                b2 = banks[3]
                proj_tm((1024, 1536), b2, 512, hT=hT_)
                b3 = banks[4]
                proj_tm((1536, 1792), b3, 256, hT=hT_)
                P.op("act", lambda e: e.copy(out=k_tm[:], in_=b2[:, 0:256]), reads=[b2], writes=[k_tm])
                P.op("act", lambda e: e.activation(out=v_tm[:, 0:256], in_=b2[:, 256:512], func=AF.Copy, scale=pvalid[:, pj:pj + 1]),
                     reads=[b2, pvalid], writes=[(v_tm, 0)])
                P.op("act", lambda e: e.activation(out=v_tm[:, 256:512], in_=b3[:, 0:256], func=AF.Copy, scale=pvalid[:, pj:pj + 1]),
                     reads=[b3, pvalid], writes=[(v_tm, 1)])
                bf = banks[5]
                proj_fm(lambda kc: w_in_r[:, kc, 2192:2320], bf, 0, hT=hT_)
                P.op("act", lambda e: e.copy(out=glrT[:], in_=bf[:, 0:128]), reads=[bf], writes=[glrT])
                if pj == NPRE - 1:
                    b1 = banks[3]
                    proj_tm((640, 768), b1, 128, hT=hT_)
                    P.op("act", lambda e: e.copy(out=av[1][:], in_=b1[:, 0:128]), reads=[b1], writes=[av[1]])
                    bg = banks[4]
                    proj_fm(lambda kc: w_in_r[:, kc, 512:640], bg, 0, hT=hT_)
                    P.op("act", lambda e: e.copy(out=akT[1][:], in_=bg[:, 0:128]), reads=[bg], writes=[akT[1]])

            def pS3(pj):
                b_ = pj % 2
                glrT = glrTL[b_]
                e1_, spb, er_, dec_ = e1L[b_], spL[b_], erL[b_], DEC[b_]
                bz = banks[1]
                mm(bz, bz[:, 0:256], glrT[:, :], gup[:, :], True, False, [glrT, gup])
                mm(bz, bz[:, 0:256], ones[0:1, :], gbias[0:1, :], False, True, [ones, gbias])
                P.op("act", lambda e: e.activation(out=e1_[:], in_=bz[:, 0:256], func=AF.Exp, scale=-1.0), reads=[bz], writes=[e1_])
                P.op("act", lambda e: e.activation(out=spb[:], in_=e1_[:], func=AF.Ln, bias=1.0), reads=[e1_], writes=[spb])
                br = banks[2]
                mm(br, br[:, 0:256], triu[:, :], spb[:], True, True, [triu, spb])
                for hc in range(2):
                    mm(br, br[:, 256 + 2 * hc:258 + 2 * hc], spb[:, hc * 128:(hc + 1) * 128], ones[:, 0:2], True, True, [spb, ones])
                P.op("act", lambda e: e.activation(out=er_[:], in_=br[:, 0:256], func=AF.Exp, scale=-1.0 / 16), reads=[br], writes=[er_])
                P.op("act", lambda e: e.activation(out=dec_[:], in_=br[:, 256:260].rearrange("p (a b) -> p a b", a=2)[:, :, 0],
                                                   func=AF.Exp, scale=-1.0 / 16), reads=[br], writes=[dec_])

            def pS4(pj):
                b_ = pj % 2
                k_tm, v_tm = k_tm3[pj % 3], v_tm3[pj % 3]
                er_, kh_, dec_ = erL[b_], khatL[b_], DEC[b_]
                P.op("dve", lambda e: e.tensor_tensor(out=kh_[:], in0=k_tm[:], in1=er_[:], op=ALU.mult), reads=[k_tm, er_], writes=[kh_])
                bd = banks[0]
                for hc in range(2):
                    mm(bd, bd[:, hc * 256:(hc + 1) * 256], kh_[:, hc * 128:(hc + 1) * 128], v_tm[:, hc * 256:(hc + 1) * 256], True, True, [kh_, v_tm])
                for hh in range(4):
                    p0, c = (hh % 2) * 64, hh // 2
                    P.op("dve", (lambda p0, c, hh: lambda e: e.scalar_tensor_tensor(
                        out=S_sb[p0:p0 + 64, c, :], in0=S_sb[p0:p0 + 64, c, :], scalar=dec_[p0:p0 + 64, c:c + 1],
                        in1=bd[p0:p0 + 64, c * 256 + (hh % 2) * 128:c * 256 + (hh % 2) * 128 + 128], op0=ALU.mult, op1=ALU.add))(p0, c, hh),
                        reads=[(S_sb, hh), dec_, bd], writes=[(S_sb, hh)])

            pjs = list(range(NPRE - NPRE_RUN, NPRE)) if STAGE >= 1 else []
            stages = [(pS1a, 0), (pS2, 1), (pS1b, 0), (pS3, 2), (pS4, 3)]
            for t_ in range(len(pjs) + 3):
                for fn_, lag in stages:
                    jj = t_ - lag
                    if 0 <= jj < len(pjs):
                        fn_(pjs[jj])
            P.barrier(skip_queue="pool")
            if STAGE == 1:
                P.barrier(skip_queue="pool")
                maybe_stop(1)
            for i in range(NT_RUN if STAGE >= 2 else 0):
                cur, prv = i % 2, (i + 1) % 2
                glrT, k_tm, v_tm = glrTL[i % 2], k_tmL[i % 2], v_tmL[i % 2]
                x_, _ = tile_front((d_xpre if XSRC_PRE else d_xown)[i * 128:(i + 1) * 128, :], True, None)
                mix = xstore[i]
                if SUB < -3:
                    continue
                b1 = psum(); proj_tm((640, 768), b1, 128)
                b2 = psum(); proj_tm((1024, 1536), b2, 512)
                b3 = psum(); proj_tm((1536, 2048), b3, 512)
                b4 = psum(); proj_tm((2048, 2304), b4, 256)
                P.op("act", (lambda cur, b1: lambda e: e.copy(out=av[cur][:], in_=b1[:, 0:128]))(cur, b1), reads=[b1], writes=[av[cur]])
                P.op("act", (lambda b2, k_tm: lambda e: e.copy(out=k_tm[:], in_=b2[:, 0:256]))(b2, k_tm), reads=[b2], writes=[k_tm])
                P.op("dve", (lambda b2, v_tm: lambda e: e.tensor_copy(out=v_tm[:, 0:256], in_=b2[:, 256:512]))(b2, v_tm), reads=[b2], writes=[(v_tm, 0)])
                P.op("dve", (lambda b3, v_tm: lambda e: e.tensor_copy(out=v_tm[:, 256:512], in_=b3[:, 0:256]))(b3, v_tm), reads=[b3], writes=[(v_tm, 1)])
                P.op("act", (lambda b3: lambda e: e.activation(out=sg_[:, 0:256], in_=b3[:, 256:512], func=AF.Silu))(b3), reads=[b3], writes=[(sg_, 0)])
                P.op("act", (lambda b4: lambda e: e.activation(out=sg_[:, 256:512], in_=b4[:, 0:256], func=AF.Silu))(b4), reads=[b4], writes=[(sg_, 1)])
                if SUB < -2:
                    continue
                f1 = psum()
                for c in range(4):
                    proj_fm((lambda c: lambda kc: w_in_r[:, kc, c * 128:(c + 1) * 128])(c), f1, c * 128)
                f2 = psum()
                proj_fm(lambda kc: w_in_r[:, kc, 512:640], f2, 0)
                proj_fm(lambda kc: w_in_r[:, kc, 768:896], f2, 128)
                proj_fm(lambda kc: w_in_r[:, kc, 896:1024], f2, 256)
                f3 = psum()
                proj_fm(lambda kc: w_in_r[:, kc, 1024:1152], f3, 0)
                proj_fm(lambda kc: w_in_r[:, kc, 1152:1280], f3, 128)
                f4 = psum()
                proj_fm(lambda kc: w_in_r[:, kc, 2192:2320], f4, 0)
                if SUB < -1:
                    continue
                P.op("act", (lambda f1: lambda e: e.activation(out=aqTp[0:64, 0:4, :], in_=f1[0:64, :].rearrange("p (a b) -> p a b", a=4), func=AF.Copy, scale=0.125))(f1),
                     reads=[f1], writes=[(aqTp, 0)])
                P.op("act", (lambda f1: lambda e: e.activation(out=aqTp[64:128, 4:8, :], in_=f1[64:128, :].rearrange("p (a b) -> p a b", a=4), func=AF.Copy, scale=0.125))(f1),
                     reads=[f1], writes=[(aqTp, 1)])
                P.op("dve", (lambda cur, f2: lambda e: e.tensor_copy(out=akT[cur][:], in_=f2[:, 0:128]))(cur, f2), reads=[f2], writes=[akT[cur]])
                P.op("dve", (lambda f2: lambda e: e.tensor_copy(out=raw[:, 0:2, :], in_=f2[:, 128:384].rearrange("p (a b) -> p a b", a=2)))(f2), reads=[f2], writes=[(raw, 0)])
                P.op("act", (lambda f3: lambda e: e.copy(out=raw[:, 2:4, :], in_=f3[:, 0:256].rearrange("p (a b) -> p a b", a=2)))(f3), reads=[f3], writes=[(raw, 1)])
                P.op("dve", (lambda f4, glrT: lambda e: e.tensor_copy(out=glrT[:], in_=f4[:, 0:128]))(f4, glrT), reads=[f4], writes=[glrT])

                msk = mask0 if i == 0 else maskr
                batt = banks[6]
                mxv, nmx, rsum, es, rden = (st8[:, j, :] for j in range(5))
                def G1():
                    gla_common(True, None, i % 2)

                def G2():
                    bat = psum()
                    for hh in range(4):
                        mm(bat, bat[:, hh * 128:(hh + 1) * 128], kt[:, hh // 2, :], gqTp[:, hh, :], True, True, [kt, gqTp])
                    P.op("dve", (lambda bat: lambda e: e.tensor_tensor(out=attTm[:], in0=bat[:, :].rearrange("p (a b) -> p a b", a=4),
                                                                      in1=tri[:, :].unsqueeze(1).to_broadcast([128, 4, 128]), op=ALU.mult))(bat),
                         reads=[bat, tri], writes=[attTm])

                def G3():
                    bo = banks[7]
                    for hh in range(4):
                        mm(bo, bo[:, hh * 128:(hh + 1) * 128], gqTp[:, hh, :], S_sb[:, hh // 2, :], True, False, [gqTp, S_sb])
                        mm(bo, bo[:, hh * 128:(hh + 1) * 128], attTm[:, hh, :], v_tm[:, hh * 128:(hh + 1) * 128], False, True, [attTm, v_tm])
                    state_update(i % 2)

                def G4():
                    bo = banks[7]
                    for hh in range(4):
                        P.op("act", (lambda hh, bo: lambda e: e.activation(out=gtmp[:, hh * 128:(hh + 1) * 128], in_=bo[:, hh * 128:(hh + 1) * 128],
                                                                          func=AF.Square, accum_out=ss4[:, hh:hh + 1]))(hh, bo), reads=[bo], writes=[(gtmp, hh), (ss4, hh)])
                    rstd_from_ss(ss4, ss4[:, :], 128)
                    for hh in range(4):
                        P.op("dve", (lambda hh, bo: lambda e: e.scalar_tensor_tensor(out=gtmp[:, hh * 128:(hh + 1) * 128], in0=bo[:, hh * 128:(hh + 1) * 128],
                                                                                    scalar=ss4[:, hh:hh + 1], in1=gnw_bc[:], op0=ALU.mult, op1=ALU.mult))(hh, bo),
                             reads=[bo, ss4, gnw_bc], writes=[(gtmp, hh)])
                    P.op("pool", (lambda mix: lambda e: e.tensor_tensor(out=mix[:, 512:1024], in0=gtmp[:], in1=sg_[:], op=ALU.mult))(mix),
                         reads=[gtmp, sg_], writes=[(mix, 1)])


                att_state = {}

                def A1(hg):
                    sbk = [psum(), psum()]
                    for j in range(4):
                        hh = hg * 4 + j
                        bk = sbk[j // 2]
                        c0 = (j % 2) * 256
                        mm(bk, bk[:, c0:c0 + 128], aqTp[:, hh, :], akT[prv][:, :], True, True, [aqTp, akT[prv]])
                        mm(bk, bk[:, c0 + 128:c0 + 256], aqTp[:, hh, :], akT[cur][:, :], True, True, [aqTp, akT[cur]])
                    for jj in range(2):
                        P.op("dve", (lambda jj, bk, msk: lambda e: e.tensor_tensor(out=sc[:, 2 * jj:2 * jj + 2, :], in0=bk[:, :].rearrange("p (a b) -> p a b", a=2),
                                                                                   in1=msk[:, :].unsqueeze(1).to_broadcast([128, 2, 256]), op=ALU.add))(jj, sbk[jj], msk),
                             reads=[sbk[jj], msk], writes=[(sc, jj)])
                    att_state['sbk'] = sbk

                def A2(hg):
                    sbk = att_state['sbk']
                    hs = slice(hg * 4, hg * 4 + 4)
                    P.op("dve", (lambda hs: lambda e: e.tensor_reduce(out=mxv[:, hs], in_=sc[:], axis=AX.X, op=ALU.max))(hs), reads=[sc], writes=[(st8, "mx")])
                    P.op("dve", (lambda hs: lambda e: e.tensor_tensor(out=mxv[:, hs], in0=mxv[:, hs], in1=sink_bc[:, hs], op=ALU.max))(hs),
                         reads=[(st8, "mx"), sink_bc], writes=[(st8, "mx")])
                    P.op("dve", (lambda hs: lambda e: e.tensor_scalar(out=nmx[:, hs], in0=mxv[:, hs], scalar1=-1.0, scalar2=None, op0=ALU.mult))(hs),
                         reads=[(st8, "mx")], writes=[(st8, "nmx")])
                    for j in range(4):
                        hh = hg * 4 + j
                        P.op("act", (lambda j, hh: lambda e: e.activation(out=sc[:, j, :], in_=sc[:, j, :], func=AF.Exp, bias=nmx[:, hh:hh + 1],
                                                                          accum_out=rsum[:, hh:hh + 1]))(j, hh),
                             reads=[(sc, j // 2), (st8, "nmx")], writes=[(sc, j // 2), (st8, ("rs", hh))])

                def A3(hg):
                    tb = [psum(), psum()]
                    for j in range(4):
                        for blk in range(2):
                            q = j * 2 + blk
                            bk = tb[q // 4]
                            tr(bk, bk[:, (q % 4) * 128:(q % 4 + 1) * 128], sc[:, j, blk * 128:(blk + 1) * 128], [(sc, j // 2)])
                    P.op("act", (lambda bk: lambda e: e.copy(out=PT[:, 0:4, :], in_=bk[:, :].rearrange("p (a b) -> p a b", a=4)))(tb[0]), reads=[tb[0]], writes=[(PT, 0)])
                    P.op("dve", (lambda bk: lambda e: e.tensor_copy(out=PT[:, 4:8, :], in_=bk[:, :].rearrange("p (a b) -> p a b", a=4)))(tb[1]), reads=[tb[1]], writes=[(PT, 1)])
                    att_state['tb'] = tb

                def A4(hg):
                    tb = att_state['tb']
                    for j in range(4):
                        hh = hg * 4 + j
                        for blk in range(2):
                            q = j * 2 + blk
                            avb = av[prv] if blk == 0 else av[cur]
                            mm(batt, batt[:, hh * 64:(hh + 1) * 64], PT[:, q, :], avb[:, hg * 64:(hg + 1) * 64], blk == 0, blk == 1, [(PT, q // 4), avb])

                def ATTF():
                    P.op("dve", lambda e: e.tensor_tensor(out=es[:, :], in0=sink_bc[:], in1=mxv[:, :], op=ALU.subtract), reads=[sink_bc, (st8, "mx")], writes=[(st8, "es")])
                    P.op("act", lambda e: e.activation(out=es[:, :], in_=es[:, :], func=AF.Exp), reads=[(st8, "es")], writes=[(st8, "es")])
                    P.op("dve", lambda e: e.tensor_tensor(out=rden[:, :], in0=rsum[:, :], in1=es[:, :], op=ALU.add),
                         reads=[(st8, "es")] + [(st8, ("rs", hh)) for hh in range(8)], writes=[(st8, "rden")])
                    P.op("dve", lambda e: e.reciprocal(out=rden[:, :], in_=rden[:, :]), reads=[(st8, "rden")], writes=[(st8, "rden")])
                    P.op("dve", (lambda mix, batt: lambda e: e.tensor_tensor(out=mix[:, 0:512].rearrange("p (a b) -> p a b", a=8),
                                                                            in0=batt[:, :].rearrange("p (a b) -> p a b", a=8),
                                                                            in1=rden[:, :].unsqueeze(2).to_broadcast([128, 8, 64]), op=ALU.mult))(mix, batt),
                         reads=[batt, (st8, "rden")], writes=[(mix, 0)])

                if SUB < 1:
                    continue
                A1(0); G1(); A2(0); G2(); A3(0); G3(); A4(0); A1(1); G4(); A2(1); A3(1); A4(1); ATTF()
                if DEBUG:
                    P.dma("sp", (lambda mix, i: lambda e: e.dma_start(out=dbg["dbg_mix"][i * 128:(i + 1) * 128, :], in_=mix[:]))(mix, i), reads=[mix])
            P.barrier(skip_queue="pool")
            P.flush()
            maybe_stop(2)

        with ExitStack() as ph:
            w_out = sb(ph, "w_out", [128, 8, D], F32R)
            if STAGE >= 3:
                load_round(w_out, wout_r, D, "b")
            G1t = sb(ph, "G1t", [128, D])
            xt = [sb(ph, f"xta{j}", [128, D]) for j in range(2)]
            mT = [sb(ph, f"mT{j}", [128, 8, 128], F32R) for j in range(2)]
            tmp = sb(ph, "tmpa", [128, D])
            load_bc(G1t, 2)
            w_out_r = w_out.t[:]
            for i in range(NT if STAGE >= 3 else 0):
                mix = xstore[i]
                x_ = xt[i % 2]
                m_ = mT[i % 2]
                P.dma("sp", (lambda x_, i: lambda e: e.dma_start(out=x_[:], in_=d_xown[i * 128:(i + 1) * 128, :]))(x_, i), writes=[x_])
                ba, bb = psum(), psum()
                for j in range(8):
                    bk = ba if j < 4 else bb
                    tr(bk, bk[:, (j % 4) * 128:(j % 4 + 1) * 128], mix[:, j * 128:(j + 1) * 128], [mix])
                P.op("act", (lambda m_, ba: lambda e: e.copy(out=m_[:, 0:4, :], in_=ba[:, :].rearrange("p (a b) -> p a b", a=4)))(m_, ba), reads=[ba], writes=[(m_, 0)])
                P.op("dve", (lambda m_, bb: lambda e: e.tensor_copy(out=m_[:, 4:8, :], in_=bb[:, :].rearrange("p (a b) -> p a b", a=4)))(m_, bb), reads=[bb], writes=[(m_, 1)])
                for half in range(2):
                    bk = psum()
                    for kc in range(8):
                        mm(bk, bk[:, :], m_[:, kc, :], w_out_r[:, kc, half * 512:(half + 1) * 512], kc == 0, kc == 7, [(m_, 0 if kc < 4 else 1), (w_out, kc)])
                    hsl = slice(half * 512, (half + 1) * 512)
                    P.op("dve", (lambda bk, hsl: lambda e: e.tensor_tensor(out=tmp[:, hsl], in0=bk[:, :], in1=G1t[:, hsl], op=ALU.mult))(bk, hsl),
                         reads=[bk, G1t], writes=[(tmp, half)])
                    P.op("pool", (lambda mix, x_, hsl: lambda e: e.tensor_tensor(out=mix[:, hsl], in0=tmp[:, hsl], in1=x_[:, hsl], op=ALU.add))(mix, x_, hsl),
                         reads=[(tmp, half), x_], writes=[mix])
                if DEBUG:
                    P.dma("sp", (lambda mix, i: lambda e: e.dma_start(out=dbg["dbg_x1"][i * 128:(i + 1) * 128, :], in_=mix[:]))(mix, i), reads=[mix])
            P.barrier(skip_queue="pool")
            P.flush()
            maybe_stop(3)

        idxst = [sb(top, f"idx{i}", [128, 128], I32) for i in range(NT)]
        gatest = [sb(top, f"gate{i}", [128, 128]) for i in range(NT)]

        def norm2_h2(ph_bufs, x1, h2, A2t, B2t, ss, add_eng="pool"):
            P.op("act", lambda e: e.activation(out=h2[:], in_=x1[:], func=AF.Square, accum_out=ss[:, 0:1]), reads=[x1], writes=[h2, ss])
            rstd_from_ss(ss, ss[:, 0:1], D)
            P.op("dve", lambda e: e.scalar_tensor_tensor(out=h2[:], in0=x1[:], scalar=ss[:, 0:1], in1=A2t[:], op0=ALU.mult, op1=ALU.mult),
                 reads=[x1, ss, A2t], writes=[h2])
            P.op(add_eng, lambda e: e.tensor_tensor(out=h2[:], in0=h2[:], in1=B2t[:], op=ALU.add), reads=[h2, B2t], writes=[h2])

        with ExitStack() as ph:
            wqh = sb(ph, "wqh", [128, 8, 1024], F32R)
            skT = sb(ph, "skT", [128, 16, 128])
            A2t = sb(ph, "A2t", [128, D])
            B2t = sb(ph, "B2t", [128, D])
            h2 = sb(ph, "h2b", [128, D])
            h2T = sb(ph, "h2T", [128, 8, 128], F32R)
            qT = sb(ph, "qT", [128, 8, 128])
            scsL = [sb(ph, f"scs{j}", [128, 8, 128]) for j in range(2)]
            sc2 = sb(ph, "sc2", [128, 8, 128])
            topv = sb(ph, "topv", [128, 8, 16])
            topi = sb(ph, "topi", [128, 8, 16], U32)
            topif = sb(ph, "topif", [128, 8, 16])
            cand = sb(ph, "cand", [128, 4, 256])
            cand2 = sb(ph, "cand2", [128, 4, 256])
            bestv = sb(ph, "bestv", [128, 4, 16])
            pos = sb(ph, "pos", [128, 4, 16], U32)
            pab = sb(ph, "pab", [128, 2, 64], U32)
            pabf = sb(ph, "pabf", [128, 2, 64])
            ohs = [sb(ph, f"oh{j}", [128, 64, 16]) for j in range(2)]
            isel = sb(ph, "isel", [128, 2, 64])
            ef = sb(ph, "ef", [128, 64])
            gs = sb(ph, "gs", [128, 3, 4])
            eg = sb(ph, "eg", [128, 4, 16])
            ssb = sb(ph, "ssb", [128, 1])
            P.dma("sp", lambda e: e.dma_start(out=skT[:], in_=d_skT), writes=[skT])
            load_bc(A2t, 4)
            load_bc(B2t, 3)
            for ps_ in range(2 if STAGE >= 4 else 0):
                load_round(wqh, wq_r[:, :, ps_ * 1024:(ps_ + 1) * 1024], 1024, f"q{ps_}")
                wqh_r = wqh.t[:]
                def b1_front(i, scs):
                    x1 = xstore[i]
                    norm2_h2(None, x1, h2, A2t, B2t, ssb)
                    ba, bb = psum(), psum()
                    for j in range(8):
                        bk = ba if j < 4 else bb
                        tr(bk, bk[:, (j % 4) * 128:(j % 4 + 1) * 128], h2[:, j * 128:(j + 1) * 128], [h2])
                    P.op("act", (lambda ba: lambda e: e.copy(out=h2T[:, 0:4, :], in_=ba[:, :].rearrange("p (a b) -> p a b", a=4)))(ba), reads=[ba], writes=[(h2T, 0)])
                    P.op("act", (lambda bb: lambda e: e.copy(out=h2T[:, 4:8, :], in_=bb[:, :].rearrange("p (a b) -> p a b", a=4)))(bb), reads=[bb], writes=[(h2T, 1)])
                    qb = [psum(), psum()]
                    for g in range(8):
                        bk = qb[g // 4]
                        for kc in range(8):
                            mm(bk, bk[:, (g % 4) * 128:(g % 4 + 1) * 128], wqh_r[:, kc, g * 128:(g + 1) * 128], h2T[:, kc, :], kc == 0, kc == 7,
                               [(wqh, kc), (h2T, 0 if kc < 4 else 1)])
                    P.op("act", (lambda bk: lambda e: e.copy(out=qT[:, 0:4, :], in_=bk[:, :].rearrange("p (a b) -> p a b", a=4)))(qb[0]), reads=[qb[0]], writes=[(qT, 0)])
                    P.op("act", (lambda bk: lambda e: e.copy(out=qT[:, 4:8, :], in_=bk[:, :].rearrange("p (a b) -> p a b", a=4)))(qb[1]), reads=[qb[1]], writes=[(qT, 1)])
                    sbk = [psum(), psum()]
                    for g in range(8):
                        bk = sbk[g // 4]
                        mm(bk, bk[:, (g % 4) * 128:(g % 4 + 1) * 128], qT[:, g, :], skT[:, ps_ * 8 + g, :], True, True, [(qT, g // 4), skT])
                    P.op("act", (lambda bk: lambda e: e.copy(out=scs[:, 0:4, :], in_=bk[:, :].rearrange("p (a b) -> p a b", a=4)))(sbk[0]), reads=[sbk[0]], writes=[(scs, 0)])
                    P.op("act", (lambda bk: lambda e: e.copy(out=scs[:, 4:8, :], in_=bk[:, :].rearrange("p (a b) -> p a b", a=4)))(sbk[1]), reads=[sbk[1]], writes=[(scs, 1)])
                def b1_topk(i, scs):
                    for g in range(8):
                        P.op("dve", (lambda g: lambda e: e.max(out=topv[:, g, 0:8], in_=scs[:, g, :]))(g), reads=[(scs, g // 4)], writes=[(topv, (g, 0))])
                    for g in range(8):
                        P.op("dve", (lambda g: lambda e: e.match_replace(out=sc2[:, g, :], in_to_replace=topv[:, g, 0:8], in_values=scs[:, g, :], imm_value=-1e30))(g),
                             reads=[(scs, g // 4), (topv, (g, 0))], writes=[(sc2, g)])
                    for g in range(8):
                        P.op("dve", (lambda g: lambda e: e.max(out=topv[:, g, 8:16], in_=sc2[:, g, :]))(g), reads=[(sc2, g)], writes=[(topv, (g, 1))])
                    for g in range(8):
                        P.op("dve", (lambda g: lambda e: e.max_index(out=topi[:, g, 0:8], in_max=topv[:, g, 0:8], in_values=scs[:, g, :]))(g),
                             reads=[(scs, g // 4), (topv, (g, 0))], writes=[(topi, (g, 0))])
                    for g in range(8):
                        P.op("dve", (lambda g: lambda e: e.max_index(out=topi[:, g, 8:16], in_max=topv[:, g, 8:16], in_values=scs[:, g, :]))(g),
                             reads=[(scs, g // 4), (topv, (g, 1))], writes=[(topi, (g, 1))])
                    P.op("dve", lambda e: e.tensor_copy(out=topif[:], in_=topi[:]), reads=[topi], writes=[topif])
                    tv4 = topv[:, :, :].rearrange("p (h two) a -> p h two a", two=2)
                    ti4 = topif[:, :, :].rearrange("p (h two) a -> p h two a", two=2)
                    P.op("dve", lambda e: e.tensor_tensor(out=cand[:, :, :].rearrange("p h (a b) -> p h a b", a=16),
                                                          in0=tv4[:, :, 0, :].unsqueeze(3).to_broadcast([128, 4, 16, 16]),
                                                          in1=tv4[:, :, 1, :].unsqueeze(2).to_broadcast([128, 4, 16, 16]), op=ALU.add),
                         reads=[topv], writes=[cand])
                    for hh in range(4):
                        P.op("dve", (lambda hh: lambda e: e.max(out=bestv[:, hh, 0:8], in_=cand[:, hh, :]))(hh), reads=[cand], writes=[(bestv, (hh, 0))])
                    for hh in range(4):
                        P.op("dve", (lambda hh: lambda e: e.match_replace(out=cand2[:, hh, :], in_to_replace=bestv[:, hh, 0:8], in_values=cand[:, hh, :], imm_value=-1e30))(hh),
                             reads=[cand, (bestv, (hh, 0))], writes=[(cand2, hh)])
                    for hh in range(4):
                        P.op("dve", (lambda hh: lambda e: e.max(out=bestv[:, hh, 8:16], in_=cand2[:, hh, :]))(hh), reads=[(cand2, hh)], writes=[(bestv, (hh, 1))])
                    for hh in range(4):
                        P.op("dve", (lambda hh: lambda e: e.max_index(out=pos[:, hh, 0:8], in_max=bestv[:, hh, 0:8], in_values=cand[:, hh, :]))(hh),
                             reads=[cand, (bestv, (hh, 0))], writes=[(pos, (hh, 0))])
                    for hh in range(4):
                        P.op("dve", (lambda hh: lambda e: e.max_index(out=pos[:, hh, 8:16], in_max=bestv[:, hh, 8:16], in_values=cand[:, hh, :]))(hh),
                             reads=[cand, (bestv, (hh, 1))], writes=[(pos, (hh, 1))])
                    posf = pos[:, :, :].rearrange("p h r -> p (h r)")
                    P.op("dve", lambda e: e.tensor_single_scalar(out=pab[:, 0, :], in_=posf, scalar=4, op=ALU.logical_shift_right), reads=[pos], writes=[(pab, 0)])
                    P.op("dve", lambda e: e.tensor_single_scalar(out=pab[:, 1, :], in_=posf, scalar=15, op=ALU.bitwise_and), reads=[pos], writes=[(pab, 1)])
                    P.op("dve", lambda e: e.tensor_copy(out=pabf[:, 0, :], in_=pab[:, 0, :]), reads=[(pab, 0)], writes=[(pabf, 0)])
                    P.op("dve", lambda e: e.tensor_copy(out=pabf[:, 1, :], in_=pab[:, 1, :]), reads=[(pab, 1)], writes=[(pabf, 1)])
                    for ab in range(2):
                        eng = "dve"
                        P.op(eng, (lambda ab: lambda e: e.tensor_tensor(out=ohs[ab][:], in0=pabf[:, ab, :].unsqueeze(2).to_broadcast([128, 64, 16]),
                                                                        in1=iota16[:, :].unsqueeze(1).to_broadcast([128, 64, 16]), op=ALU.is_equal))(ab),
                             reads=[(pabf, ab), iota16], writes=[ohs[ab]])
                    for ab in range(2):
                        eng = "dve" if ab == 0 else "pool"
                        P.op(eng, (lambda ab: lambda e: e.tensor_tensor(out=ohs[ab][:, :, :].rearrange("p (h r) a -> p h r a", h=4),
                                                                        in0=ohs[ab][:, :, :].rearrange("p (h r) a -> p h r a", h=4),
                                                                        in1=ti4[:, :, ab, :].unsqueeze(2).to_broadcast([128, 4, 16, 16]), op=ALU.mult))(ab),
                             reads=[ohs[ab], topif], writes=[ohs[ab]])
                    for ab in range(2):
                        P.op("dve", (lambda ab: lambda e: e.tensor_reduce(out=isel[:, ab, :], in_=ohs[ab][:], axis=AX.X, op=ALU.add))(ab), reads=[ohs[ab]], writes=[(isel, ab)])
                    P.op("dve", lambda e: e.scalar_tensor_tensor(out=ef[:], in0=isel[:, 0, :], scalar=128.0, in1=isel[:, 1, :], op0=ALU.mult, op1=ALU.add),
                         reads=[isel], writes=[ef])
                    csl = slice(ps_ * 64, ps_ * 64 + 64)
                    P.op("dve", (lambda i, csl: lambda e: e.tensor_copy(out=idxst[i][:, csl], in_=ef[:]))(i, csl), reads=[ef], writes=[(idxst[i], ps_)])
                    P.op("dve", lambda e: e.tensor_tensor(out=eg[:], in0=bestv[:], in1=bestv[:, :, 0:1].to_broadcast([128, 4, 16]), op=ALU.subtract),
                         reads=[bestv], writes=[eg])
                    P.op("act", lambda e: e.activation(out=eg[:], in_=eg[:], func=AF.Exp), reads=[eg], writes=[eg])
                    P.op("dve", lambda e: e.tensor_reduce(out=gs[:, 0, :], in_=eg[:], axis=AX.X, op=ALU.add), reads=[eg], writes=[gs])
                    P.op("dve", lambda e: e.reciprocal(out=gs[:, 1, :], in_=gs[:, 0, :]), reads=[gs], writes=[gs])
                    P.op("dve", (lambda i, csl: lambda e: e.tensor_tensor(out=gatest[i][:, csl].rearrange("p (h r) -> p h r", h=4), in0=eg[:],
                                                                          in1=gs[:, 1, :].unsqueeze(2).to_broadcast([128, 4, 16]), op=ALU.mult))(i, csl),
                         reads=[eg, gs], writes=[(gatest[i], ps_)])
                    if DEBUG and ps_ == 1:
                        P.dma("sp", (lambda i: lambda e: e.dma_start(out=dbg["dbg_idx"][i * 128:(i + 1) * 128, :], in_=idxst[i][:]))(i), reads=[idxst[i]])
                        P.dma("sp", (lambda i: lambda e: e.dma_start(out=dbg["dbg_gate"][i * 128:(i + 1) * 128, :], in_=gatest[i][:]))(i), reads=[gatest[i]])
                b1_front(0, scsL[0])
                for i in range(NT):
                    if i + 1 < NT:
                        b1_front(i + 1, scsL[(i + 1) % 2])
                    b1_topk(i, scsL[i % 2])
                    issue_cast()
                    issue_cast()
            P.barrier(skip_queue="pool")
            P.flush()
            maybe_stop(4)

        with ExitStack() as ph:
            NB = 20
            while castn[0] < NCAST:
                issue_cast()
            A2t = sb(ph, "A2t2", [128, D])
            B2t = sb(ph, "B2t2", [128, D])
            G2t = sb(ph, "G2t", [128, D])
            FWt = sb(ph, "FWt", [128, D])
            h2 = [sb(ph, f"h2c{j}", [128, D]) for j in range(2)]
            ring = [sb(ph, f"rb{j}", [128, 2 * D], BF16) for j in range(NB)]
            hraw = [sb(ph, f"hraw{j}", [128, 128]) for j in range(2)]
            hgel = [sb(ph, f"hgel{j}", [128, 128]) for j in range(2)]
            Dr = [sb(ph, f"Dr{j}", [128, 128], BF16) for j in range(8)]
            ot = [sb(ph, f"ot{j}", [128, D]) for j in range(2)]
            ssb = sb(ph, "ssb2", [128, 2])
            load_bc(A2t, 4)
            load_bc(B2t, 3)
            load_bc(G2t, 5)
            P.dma("sp", lambda e: e.dma_start(out=FWt[:], in_=bcast_row(h_fnw, D)), writes=[FWt])
            cu = cv = cd = 0
            YB = [(banks[6], banks[7]), (banks[4], banks[5])]
            NTB = NT if STAGE >= 5 else 0

            def b2_final(i):
                x1 = xstore[i]
                h2_ = h2[i % 2]
                o_ = ot[i % 2]
                Y0, Y1 = YB[i % 2]
                P.op("dve", (lambda o_: lambda e: e.tensor_tensor(out=o_[:, 0:512], in0=Y0[:, :], in1=G2t[:, 0:512], op=ALU.mult))(o_), reads=[Y0, G2t], writes=[(o_, 0)])
                P.op("dve", (lambda o_: lambda e: e.tensor_tensor(out=o_[:, 512:1024], in0=Y1[:, :], in1=G2t[:, 512:1024], op=ALU.mult))(o_), reads=[Y1, G2t], writes=[(o_, 1)])
                P.op("dve", (lambda o_, x1: lambda e: e.tensor_tensor(out=o_[:], in0=o_[:], in1=x1[:], op=ALU.add))(o_, x1), reads=[o_, x1], writes=[o_])
                P.op("act", (lambda o_, h2_: lambda e: e.activation(out=h2_[:], in_=o_[:], func=AF.Square, accum_out=ssb[:, 1:2]))(o_, h2_), reads=[o_], writes=[h2_, (ssb, "f")])
                P.op("act", lambda e: e.activation(out=ssb[:, 1:2], in_=ssb[:, 1:2], func=AF.Ln, scale=1.0 / D, bias=EPS), reads=[(ssb, "f")], writes=[(ssb, "f")])
                P.op("act", lambda e: e.activation(out=ssb[:, 1:2], in_=ssb[:, 1:2], func=AF.Exp, scale=-0.5), reads=[(ssb, "f")], writes=[(ssb, "f")])
                P.op("dve", (lambda o_: lambda e: e.scalar_tensor_tensor(out=o_[:], in0=o_[:], scalar=ssb[:, 1:2], in1=FWt[:], op0=ALU.mult, op1=ALU.mult))(o_),
                     reads=[o_, (ssb, "f"), FWt], writes=[o_])
                P.dma("sp", (lambda o_, i: lambda e: e.dma_start(out=d_out[i * 128:(i + 1) * 128, :], in_=o_[:]))(o_, i), reads=[o_])

            if NTB:
                norm2_h2(None, xstore[0], h2[0], A2t, B2t, ssb, add_eng="dve")
            for i in range(NTB):
                x1 = xstore[i]
                h2_ = h2[i % 2]
                hr, hgb, o_ = hraw[i % 2], hgel[i % 2], ot[i % 2]
                Y0, Y1 = YB[i % 2]
                for k4 in range(32):
                    if k4 == 2 and i > 0:
                        b2_final(i - 1)
                    if k4 == 16 and i + 1 < NTB:
                        norm2_h2(None, xstore[i + 1], h2[(i + 1) % 2], A2t, B2t, ssb, add_eng="dve")
                    bufs = []
                    for r in range(4):
                        k = k4 * 4 + r
                        rb = ring[cu % NB]; cu += 1
                        bufs.append(rb)
                        P.dma("pool", (lambda rb, i, k: lambda e: e.indirect_dma_start(out=rb[:], out_offset=None, in_=d_t16,
                                                                                       in_offset=bass.IndirectOffsetOnAxis(ap=idxst[i][:, k:k + 1], axis=0)))(rb, i, k),
                              reads=[idxst[i], t16b], writes=[rb])
                        P.op("dve", (lambda rb, k, hr, h2_: lambda e: e.scalar_tensor_tensor(out=rb[:, 0:D], in0=rb[:, 0:D], scalar=1.0, in1=h2_[:], op0=ALU.mult, op1=ALU.mult,
                                                                                             accum_out=hr[:, k:k + 1]))(rb, k, hr, h2_),
                             reads=[rb, h2_], writes=[(rb, "u"), (hr, k)])
                    ksl = slice(k4 * 4, k4 * 4 + 4)
                    P.op("act", (lambda hgb, hr, ksl: lambda e: e.activation(out=hgb[:, ksl], in_=hr[:, ksl], func=AF.Gelu))(hgb, hr, ksl),
                         reads=[(hr, k4 * 4 + r) for r in range(4)], writes=[(hgb, k4)])
                    P.op("dve", (lambda hgb, i, ksl: lambda e: e.tensor_tensor(out=hgb[:, ksl], in0=hgb[:, ksl], in1=gatest[i][:, ksl], op=ALU.mult))(hgb, i, ksl),
                         reads=[(hgb, k4), gatest[i]], writes=[(hgb, k4)])
                    for r in range(4):
                        k = k4 * 4 + r
                        rb = bufs[r]
                        dk_ = Dr[cd % 8]; cd += 1
                        P.op("act", (lambda dk_, hgb, k: lambda e: e.activation(out=dk_[:], in_=ident[:], func=AF.Copy, scale=hgb[:, k:k + 1]))(dk_, hgb, k),
                             reads=[ident, (hgb, k4)], writes=[dk_])
                        mm(Y0, Y0[:, :], dk_[:, :], rb[:, D:D + 512], k == 0, k == 127, [dk_, rb])
                        mm(Y1, Y1[:, :], dk_[:, :], rb[:, D + 512:2 * D], k == 0, k == 127, [dk_, rb])
            if NTB:
                b2_final(NTB - 1)
            P.wait_all_dma("sp")
            P.flush()
    except _Stop:
        pass
    top.close()
    return nc


def _host_inputs(inp):
    f = lambda a: np.ascontiguousarray(np.asarray(a, dtype=np.float32))
    x = f(inp["x"])
    c = f(inp["c"])
    ar = np.arange(128)
    q = ar[:, None]
    j = np.arange(256)[None, :]
    maskr = np.where((j > q) & (j <= q + 128), 0.0, -30000.0).astype(np.float32)
    mask0_first = maskr.copy()
    mask0_first[:, :128] = -30000.0
    common = {
        "w_ada": f(inp["w_ada"][0]), "b_ada": f(inp["b_ada"][0]).reshape(1, -1), "norm1_w": f(inp["norm1_w"][0]).reshape(1, -1),
        "w_in": f(inp["w_in"][0]), "sinks": f(inp["attn_sinks"][0]).reshape(1, -1), "gate_up": f(inp["gla_gate_up"][0]),
        "gate_bias": f(inp["gla_gate_bias"][0]).reshape(1, -1), "gla_norm_w": f(inp["gla_norm_w"][0]).reshape(1, -1),
        "w_out": f(inp["w_out"][0]), "norm2_w": f(inp["norm2_w"][0]).reshape(1, -1), "peer_wq": f(inp["peer_wq"][0]),
        "skT": f(np.transpose(np.asarray(inp["peer_subkeys"][0], dtype=np.float32).reshape(16, 128, 128), (2, 0, 1))),
        "peer_uv": np.ascontiguousarray(np.concatenate([np.asarray(inp["peer_u"][0], np.float32), np.asarray(inp["peer_v"][0], np.float32)], axis=1)), "final_norm_w": f(inp["final_norm_w"]).reshape(1, -1),
        "ident": np.eye(128, dtype=np.float32),
        "tri": (ar[:, None] <= ar[None, :]).astype(np.float32),
        "triu": (ar[:, None] > ar[None, :]).astype(np.float32),
        "iota16": np.tile(np.arange(16, dtype=np.float32)[None, :], (128, 1)),
        "maskr": maskr,
    }
    maps = []
    for core in range(NCORES):
        b, s = core // 4, core % 4
        xpre = np.zeros((NPRE * 128, D), np.float32)
        nvalid = s * TOK
        if nvalid:
            xpre[NPRE * 128 - nvalid:] = x[b, :nvalid]
        pv = np.zeros((128, NPRE), np.float32)
        pv[:, NPRE - nvalid // 128:] = 1.0 if nvalid else 0.0
        m = dict(common)
        m["x_own"] = np.ascontiguousarray(x[b, s * TOK:(s + 1) * TOK])
        m["x_pre"] = xpre
        m["pvalid"] = pv
        m["mask0"] = mask0_first if s == 0 else maskr
        m["c_t"] = np.ascontiguousarray(c[b].reshape(8, 128).T)
        maps.append(m)
    return maps


def kernel(**inputs):
    maps = _host_inputs(inputs)
    nc = build_program()
    res = run_bass_kernel_spmd(nc, maps[:RUN_CORES], core_ids=list(range(RUN_CORES)))
    out = np.zeros((2, 8192, D), np.float32)
    for core in range(RUN_CORES):
        b, s = core // 4, core % 4
        out[b, s * TOK:(s + 1) * TOK] = np.asarray(res.results[core]["out"], dtype=np.float32)
    if DEBUG:
        _dbg_out["res"] = res.results
    return out
```

```python
from contextlib import ExitStack
import numpy as np
import concourse.bass as bass
import concourse.mybir as mybir
from concourse.bass_utils import run_bass_kernel_spmd

F32 = mybir.dt.float32
F32R = mybir.dt.float32r
I32 = mybir.dt.int32
BF16 = mybir.dt.bfloat16
U32 = mybir.dt.uint32
AF = mybir.ActivationFunctionType
ALU = mybir.AluOpType
AX = mybir.AxisListType

NCORES = 8
TOK = 2048
NT = TOK // 128
NPRE = 48
D = 1024
EPS = 1e-6
DEBUG = False
STAGE = 99
SUB = 99
XSRC_PRE = False
NT_RUN = 16
NPRE_RUN = 48
RUN_CORES = NCORES


class _Stop(Exception):
    pass
_dbg_out = {}

STREAMS = ("pe", "act", "dve", "pool", "sp")
DMA_RING = {"sp": 8, "act": 4, "pool": 16}


class Buf:
    def __init__(self, name, t, psum=False):
        self.name = name
        self.t = t
        self.regs = {}
        self.psum = psum

    def __getitem__(self, idx):
        return self.t[idx]


class Prog:
    def __init__(self, nc):
        self.nc = nc
        self.ops = {s: [] for s in STREAMS}
        self.seq = {s: 0 for s in STREAMS}
        self.dcount = {q: 0 for q in DMA_RING}
        self.waited = {s: {} for s in STREAMS}
        self.last_dma_tok = {}

    def _need(self, stream, tok, hazard, waits):
        if tok is None:
            return
        semkey, val, pstream, kind = tok
        if kind == "c" and pstream == stream:
            if stream == "pe":
                return
        w = self.waited[stream]
        if w.get(semkey, -1) >= val:
            return
        w[semkey] = val
        waits.append((semkey, val))

    def _collect(self, stream, reads, writes):
        waits = []
        for b, k in reads:
            regs = b.regs
            if k is None:
                for e in regs.values():
                    self._need(stream, e["w"], "RAW", waits)
            elif k in regs:
                self._need(stream, regs[k]["w"], "RAW", waits)
            elif None in regs:
                self._need(stream, regs[None]["w"], "RAW", waits)
        for b, k in writes:
            regs = b.regs
            if k is None:
                for e in regs.values():
                    self._need(stream, e["w"], "WAW", waits)
                    for t in e["r"].values():
                        self._need(stream, t, "WAR", waits)
            else:
                for kk in (k, None):
                    if kk in regs:
                        e = regs[kk]
                        self._need(stream, e["w"], "WAW", waits)
                        for t in e["r"].values():
                            self._need(stream, t, "WAR", waits)
        return waits

    def _record(self, tok, reads, writes):
        for b, k in reads:
            regs = b.regs
            if k is None:
                e = regs.setdefault(None, {"w": None, "r": {}})
            else:
                if k not in regs:
                    regs[k] = {"w": regs[None]["w"] if None in regs else None, "r": {}}
                e = regs[k]
            e["r"][tok[0]] = tok
        for b, k in writes:
            if k is None:
                b.regs = {None: {"w": tok, "r": {}}}
            else:
                b.regs[k] = {"w": tok, "r": {}}

    @staticmethod
    def _norm(lst):
        return [x if isinstance(x, tuple) else (x, None) for x in (lst or [])]

    def op(self, stream, emit, reads=None, writes=None):
        reads = self._norm(reads)
        writes = self._norm(writes)
        writes = writes + [(b, None) for b, k in reads if b.psum]
        reads = [(b, k) for b, k in reads if not b.psum]
        waits = self._collect(stream, reads, writes)
        self.seq[stream] += 1
        tok = (("c", stream), self.seq[stream], stream, "c")
        self._record(tok, reads, writes)
        self.ops[stream].append((waits, emit, (("c", stream), 1)))
        return tok

    def dma(self, queue, emit, reads=None, writes=None):
        reads = self._norm(reads)
        writes = self._norm(writes)
        waits = self._collect(queue, reads, writes)
        i = self.dcount[queue]
        self.dcount[queue] += 1
        K = DMA_RING[queue]
        slot = i % K
        semkey = ("d", queue, slot)
        if i >= K:
            self._need(queue, (semkey, 16 * (i // K), queue, "d"), "WAW", waits)
        val = 16 * (i // K + 1)
        tok = (semkey, val, queue, "d")
        self.last_dma_tok[semkey] = val
        self._record(tok, reads, writes)
        self.ops[queue].append((waits, emit, (semkey, 16)))
        return tok

    def barrier(self, skip_queue=None):
        for s in STREAMS:
            waits = []
            for s2 in STREAMS:
                if s2 != s and self.seq[s2] > 0:
                    self._need(s, (("c", s2), self.seq[s2], s2, "c"), "RAW", waits)
            for semkey, val in self.last_dma_tok.items():
                if skip_queue is not None and semkey[1] == skip_queue:
                    continue
                self._need(s, (semkey, val, semkey[1], "d"), "RAW", waits)
            if waits:
                self.ops[s].append((waits, None, None))

    def wait_all_dma(self, stream="sp"):
        waits = []
        for semkey, val in self.last_dma_tok.items():
            self._need(stream, (semkey, val, semkey[1], "d"), "RAW", waits)
        if waits:
            self.ops[stream].append((waits, None, None))

    def setup(self, stack):
        nc = self.nc
        self.sems = {}
        for s in STREAMS:
            self.sems[("c", s)] = stack.enter_context(nc.semaphore(f"c_{s}"))
        for q, K in DMA_RING.items():
            for j in range(K):
                self.sems[("d", q, j)] = stack.enter_context(nc.semaphore(f"d_{q}_{j}"))

    def flush(self):
        nc = self.nc
        sems = self.sems
        ops = self.ops
        self.ops = {s: [] for s in STREAMS}

        def run(stream):
            def f(eng):
                for waits, emit, inc in ops[stream]:
                    for semkey, val in waits:
                        eng.wait_ge(sems[semkey], val)
                    if emit is not None:
                        emit(eng).then_inc(sems[inc[0]], inc[1])
            return f

        with nc.Block() as block:
            block.tensor(run("pe"))
            block.scalar(run("act"))
            block.vector(run("dve"))
            block.gpsimd(run("pool"))
            block.sync(run("sp"))


def build_program():
    nc = bass.Bass("TRN2", target_bir_lowering=False)

    def din(name, shape, dt=F32):
        return nc.dram_tensor(name, shape, dt, kind="ExternalInput")

    d_xown = din("x_own", [TOK, D]).ap()
    d_xpre = din("x_pre", [NPRE * 128, D]).ap()
    d_pvalid = din("pvalid", [128, NPRE]).ap()
    d_mask0 = din("mask0", [128, 256]).ap()
    d_maskr = din("maskr", [128, 256]).ap()
    d_ct = din("c_t", [128, 8]).ap()
    h_wada = din("w_ada", [D, 6 * D])
    h_bada = din("b_ada", [1, 6 * D])
    h_n1w = din("norm1_w", [1, D])
    h_win = din("w_in", [D, 2320])
    h_sinks = din("sinks", [1, 8])
    d_gup = din("gate_up", [16, 256]).ap()
    d_gbias = din("gate_bias", [1, 256]).ap()
    h_gnw = din("gla_norm_w", [1, 128])
    h_wout = din("w_out", [D, D])
    h_n2w = din("norm2_w", [1, D])
    h_wq = din("peer_wq", [D, 2048])
    d_skT = din("skT", [128, 16, 128]).ap()
    d_puv = din("peer_uv", [16384, 2 * D]).ap()
    d_t16 = nc.dram_tensor("tab16", [16384, 2 * D], BF16, kind="Internal").ap()
    h_fnw = din("final_norm_w", [1, D])
    d_ident = din("ident", [128, 128]).ap()
    d_tri = din("tri", [128, 128]).ap()
    d_triu = din("triu", [128, 128]).ap()
    d_iota = din("iota16", [128, 16]).ap()
    d_out = nc.dram_tensor("out", [TOK, D], F32, kind="ExternalOutput").ap()
    d_bc = nc.dram_tensor("bc_scratch", [6, 128, D], F32, kind="Internal").ap()
    dbg = {}
    if DEBUG:
        for nm, shp, dt in (("dbg_mix", [TOK, D], F32), ("dbg_x1", [TOK, D], F32),
                            ("dbg_idx", [TOK, 128], I32), ("dbg_gate", [TOK, 128], F32),
                            ("dbg_bc", [6, 128, D], F32), ("dbg_hid", [TOK, 128], F32), ("dbg_hraw", [TOK, 128], F32), ("dbg_y", [TOK, D], F32)):
            dbg[nm] = nc.dram_tensor(nm, shp, dt, kind="ExternalOutput").ap()

    def bcast_row(handle, n, off=0):
        return bass.AP(handle, off, [[0, 128], [1, n]])

    wada_r = h_wada.ap().rearrange("(kc p) n -> p kc n", p=128)
    win_r = h_win.ap().rearrange("(kc p) n -> p kc n", p=128)
    wout_r = h_wout.ap().rearrange("(kc p) n -> p kc n", p=128)
    wq_r = h_wq.ap().rearrange("(kc p) n -> p kc n", p=128)

    P = Prog(nc)
    top = ExitStack()
    P.setup(top)

    def maybe_stop(k):
        return

    try:

        def sb(st, name, shape, dt=F32):
            return Buf(name, st.enter_context(nc.sbuf_tensor("s_" + name, shape, dt)))

        banks = [Buf(f"ps{j}", top.enter_context(nc.psum_tensor(f"ps{j}", [128, 512], F32)), psum=True) for j in range(8)]
        pcnt = [0]

        def psum():
            b = banks[pcnt[0] % 6]
            pcnt[0] += 1
            return b

        ident = sb(top, "ident", [128, 128])
        tri = sb(top, "tri", [128, 128])
        triu = sb(top, "triu", [128, 128])
        iota16 = sb(top, "iota16", [128, 16])
        ones = sb(top, "ones", [128, 128])
        for t_, d_ in ((ident, d_ident), (tri, d_tri), (triu, d_triu), (iota16, d_iota)):
            P.dma("sp", (lambda t_, d_: lambda e: e.dma_start(out=t_[:], in_=d_))(t_, d_), writes=[t_])
        P.op("dve", lambda e: e.memset(ones[:], 1.0), writes=[ones])

        t16b = Buf("tab16", None)
        NCAST = 64
        castn = [0]

        def issue_cast():
            j = castn[0]
            if j >= NCAST:
                return
            castn[0] += 1
            rows = 16384 // NCAST
            P.dma("pool", lambda e: e.dma_start(out=d_t16[j * rows:(j + 1) * rows, :], in_=d_puv[j * rows:(j + 1) * rows, :]), writes=[(t16b, j)])


        def mm(out_b, out_ap, lhsT, rhs, start, stop, reads):
            return P.op("pe", lambda e: e.matmul(out_ap, lhsT=lhsT, rhs=rhs, start=start, stop=stop),
                        reads=reads, writes=[out_b])

        def tr(out_b, out_ap, in_ap, reads):
            return P.op("pe", lambda e: e.transpose(out_ap, in_ap, ident[:]), reads=reads + [ident], writes=[out_b])

        def rstd_from_ss(ss_b, ss_ap, n, keys=None):
            P.op("act", lambda e: e.activation(out=ss_ap, in_=ss_ap, func=AF.Ln, scale=1.0 / n, bias=EPS), reads=[ss_b], writes=[ss_b])
            P.op("act", lambda e: e.activation(out=ss_ap, in_=ss_ap, func=AF.Exp, scale=-0.5), reads=[ss_b], writes=[ss_b])

        def load_round(dst, src_r, n, tag, perm_aq=False):
            with ExitStack() as stg:
                half = n // 2
                stage = [sb(stg, f"wst_{tag}{j}", [128, half]) for j in range(2)]
                q = 0
                for kc in range(8):
                    for hf in range(2):
                        st_ = stage[q % 2]
                        P.dma("sp", (lambda st_, kc, hf: lambda e: e.dma_start(out=st_[:], in_=src_r[:, kc, hf * half:(hf + 1) * half]))(st_, kc, hf), writes=[st_])
                        if perm_aq and hf == 0:
                            P.op("act", (lambda st_, kc: lambda e: e.copy(
                                out=dst[:, kc, 0:512].rearrange("p (c two d) -> p two c d", c=4, two=2, d=64),
                                in_=st_[:, 0:512].rearrange("p (two c d) -> p two c d", two=2, c=4, d=64)))(st_, kc), reads=[st_], writes=[(dst, kc)])
                            P.op("dve", (lambda st_, kc: lambda e: e.tensor_copy(out=dst[:, kc, 512:half], in_=st_[:, 512:half]))(st_, kc),
                                 reads=[st_], writes=[(dst, kc)])
                        elif q % 2 == 0:
                            P.op("act", (lambda st_, kc, hf: lambda e: e.copy(out=dst[:, kc, hf * half:(hf + 1) * half], in_=st_[:]))(st_, kc, hf),
                                 reads=[st_], writes=[(dst, kc)])
                        else:
                            P.op("dve", (lambda st_, kc, hf: lambda e: e.tensor_copy(out=dst[:, kc, hf * half:(hf + 1) * half], in_=st_[:]))(st_, kc, hf),
                                 reads=[st_], writes=[(dst, kc)])
                        q += 1
                P.barrier(skip_queue="pool")
                P.flush()

        with ExitStack() as ph:
            wada = [sb(ph, f"wada{j}", [128, 8, 512]) for j in range(4)]
            stage = [sb(ph, f"stg{j}", [128, 512]) for j in range(2)]
            n1w = sb(ph, "n1w", [128, D])
            n2w = sb(ph, "n2w", [128, D])
            bada = sb(ph, "bada", [1, 6 * D])
            ct = sb(ph, "ct", [128, 8])
            silc = sb(ph, "silc", [128, 8])
            silc_bc = sb(ph, "silc_bc", [128, 8, 128], F32R)
            wadar = [sb(ph, f"wadar{j}", [128, 8, 512], F32R) for j in range(2)]
            P.dma("sp", lambda e: e.dma_start(out=n1w[:], in_=bcast_row(h_n1w, D)), writes=[n1w])
            P.dma("sp", lambda e: e.dma_start(out=n2w[:], in_=bcast_row(h_n2w, D)), writes=[n2w])
            P.dma("sp", lambda e: e.dma_start(out=bada[:], in_=h_bada.ap()), writes=[bada])
            P.dma("sp", lambda e: e.dma_start(out=ct[:], in_=d_ct), writes=[ct])
            P.op("act", lambda e: e.activation(out=silc[:], in_=ct[:], func=AF.Silu), reads=[ct], writes=[silc])
            for kc in range(8):
                P.op("act", (lambda kc: lambda e: e.copy(out=silc_bc[:, kc, :], in_=silc[:, kc:kc + 1].to_broadcast([128, 128])))(kc),
                     reads=[silc], writes=[(silc_bc, kc)])
            for n in range(12):
                wb = wada[n % 4]
                wr = wadar[n % 2]
                P.dma("sp", (lambda wb, n: lambda e: e.dma_start(out=wb[:], in_=wada_r[:, :, n * 512:(n + 1) * 512]))(wb, n), writes=[wb])
                for kc in range(8):
                    if kc % 2 == 0:
                        P.op("act", (lambda wr, wb, kc: lambda e: e.copy(out=wr[:, kc, :], in_=wb[:, kc, :]))(wr, wb, kc), reads=[wb], writes=[(wr, kc)])
                    else:
                        P.op("dve", (lambda wr, wb, kc: lambda e: e.tensor_copy(out=wr[:, kc, :], in_=wb[:, kc, :]))(wr, wb, kc), reads=[wb], writes=[(wr, kc)])
                bk = psum()
                for kc in range(8):
                    mm(bk, bk[:, :], silc_bc[:, kc, :], wr[:, kc, :], kc == 0, False, [(silc_bc, kc), (wr, kc)])
                mm(bk, bk[:, :], ones[0:1, :], bada[0:1, n * 512:(n + 1) * 512], False, True, [ones, bada])
                sec, half = n // 2, n % 2
                sg = stage[n % 2]
                if sec in (1, 4):
                    nw = n1w if sec == 1 else n2w
                    P.op("dve", (lambda sg, bk, nw, half: lambda e: e.scalar_tensor_tensor(
                        out=sg[:], in0=bk[:, :], scalar=1.0, in1=nw[:, half * 512:(half + 1) * 512], op0=ALU.add, op1=ALU.mult))(sg, bk, nw, half),
                        reads=[bk, nw], writes=[sg])
                else:
                    P.op("act", (lambda sg, bk: lambda e: e.copy(out=sg[:], in_=bk[:, :]))(sg, bk), reads=[bk], writes=[sg])
                P.dma("pool", (lambda sg, sec, half: lambda e: e.dma_start(out=d_bc[sec, :, half * 512:(half + 1) * 512], in_=sg[:]))(sg, sec, half), reads=[sg])
            P.barrier()
            P.flush()
            maybe_stop(0)
        bcbuf = Buf("bc_dram", None)

        def load_bc(t, sec):
            P.dma("sp", lambda e: e.dma_start(out=t[:], in_=d_bc[sec]), writes=[t])

        if DEBUG:
            with ExitStack() as ph:
                tt = sb(ph, "dbgt", [128, D])
                for sec in range(6):
                    load_bc(tt, sec)
                    P.dma("sp", (lambda sec: lambda e: e.dma_start(out=dbg["dbg_bc"][sec], in_=tt[:]))(sec), reads=[tt])
                P.barrier(skip_queue="pool")
                P.flush()

        xstore = [sb(top, f"xs{i}", [128, D]) for i in range(NT)]

        with ExitStack() as ph:
            w_in = sb(ph, "w_in", [128, 8, 2320], F32R)
            load_round(w_in, win_r, 2320, "a", perm_aq=True)
            A1t = sb(ph, "A1t", [128, D])
            B1t = sb(ph, "B1t", [128, D])
            gup = sb(ph, "gup", [128, 256])
            gbias = sb(ph, "gbias", [1, 256])
            sink_bc = sb(ph, "sink_bc", [128, 8])
            gnw_bc = sb(ph, "gnw_bc", [128, 128])
            pvalid = sb(ph, "pvalid", [128, NPRE])
            maskr = sb(ph, "maskr", [128, 256])
            mask0 = sb(ph, "mask0", [128, 256])
            xt0 = sb(ph, "xt0", [128, D])
            h = sb(ph, "h", [128, D])
            hT = sb(ph, "hT", [128, 8, 128], F32R)
            hTb = sb(ph, "hTb", [128, 8, 128], F32R)
            decay2 = sb(ph, "decay2", [128, 2])
            ss1b = sb(ph, "ss1b", [128, 1])
            aqTp = sb(ph, "aqTp", [128, 8, 128], F32R)
            akT = [sb(ph, f"akT{j}", [128, 128], F32R) for j in range(2)]
            av = [sb(ph, f"av{j}", [128, 128], F32R) for j in range(2)]
            gqTp = sb(ph, "gqTp", [128, 4, 128])
            raw = sb(ph, "raw", [128, 4, 128])
            glrTL = [sb(ph, f"glrT{j}", [128, 128]) for j in range(2)]
            k_tmL = [sb(ph, f"k_tm{j}", [128, 256]) for j in range(2)]
            v_tmL = [sb(ph, f"v_tm{j}", [128, 512]) for j in range(2)]
            sg_ = sb(ph, "sgg", [128, 512])
            e1 = sb(ph, "e1", [128, 256])
            sp_ = sb(ph, "sp", [128, 256])
            eq = sb(ph, "eq", [128, 2, 128])
            ek = sb(ph, "ek", [128, 2, 128])
            kt = sb(ph, "kt", [128, 2, 128])
            er = sb(ph, "er", [128, 256])
            khat = sb(ph, "khat", [128, 256])
            attTm = sb(ph, "attTm", [128, 4, 128])
            S_sb = sb(ph, "S_sb", [128, 2, 128])
            decay = sb(ph, "decay", [128, 2])
            sc = sb(ph, "sc", [128, 4, 256])
            PT = sb(ph, "PT", [128, 8, 128], F32R)
            st8 = sb(ph, "st8", [128, 5, 8])
            ss1 = sb(ph, "ss1", [128, 1])
            ss4 = sb(ph, "ss4", [128, 4])
            gtmp = sb(ph, "gtmp", [128, 512])
            xt = [xt0, xstore[15]]
            h_alt = xstore[14]
            e1L = [e1, Buf("e1v", gtmp.t[:, 0:256])]
            spL = [sp_, Buf("spv", gtmp.t[:, 256:512])]
            erL = [er, Buf("erv", attTm.t[:, 0:2, :].rearrange("p a b -> p (a b)"))]
            khatL = [khat, Buf("khatv", attTm.t[:, 2:4, :].rearrange("p a b -> p (a b)"))]
            k_tm3 = k_tmL + [Buf("ktm3v", raw.t[:, 0:2, :].rearrange("p a b -> p (a b)"))]
            v_tm3 = v_tmL + [Buf("vtm3v", sg_.t[:, :])]

            load_bc(A1t, 1)
            load_bc(B1t, 0)
            P.op("dve", lambda e: e.memset(gup[:], 0.0), writes=[gup])
            P.dma("sp", lambda e: e.dma_start(out=gup[112:128, :], in_=d_gup), writes=[gup])
            P.dma("sp", lambda e: e.dma_start(out=gbias[:], in_=d_gbias), writes=[gbias])
            P.dma("sp", lambda e: e.dma_start(out=sink_bc[:], in_=bcast_row(h_sinks, 8)), writes=[sink_bc])
            P.dma("sp", lambda e: e.dma_start(out=gnw_bc[:], in_=bcast_row(h_gnw, 128)), writes=[gnw_bc])
            P.dma("sp", lambda e: e.dma_start(out=pvalid[:], in_=d_pvalid), writes=[pvalid])
            P.dma("sp", lambda e: e.dma_start(out=maskr[:], in_=d_maskr), writes=[maskr])
            P.dma("sp", lambda e: e.dma_start(out=mask0[:], in_=d_mask0), writes=[mask0])
            w_in_r = w_in.t[:]
            w_in_f = w_in.t[:].bitcast(F32)
            zsrc = xstore[13]
            P.op("dve", lambda e: e.memset(zsrc[:], 0.0), writes=[zsrc])
            P.op("act", lambda e: e.copy(out=aqTp[:], in_=zsrc[:].rearrange("p (a b) -> p a b", a=8)), reads=[zsrc], writes=[aqTp])
            P.op("dve", lambda e: e.memset(gqTp[:], 0.0), writes=[gqTp])
            P.op("dve", lambda e: e.memset(S_sb[:], 0.0), writes=[S_sb])
            P.op("act", lambda e: e.copy(out=akT[1][:], in_=zsrc[:, 0:128]), reads=[zsrc], writes=[akT[1]])
            P.op("act", lambda e: e.copy(out=av[1][:], in_=zsrc[:, 0:128]), reads=[zsrc], writes=[av[1]])

            cnt = [0]

            def tile_front(xsrc, own, pj):
                x_ = xt0
                cnt[0] += 1
                P.dma("sp", lambda e: e.dma_start(out=x_[:], in_=xsrc), writes=[x_])
                P.op("act", lambda e: e.activation(out=h[:], in_=x_[:], func=AF.Square, accum_out=ss1[:, 0:1]), reads=[x_], writes=[h, ss1])
                rstd_from_ss(ss1, ss1[:, 0:1], D)
                P.op("dve", lambda e: e.scalar_tensor_tensor(out=h[:], in0=x_[:], scalar=ss1[:, 0:1], in1=A1t[:], op0=ALU.mult, op1=ALU.mult),
                     reads=[x_, ss1, A1t], writes=[h])
                P.op("dve", lambda e: e.tensor_tensor(out=h[:], in0=h[:], in1=B1t[:], op=ALU.add), reads=[h, B1t], writes=[h])
                ba, bb = psum(), psum()
                for j in range(8):
                    bk = ba if j < 4 else bb
                    tr(bk, bk[:, (j % 4) * 128:(j % 4 + 1) * 128], h[:, j * 128:(j + 1) * 128], [h])
                P.op("act", lambda e: e.copy(out=hT[:, 0:4, :], in_=ba[:, :].rearrange("p (a b) -> p a b", a=4)), reads=[ba], writes=[(hT, 0)])
                P.op("dve", lambda e: e.tensor_copy(out=hT[:, 4:8, :], in_=bb[:, :].rearrange("p (a b) -> p a b", a=4)), reads=[bb], writes=[(hT, 1)])
                hTk = lambda kc: (hT, 0 if kc < 4 else 1)
                last_pre = (not own) and pj == NPRE - 1
                cur = (cnt[0] - 1) % 2 if own else 1
                return x_, last_pre

            def proj_tm(cols, bk, ncol, hT=hT):
                for kc in range(8):
                    mm(bk, bk[:, 0:ncol], hT[:, kc, :], w_in_r[:, kc, cols[0]:cols[1]], kc == 0, kc == 7,
                       [(hT, 0 if kc < 4 else 1), (w_in, kc)])

            def proj_fm(lhs_fn, bk, col0, m=128, f32=False, hT=hT):
                for kc in range(8):
                    lhsT = lhs_fn(kc)
                    rhs = hT[:, kc, :]
                    if f32:
                        rhs = rhs.bitcast(F32)
                    mm(bk, bk[0:m, col0:col0 + 128], lhsT, rhs, kc == 0, kc == 7, [(hT, 0 if kc < 4 else 1), (w_in, kc)])

            def gla_common(own, pj, bs):
                glrT, k_tm = glrTL[bs], k_tmL[bs]
                bz = psum()
                mm(bz, bz[:, 0:256], glrT[:, :], gup[:, :], True, False, [glrT, gup])
                mm(bz, bz[:, 0:256], ones[0:1, :], gbias[0:1, :], False, True, [ones, gbias])
                P.op("act", lambda e: e.activation(out=e1[:], in_=bz[:, 0:256], func=AF.Exp, scale=-1.0), reads=[bz], writes=[e1])
                P.op("act", lambda e: e.activation(out=sp_[:], in_=e1[:], func=AF.Ln, bias=1.0), reads=[e1], writes=[sp_])
                br = psum()
                mm(br, br[:, 0:256], triu[:, :], sp_[:, :], True, True, [triu, sp_])
                bt = psum()
                if own:
                    for hc in range(2):
                        mm(bt, bt[:, hc * 128:(hc + 1) * 128], sp_[:, hc * 128:(hc + 1) * 128], tri[:, :], True, True, [sp_, tri])
                for hc in range(2):
                    mm(bt, bt[:, 256 + 2 * hc:258 + 2 * hc], sp_[:, hc * 128:(hc + 1) * 128], ones[:, 0:2], True, True, [sp_, ones])
                P.op("act", lambda e: e.activation(out=er[:], in_=br[:, 0:256], func=AF.Exp, scale=-1.0 / 16), reads=[br], writes=[er])
                P.op("dve", lambda e: e.tensor_tensor(out=khat[:], in0=k_tm[:], in1=er[:], op=ALU.mult), reads=[k_tm, er], writes=[khat])
                P.op("act", lambda e: e.activation(out=decay[:], in_=bt[:, 256:260].rearrange("p (a b) -> p a b", a=2)[:, :, 0],
                                                   func=AF.Exp, scale=-1.0 / 16), reads=[bt], writes=[decay])
                if own:
                    btv = bt[:, 0:256].rearrange("p (a b) -> p a b", a=2)
                    P.op("act", lambda e: e.activation(out=eq[:], in_=btv, func=AF.Exp, scale=-1.0 / 16), reads=[bt], writes=[eq])
                    P.op("act", lambda e: e.activation(out=ek[:], in_=btv, func=AF.Exp, scale=1.0 / 16), reads=[bt], writes=[ek])
                    P.op("dve", lambda e: e.scalar_tensor_tensor(out=gqTp[0:64, 0:4:2, :], in0=eq[0:64, :, :], scalar=0.125, in1=raw[0:64, 0:2, :],
                                                                 op0=ALU.mult, op1=ALU.mult), reads=[eq, raw], writes=[(gqTp, 0)])
                    P.op("dve", lambda e: e.scalar_tensor_tensor(out=gqTp[64:128, 1:4:2, :], in0=eq[64:128, :, :], scalar=0.125, in1=raw[64:128, 0:2, :],
                                                                 op0=ALU.mult, op1=ALU.mult), reads=[eq, raw], writes=[(gqTp, 1)])
                    P.op("dve", lambda e: e.tensor_tensor(out=kt[:], in0=ek[:], in1=raw[:, 2:4, :], op=ALU.mult), reads=[ek, raw], writes=[kt])

            def state_update(bs):
                v_tm = v_tmL[bs]
                bd = psum()
                for hc in range(2):
                    mm(bd, bd[:, hc * 256:(hc + 1) * 256], khat[:, hc * 128:(hc + 1) * 128], v_tm[:, hc * 256:(hc + 1) * 256], True, True, [khat, v_tm])
                for hh in range(4):
                    p0, c = (hh % 2) * 64, hh // 2
                    P.op("dve", (lambda p0, c, hh: lambda e: e.scalar_tensor_tensor(
                        out=S_sb[p0:p0 + 64, c, :], in0=S_sb[p0:p0 + 64, c, :], scalar=decay[p0:p0 + 64, c:c + 1],
                        in1=bd[p0:p0 + 64, c * 256 + (hh % 2) * 128:c * 256 + (hh % 2) * 128 + 128], op0=ALU.mult, op1=ALU.add))(p0, c, hh),
                        reads=[(S_sb, hh), decay, bd], writes=[(S_sb, hh)])

            HH, HT, SS1, DEC = [h, h_alt], [hT, hTb], [ss1, ss1b], [decay, decay2]

            def pS1a(pj):
                b_ = pj % 2
                x_, h_, hT_, ss_ = xt[b_], HH[b_], HT[b_], SS1[b_]
                P.dma("sp", lambda e: e.dma_start(out=x_[:], in_=d_xpre[pj * 128:(pj + 1) * 128, :]), writes=[x_])
                P.op("act", lambda e: e.activation(out=h_[:], in_=x_[:], func=AF.Square, accum_out=ss_[:, 0:1]), reads=[x_], writes=[h_, ss_])
                rstd_from_ss(ss_, ss_[:, 0:1], D)
                P.op("dve", lambda e: e.scalar_tensor_tensor(out=h_[:], in0=x_[:], scalar=ss_[:, 0:1], in1=A1t[:], op0=ALU.mult, op1=ALU.mult),
                     reads=[x_, ss_, A1t], writes=[h_])
                P.op("dve", lambda e: e.tensor_tensor(out=h_[:], in0=h_[:], in1=B1t[:], op=ALU.add), reads=[h_, B1t], writes=[h_])

            def pS1b(pj):
                b_ = pj % 2
                h_, hT_ = HH[b_], HT[b_]
                ba, bb = banks[6], banks[7]
                for j in range(8):
                    bk = ba if j < 4 else bb
                    tr(bk, bk[:, (j % 4) * 128:(j % 4 + 1) * 128], h_[:, j * 128:(j + 1) * 128], [h_])
                P.op("act", lambda e: e.copy(out=hT_[:, 0:4, :], in_=ba[:, :].rearrange("p (a b) -> p a b", a=4)), reads=[ba], writes=[(hT_, 0)])
                P.op("dve", lambda e: e.tensor_copy(out=hT_[:, 4:8, :], in_=bb[:, :].rearrange("p (a b) -> p a b", a=4)), reads=[bb], writes=[(hT_, 1)])

            def pS2(pj):
                b_ = pj % 2
                hT_ = HT[b_]
                glrT, k_tm, v_tm = glrTL[b_], k_tm3[pj % 3], v_tm3[pj % 3]
                b2 = banks[3]
                proj_tm((1024, 1536), b2, 512, hT=hT_)
                b3 = banks[4]
                proj_tm((1536, 1792), b3, 256, hT=hT_)
                P.op("act", lambda e: e.copy(out=k_tm[:], in_=b2[:, 0:256]), reads=[b2], writes=[k_tm])
                P.op("act", lambda e: e.activation(out=v_tm[:, 0:256], in_=b2[:, 256:512], func=AF.Copy, scale=pvalid[:, pj:pj + 1]),
                     reads=[b2, pvalid], writes=[(v_tm, 0)])
                P.op("act", lambda e: e.activation(out=v_tm[:, 256:512], in_=b3[:, 0:256], func=AF.Copy, scale=pvalid[:, pj:pj + 1]),
                     reads=[b3, pvalid], writes=[(v_tm, 1)])
                bf = banks[5]
                proj_fm(lambda kc: w_in_r[:, kc, 2192:2320], bf, 0, hT=hT_)
                P.op("act", lambda e: e.copy(out=glrT[:], in_=bf[:, 0:128]), reads=[bf], writes=[glrT])
                if pj == NPRE - 1:
                    b1 = banks[3]
                    proj_tm((640, 768), b1, 128, hT=hT_)
                    P.op("act", lambda e: e.copy(out=av[1][:], in_=b1[:, 0:128]), reads=[b1], writes=[av[1]])
                    bg = banks[4]
                    proj_fm(lambda kc: w_in_r[:, kc, 512:640], bg, 0, hT=hT_)
                    P.op("act", lambda e: e.copy(out=akT[1][:], in_=bg[:, 0:128]), reads=[bg], writes=[akT[1]])

            def pS3(pj):
                b_ = pj % 2
                glrT = glrTL[b_]
                e1_, spb, er_, dec_ = e1L[b_], spL[b_], erL[b_], DEC[b_]
                bz = banks[1]
                mm(bz, bz[:, 0:256], glrT[:, :], gup[:, :], True, False, [glrT, gup])
                mm(bz, bz[:, 0:256], ones[0:1, :], gbias[0:1, :], False, True, [ones, gbias])
                P.op("act", lambda e: e.activation(out=e1_[:], in_=bz[:, 0:256], func=AF.Exp, scale=-1.0), reads=[bz], writes=[e1_])
                P.op("act", lambda e: e.activation(out=spb[:], in_=e1_[:], func=AF.Ln, bias=1.0), reads=[e1_], writes=[spb])
                br = banks[2]
                mm(br, br[:, 0:256], triu[:, :], spb[:], True, True, [triu, spb])
                for hc in range(2):
                    mm(br, br[:, 256 + 2 * hc:258 + 2 * hc], spb[:, hc * 128:(hc + 1) * 128], ones[:, 0:2], True, True, [spb, ones])
                P.op("act", lambda e: e.activation(out=er_[:], in_=br[:, 0:256], func=AF.Exp, scale=-1.0 / 16), reads=[br], writes=[er_])
                P.op("act", lambda e: e.activation(out=dec_[:], in_=br[:, 256:260].rearrange("p (a b) -> p a b", a=2)[:, :, 0],
                                                   func=AF.Exp, scale=-1.0 / 16), reads=[br], writes=[dec_])

            def pS4(pj):
                b_ = pj % 2
                k_tm, v_tm = k_tm3[pj % 3], v_tm3[pj % 3]
                er_, kh_, dec_ = erL[b_], khatL[b_], DEC[b_]
                P.op("dve", lambda e: e.tensor_tensor(out=kh_[:], in0=k_tm[:], in1=er_[:], op=ALU.mult), reads=[k_tm, er_], writes=[kh_])
                bd = banks[0]
                for hc in range(2):
                    mm(bd, bd[:, hc * 256:(hc + 1) * 256], kh_[:, hc * 128:(hc + 1) * 128], v_tm[:, hc * 256:(hc + 1) * 256], True, True, [kh_, v_tm])
                for hh in range(4):
                    p0, c = (hh % 2) * 64, hh // 2
                    P.op("dve", (lambda p0, c, hh: lambda e: e.scalar_tensor_tensor(
                        out=S_sb[p0:p0 + 64, c, :], in0=S_sb[p0:p0 + 64, c, :], scalar=dec_[p0:p0 + 64, c:c + 1],
                        in1=bd[p0:p0 + 64, c * 256 + (hh % 2) * 128:c * 256 + (hh % 2) * 128 + 128], op0=ALU.mult, op1=ALU.add))(p0, c, hh),
                        reads=[(S_sb, hh), dec_, bd], writes=[(S_sb, hh)])

            pjs = list(range(NPRE - NPRE_RUN, NPRE)) if STAGE >= 1 else []
            stages = [(pS1a, 0), (pS2, 1), (pS1b, 0), (pS3, 2), (pS4, 3)]
            for t_ in range(len(pjs) + 3):
                for fn_, lag in stages:
                    jj = t_ - lag
                    if 0 <= jj < len(pjs):
                        fn_(pjs[jj])
            P.barrier(skip_queue="pool")
            if STAGE == 1:
                P.barrier(skip_queue="pool")
                maybe_stop(1)
            for i in range(NT_RUN if STAGE >= 2 else 0):
                cur, prv = i % 2, (i + 1) % 2
                glrT, k_tm, v_tm = glrTL[i % 2], k_tmL[i % 2], v_tmL[i % 2]
                x_, _ = tile_front((d_xpre if XSRC_PRE else d_xown)[i * 128:(i + 1) * 128, :], True, None)
                mix = xstore[i]
                if SUB < -3:
                    continue
                b1 = psum(); proj_tm((640, 768), b1, 128)
                b2 = psum(); proj_tm((1024, 1536), b2, 512)
                b3 = psum(); proj_tm((1536, 2048), b3, 512)
                b4 = psum(); proj_tm((2048, 2304), b4, 256)
                P.op("act", (lambda cur, b1: lambda e: e.copy(out=av[cur][:], in_=b1[:, 0:128]))(cur, b1), reads=[b1], writes=[av[cur]])
                P.op("act", (lambda b2, k_tm: lambda e: e.copy(out=k_tm[:], in_=b2[:, 0:256]))(b2, k_tm), reads=[b2], writes=[k_tm])
                P.op("dve", (lambda b2, v_tm: lambda e: e.tensor_copy(out=v_tm[:, 0:256], in_=b2[:, 256:512]))(b2, v_tm), reads=[b2], writes=[(v_tm, 0)])
                P.op("dve", (lambda b3, v_tm: lambda e: e.tensor_copy(out=v_tm[:, 256:512], in_=b3[:, 0:256]))(b3, v_tm), reads=[b3], writes=[(v_tm, 1)])
                P.op("act", (lambda b3: lambda e: e.activation(out=sg_[:, 0:256], in_=b3[:, 256:512], func=AF.Silu))(b3), reads=[b3], writes=[(sg_, 0)])
                P.op("act", (lambda b4: lambda e: e.activation(out=sg_[:, 256:512], in_=b4[:, 0:256], func=AF.Silu))(b4), reads=[b4], writes=[(sg_, 1)])
                if SUB < -2:
                    continue
                f1 = psum()
                for c in range(4):
                    proj_fm((lambda c: lambda kc: w_in_r[:, kc, c * 128:(c + 1) * 128])(c), f1, c * 128)
                f2 = psum()
                proj_fm(lambda kc: w_in_r[:, kc, 512:640], f2, 0)
                proj_fm(lambda kc: w_in_r[:, kc, 768:896], f2, 128)
                proj_fm(lambda kc: w_in_r[:, kc, 896:1024], f2, 256)
                f3 = psum()
                proj_fm(lambda kc: w_in_r[:, kc, 1024:1152], f3, 0)
                proj_fm(lambda kc: w_in_r[:, kc, 1152:1280], f3, 128)
                f4 = psum()
                proj_fm(lambda kc: w_in_r[:, kc, 2192:2320], f4, 0)
                if SUB < -1:
                    continue
                P.op("act", (lambda f1: lambda e: e.activation(out=aqTp[0:64, 0:4, :], in_=f1[0:64, :].rearrange("p (a b) -> p a b", a=4), func=AF.Copy, scale=0.125))(f1),
                     reads=[f1], writes=[(aqTp, 0)])
                P.op("act", (lambda f1: lambda e: e.activation(out=aqTp[64:128, 4:8, :], in_=f1[64:128, :].rearrange("p (a b) -> p a b", a=4), func=AF.Copy, scale=0.125))(f1),
                     reads=[f1], writes=[(aqTp, 1)])
                P.op("dve", (lambda cur, f2: lambda e: e.tensor_copy(out=akT[cur][:], in_=f2[:, 0:128]))(cur, f2), reads=[f2], writes=[akT[cur]])
                P.op("dve", (lambda f2: lambda e: e.tensor_copy(out=raw[:, 0:2, :], in_=f2[:, 128:384].rearrange("p (a b) -> p a b", a=2)))(f2), reads=[f2], writes=[(raw, 0)])
                P.op("act", (lambda f3: lambda e: e.copy(out=raw[:, 2:4, :], in_=f3[:, 0:256].rearrange("p (a b) -> p a b", a=2)))(f3), reads=[f3], writes=[(raw, 1)])
                P.op("dve", (lambda f4, glrT: lambda e: e.tensor_copy(out=glrT[:], in_=f4[:, 0:128]))(f4, glrT), reads=[f4], writes=[glrT])

                msk = mask0 if i == 0 else maskr
                batt = banks[6]
                mxv, nmx, rsum, es, rden = (st8[:, j, :] for j in range(5))
                def G1():
                    gla_common(True, None, i % 2)

                def G2():
                    bat = psum()
                    for hh in range(4):
                        mm(bat, bat[:, hh * 128:(hh + 1) * 128], kt[:, hh // 2, :], gqTp[:, hh, :], True, True, [kt, gqTp])
                    P.op("dve", (lambda bat: lambda e: e.tensor_tensor(out=attTm[:], in0=bat[:, :].rearrange("p (a b) -> p a b", a=4),
                                                                      in1=tri[:, :].unsqueeze(1).to_broadcast([128, 4, 128]), op=ALU.mult))(bat),
                         reads=[bat, tri], writes=[attTm])

                def G3():
                    bo = banks[7]
                    for hh in range(4):
                        mm(bo, bo[:, hh * 128:(hh + 1) * 128], gqTp[:, hh, :], S_sb[:, hh // 2, :], True, False, [gqTp, S_sb])
                        mm(bo, bo[:, hh * 128:(hh + 1) * 128], attTm[:, hh, :], v_tm[:, hh * 128:(hh + 1) * 128], False, True, [attTm, v_tm])
                    state_update(i % 2)

                def G4():
                    bo = banks[7]
                    for hh in range(4):
                        P.op("act", (lambda hh, bo: lambda e: e.activation(out=gtmp[:, hh * 128:(hh + 1) * 128], in_=bo[:, hh * 128:(hh + 1) * 128],
                                                                          func=AF.Square, accum_out=ss4[:, hh:hh + 1]))(hh, bo), reads=[bo], writes=[(gtmp, hh), (ss4, hh)])
                    rstd_from_ss(ss4, ss4[:, :], 128)
                    for hh in range(4):
                        P.op("dve", (lambda hh, bo: lambda e: e.scalar_tensor_tensor(out=gtmp[:, hh * 128:(hh + 1) * 128], in0=bo[:, hh * 128:(hh + 1) * 128],
                                                                                    scalar=ss4[:, hh:hh + 1], in1=gnw_bc[:], op0=ALU.mult, op1=ALU.mult))(hh, bo),
                             reads=[bo, ss4, gnw_bc], writes=[(gtmp, hh)])
                    P.op("pool", (lambda mix: lambda e: e.tensor_tensor(out=mix[:, 512:1024], in0=gtmp[:], in1=sg_[:], op=ALU.mult))(mix),
                         reads=[gtmp, sg_], writes=[(mix, 1)])


                att_state = {}

                def A1(hg):
                    sbk = [psum(), psum()]
                    for j in range(4):
                        hh = hg * 4 + j
                        bk = sbk[j // 2]
                        c0 = (j % 2) * 256
                        mm(bk, bk[:, c0:c0 + 128], aqTp[:, hh, :], akT[prv][:, :], True, True, [aqTp, akT[prv]])
                        mm(bk, bk[:, c0 + 128:c0 + 256], aqTp[:, hh, :], akT[cur][:, :], True, True, [aqTp, akT[cur]])
                    for jj in range(2):
                        P.op("dve", (lambda jj, bk, msk: lambda e: e.tensor_tensor(out=sc[:, 2 * jj:2 * jj + 2, :], in0=bk[:, :].rearrange("p (a b) -> p a b", a=2),
                                                                                   in1=msk[:, :].unsqueeze(1).to_broadcast([128, 2, 256]), op=ALU.add))(jj, sbk[jj], msk),
                             reads=[sbk[jj], msk], writes=[(sc, jj)])
                    att_state['sbk'] = sbk

                def A2(hg):
                    sbk = att_state['sbk']
                    hs = slice(hg * 4, hg * 4 + 4)
                    P.op("dve", (lambda hs: lambda e: e.tensor_reduce(out=mxv[:, hs], in_=sc[:], axis=AX.X, op=ALU.max))(hs), reads=[sc], writes=[(st8, "mx")])
                    P.op("dve", (lambda hs: lambda e: e.tensor_tensor(out=mxv[:, hs], in0=mxv[:, hs], in1=sink_bc[:, hs], op=ALU.max))(hs),
                         reads=[(st8, "mx"), sink_bc], writes=[(st8, "mx")])
                    P.op("dve", (lambda hs: lambda e: e.tensor_scalar(out=nmx[:, hs], in0=mxv[:, hs], scalar1=-1.0, scalar2=None, op0=ALU.mult))(hs),
                         reads=[(st8, "mx")], writes=[(st8, "nmx")])
                    for j in range(4):
                        hh = hg * 4 + j
                        P.op("act", (lambda j, hh: lambda e: e.activation(out=sc[:, j, :], in_=sc[:, j, :], func=AF.Exp, bias=nmx[:, hh:hh + 1],
                                                                          accum_out=rsum[:, hh:hh + 1]))(j, hh),
                             reads=[(sc, j // 2), (st8, "nmx")], writes=[(sc, j // 2), (st8, ("rs", hh))])

                def A3(hg):
                    tb = [psum(), psum()]
                    for j in range(4):
                        for blk in range(2):
                            q = j * 2 + blk
                            bk = tb[q // 4]
                            tr(bk, bk[:, (q % 4) * 128:(q % 4 + 1) * 128], sc[:, j, blk * 128:(blk + 1) * 128], [(sc, j // 2)])
                    P.op("act", (lambda bk: lambda e: e.copy(out=PT[:, 0:4, :], in_=bk[:, :].rearrange("p (a b) -> p a b", a=4)))(tb[0]), reads=[tb[0]], writes=[(PT, 0)])
                    P.op("dve", (lambda bk: lambda e: e.tensor_copy(out=PT[:, 4:8, :], in_=bk[:, :].rearrange("p (a b) -> p a b", a=4)))(tb[1]), reads=[tb[1]], writes=[(PT, 1)])
                    att_state['tb'] = tb

                def A4(hg):
                    tb = att_state['tb']
                    for j in range(4):
                        hh = hg * 4 + j
                        for blk in range(2):
                            q = j * 2 + blk
                            avb = av[prv] if blk == 0 else av[cur]
                            mm(batt, batt[:, hh * 64:(hh + 1) * 64], PT[:, q, :], avb[:, hg * 64:(hg + 1) * 64], blk == 0, blk == 1, [(PT, q // 4), avb])

                def ATTF():
                    P.op("dve", lambda e: e.tensor_tensor(out=es[:, :], in0=sink_bc[:], in1=mxv[:, :], op=ALU.subtract), reads=[sink_bc, (st8, "mx")], writes=[(st8, "es")])
                    P.op("act", lambda e: e.activation(out=es[:, :], in_=es[:, :], func=AF.Exp), reads=[(st8, "es")], writes=[(st8, "es")])
                    P.op("dve", lambda e: e.tensor_tensor(out=rden[:, :], in0=rsum[:, :], in1=es[:, :], op=ALU.add),
                         reads=[(st8, "es")] + [(st8, ("rs", hh)) for hh in range(8)], writes=[(st8, "rden")])
                    P.op("dve", lambda e: e.reciprocal(out=rden[:, :], in_=rden[:, :]), reads=[(st8, "rden")], writes=[(st8, "rden")])
                    P.op("dve", (lambda mix, batt: lambda e: e.tensor_tensor(out=mix[:, 0:512].rearrange("p (a b) -> p a b", a=8),
                                                                            in0=batt[:, :].rearrange("p (a b) -> p a b", a=8),
                                                                            in1=rden[:, :].unsqueeze(2).to_broadcast([128, 8, 64]), op=ALU.mult))(mix, batt),
                         reads=[batt, (st8, "rden")], writes=[(mix, 0)])

                if SUB < 1:
                    continue
                A1(0); G1(); A2(0); G2(); A3(0); G3(); A4(0); A1(1); G4(); A2(1); A3(1); A4(1); ATTF()
                if DEBUG:
                    P.dma("sp", (lambda mix, i: lambda e: e.dma_start(out=dbg["dbg_mix"][i * 128:(i + 1) * 128, :], in_=mix[:]))(mix, i), reads=[mix])
            P.barrier(skip_queue="pool")
            P.flush()
            maybe_stop(2)

        with ExitStack() as ph:
            w_out = sb(ph, "w_out", [128, 8, D], F32R)
            if STAGE >= 3:
                load_round(w_out, wout_r, D, "b")
            G1t = sb(ph, "G1t", [128, D])
            xt = [sb(ph, f"xta{j}", [128, D]) for j in range(2)]
            mT = [sb(ph, f"mT{j}", [128, 8, 128], F32R) for j in range(2)]
            tmp = sb(ph, "tmpa", [128, D])
            load_bc(G1t, 2)
            w_out_r = w_out.t[:]
            for i in range(NT if STAGE >= 3 else 0):
                mix = xstore[i]
                x_ = xt[i % 2]
                m_ = mT[i % 2]
                P.dma("sp", (lambda x_, i: lambda e: e.dma_start(out=x_[:], in_=d_xown[i * 128:(i + 1) * 128, :]))(x_, i), writes=[x_])
                ba, bb = psum(), psum()
                for j in range(8):
                    bk = ba if j < 4 else bb
                    tr(bk, bk[:, (j % 4) * 128:(j % 4 + 1) * 128], mix[:, j * 128:(j + 1) * 128], [mix])
                P.op("act", (lambda m_, ba: lambda e: e.copy(out=m_[:, 0:4, :], in_=ba[:, :].rearrange("p (a b) -> p a b", a=4)))(m_, ba), reads=[ba], writes=[(m_, 0)])
                P.op("dve", (lambda m_, bb: lambda e: e.tensor_copy(out=m_[:, 4:8, :], in_=bb[:, :].rearrange("p (a b) -> p a b", a=4)))(m_, bb), reads=[bb], writes=[(m_, 1)])
                for half in range(2):
                    bk = psum()
                    for kc in range(8):
                        mm(bk, bk[:, :], m_[:, kc, :], w_out_r[:, kc, half * 512:(half + 1) * 512], kc == 0, kc == 7, [(m_, 0 if kc < 4 else 1), (w_out, kc)])
                    hsl = slice(half * 512, (half + 1) * 512)
                    P.op("dve", (lambda bk, hsl: lambda e: e.tensor_tensor(out=tmp[:, hsl], in0=bk[:, :], in1=G1t[:, hsl], op=ALU.mult))(bk, hsl),
                         reads=[bk, G1t], writes=[(tmp, half)])
                    P.op("pool", (lambda mix, x_, hsl: lambda e: e.tensor_tensor(out=mix[:, hsl], in0=tmp[:, hsl], in1=x_[:, hsl], op=ALU.add))(mix, x_, hsl),
                         reads=[(tmp, half), x_], writes=[mix])
                if DEBUG:
                    P.dma("sp", (lambda mix, i: lambda e: e.dma_start(out=dbg["dbg_x1"][i * 128:(i + 1) * 128, :], in_=mix[:]))(mix, i), reads=[mix])
            P.barrier(skip_queue="pool")
            P.flush()
            maybe_stop(3)

        idxst = [sb(top, f"idx{i}", [128, 128], I32) for i in range(NT)]
        gatest = [sb(top, f"gate{i}", [128, 128]) for i in range(NT)]

        def norm2_h2(ph_bufs, x1, h2, A2t, B2t, ss, add_eng="pool"):
            P.op("act", lambda e: e.activation(out=h2[:], in_=x1[:], func=AF.Square, accum_out=ss[:, 0:1]), reads=[x1], writes=[h2, ss])
            rstd_from_ss(ss, ss[:, 0:1], D)
            P.op("dve", lambda e: e.scalar_tensor_tensor(out=h2[:], in0=x1[:], scalar=ss[:, 0:1], in1=A2t[:], op0=ALU.mult, op1=ALU.mult),
                 reads=[x1, ss, A2t], writes=[h2])
            P.op(add_eng, lambda e: e.tensor_tensor(out=h2[:], in0=h2[:], in1=B2t[:], op=ALU.add), reads=[h2, B2t], writes=[h2])

        with ExitStack() as ph:
            wqh = sb(ph, "wqh", [128, 8, 1024], F32R)
            skT = sb(ph, "skT", [128, 16, 128])
            A2t = sb(ph, "A2t", [128, D])
            B2t = sb(ph, "B2t", [128, D])
            h2 = sb(ph, "h2b", [128, D])
            h2T = sb(ph, "h2T", [128, 8, 128], F32R)
            qT = sb(ph, "qT", [128, 8, 128])
            scsL = [sb(ph, f"scs{j}", [128, 8, 128]) for j in range(2)]
            sc2 = sb(ph, "sc2", [128, 8, 128])
            topv = sb(ph, "topv", [128, 8, 16])
            topi = sb(ph, "topi", [128, 8, 16], U32)
            topif = sb(ph, "topif", [128, 8, 16])
            cand = sb(ph, "cand", [128, 4, 256])
            cand2 = sb(ph, "cand2", [128, 4, 256])
            bestv = sb(ph, "bestv", [128, 4, 16])
            pos = sb(ph, "pos", [128, 4, 16], U32)
            pab = sb(ph, "pab", [128, 2, 64], U32)
            pabf = sb(ph, "pabf", [128, 2, 64])
            ohs = [sb(ph, f"oh{j}", [128, 64, 16]) for j in range(2)]
            isel = sb(ph, "isel", [128, 2, 64])
            ef = sb(ph, "ef", [128, 64])
            gs = sb(ph, "gs", [128, 3, 4])
            eg = sb(ph, "eg", [128, 4, 16])
            ssb = sb(ph, "ssb", [128, 1])
            P.dma("sp", lambda e: e.dma_start(out=skT[:], in_=d_skT), writes=[skT])
            load_bc(A2t, 4)
            load_bc(B2t, 3)
            for ps_ in range(2 if STAGE >= 4 else 0):
                load_round(wqh, wq_r[:, :, ps_ * 1024:(ps_ + 1) * 1024], 1024, f"q{ps_}")
                wqh_r = wqh.t[:]
                def b1_front(i, scs):
                    x1 = xstore[i]
                    norm2_h2(None, x1, h2, A2t, B2t, ssb)
                    ba, bb = psum(), psum()
                    for j in range(8):
                        bk = ba if j < 4 else bb
                        tr(bk, bk[:, (j % 4) * 128:(j % 4 + 1) * 128], h2[:, j * 128:(j + 1) * 128], [h2])
                    P.op("act", (lambda ba: lambda e: e.copy(out=h2T[:, 0:4, :], in_=ba[:, :].rearrange("p (a b) -> p a b", a=4)))(ba), reads=[ba], writes=[(h2T, 0)])
                    P.op("act", (lambda bb: lambda e: e.copy(out=h2T[:, 4:8, :], in_=bb[:, :].rearrange("p (a b) -> p a b", a=4)))(bb), reads=[bb], writes=[(h2T, 1)])
                    qb = [psum(), psum()]
                    for g in range(8):
                        bk = qb[g // 4]
                        for kc in range(8):
                            mm(bk, bk[:, (g % 4) * 128:(g % 4 + 1) * 128], wqh_r[:, kc, g * 128:(g + 1) * 128], h2T[:, kc, :], kc == 0, kc == 7,
                               [(wqh, kc), (h2T, 0 if kc < 4 else 1)])
                    P.op("act", (lambda bk: lambda e: e.copy(out=qT[:, 0:4, :], in_=bk[:, :].rearrange("p (a b) -> p a b", a=4)))(qb[0]), reads=[qb[0]], writes=[(qT, 0)])
                    P.op("act", (lambda bk: lambda e: e.copy(out=qT[:, 4:8, :], in_=bk[:, :].rearrange("p (a b) -> p a b", a=4)))(qb[1]), reads=[qb[1]], writes=[(qT, 1)])
                    sbk = [psum(), psum()]
                    for g in range(8):
                        bk = sbk[g // 4]
                        mm(bk, bk[:, (g % 4) * 128:(g % 4 + 1) * 128], qT[:, g, :], skT[:, ps_ * 8 + g, :], True, True, [(qT, g // 4), skT])
                    P.op("act", (lambda bk: lambda e: e.copy(out=scs[:, 0:4, :], in_=bk[:, :].rearrange("p (a b) -> p a b", a=4)))(sbk[0]), reads=[sbk[0]], writes=[(scs, 0)])
                    P.op("act", (lambda bk: lambda e: e.copy(out=scs[:, 4:8, :], in_=bk[:, :].rearrange("p (a b) -> p a b", a=4)))(sbk[1]), reads=[sbk[1]], writes=[(scs, 1)])
                def b1_topk(i, scs):
                    for g in range(8):
                        P.op("dve", (lambda g: lambda e: e.max(out=topv[:, g, 0:8], in_=scs[:, g, :]))(g), reads=[(scs, g // 4)], writes=[(topv, (g, 0))])
                    for g in range(8):
                        P.op("dve", (lambda g: lambda e: e.match_replace(out=sc2[:, g, :], in_to_replace=topv[:, g, 0:8], in_values=scs[:, g, :], imm_value=-1e30))(g),
                             reads=[(scs, g // 4), (topv, (g, 0))], writes=[(sc2, g)])
                    for g in range(8):
                        P.op("dve", (lambda g: lambda e: e.max(out=topv[:, g, 8:16], in_=sc2[:, g, :]))(g), reads=[(sc2, g)], writes=[(topv, (g, 1))])
                    for g in range(8):
                        P.op("dve", (lambda g: lambda e: e.max_index(out=topi[:, g, 0:8], in_max=topv[:, g, 0:8], in_values=scs[:, g, :]))(g),
                             reads=[(scs, g // 4), (topv, (g, 0))], writes=[(topi, (g, 0))])
                    for g in range(8):
                        P.op("dve", (lambda g: lambda e: e.max_index(out=topi[:, g, 8:16], in_max=topv[:, g, 8:16], in_values=scs[:, g, :]))(g),
                             reads=[(scs, g // 4), (topv, (g, 1))], writes=[(topi, (g, 1))])
                    P.op("dve", lambda e: e.tensor_copy(out=topif[:], in_=topi[:]), reads=[topi], writes=[topif])
                    tv4 = topv[:, :, :].rearrange("p (h two) a -> p h two a", two=2)
                    ti4 = topif[:, :, :].rearrange("p (h two) a -> p h two a", two=2)
                    P.op("dve", lambda e: e.tensor_tensor(out=cand[:, :, :].rearrange("p h (a b) -> p h a b", a=16),
                                                          in0=tv4[:, :, 0, :].unsqueeze(3).to_broadcast([128, 4, 16, 16]),
                                                          in1=tv4[:, :, 1, :].unsqueeze(2).to_broadcast([128, 4, 16, 16]), op=ALU.add),
                         reads=[topv], writes=[cand])
                    for hh in range(4):
                        P.op("dve", (lambda hh: lambda e: e.max(out=bestv[:, hh, 0:8], in_=cand[:, hh, :]))(hh), reads=[cand], writes=[(bestv, (hh, 0))])
                    for hh in range(4):
                        P.op("dve", (lambda hh: lambda e: e.match_replace(out=cand2[:, hh, :], in_to_replace=bestv[:, hh, 0:8], in_values=cand[:, hh, :], imm_value=-1e30))(hh),
                             reads=[cand, (bestv, (hh, 0))], writes=[(cand2, hh)])
                    for hh in range(4):
                        P.op("dve", (lambda hh: lambda e: e.max(out=bestv[:, hh, 8:16], in_=cand2[:, hh, :]))(hh), reads=[(cand2, hh)], writes=[(bestv, (hh, 1))])
                    for hh in range(4):
                        P.op("dve", (lambda hh: lambda e: e.max_index(out=pos[:, hh, 0:8], in_max=bestv[:, hh, 0:8], in_values=cand[:, hh, :]))(hh),
                             reads=[cand, (bestv, (hh, 0))], writes=[(pos, (hh, 0))])
                    for hh in range(4):
                        P.op("dve", (lambda hh: lambda e: e.max_index(out=pos[:, hh, 8:16], in_max=bestv[:, hh, 8:16], in_values=cand[:, hh, :]))(hh),
                             reads=[cand, (bestv, (hh, 1))], writes=[(pos, (hh, 1))])
                    posf = pos[:, :, :].rearrange("p h r -> p (h r)")
                    P.op("dve", lambda e: e.tensor_single_scalar(out=pab[:, 0, :], in_=posf, scalar=4, op=ALU.logical_shift_right), reads=[pos], writes=[(pab, 0)])
                    P.op("dve", lambda e: e.tensor_single_scalar(out=pab[:, 1, :], in_=posf, scalar=15, op=ALU.bitwise_and), reads=[pos], writes=[(pab, 1)])
                    P.op("dve", lambda e: e.tensor_copy(out=pabf[:, 0, :], in_=pab[:, 0, :]), reads=[(pab, 0)], writes=[(pabf, 0)])
                    P.op("dve", lambda e: e.tensor_copy(out=pabf[:, 1, :], in_=pab[:, 1, :]), reads=[(pab, 1)], writes=[(pabf, 1)])
                    for ab in range(2):
                        eng = "dve"
                        P.op(eng, (lambda ab: lambda e: e.tensor_tensor(out=ohs[ab][:], in0=pabf[:, ab, :].unsqueeze(2).to_broadcast([128, 64, 16]),
                                                                        in1=iota16[:, :].unsqueeze(1).to_broadcast([128, 64, 16]), op=ALU.is_equal))(ab),
                             reads=[(pabf, ab), iota16], writes=[ohs[ab]])
                    for ab in range(2):
                        eng = "dve" if ab == 0 else "pool"
                        P.op(eng, (lambda ab: lambda e: e.tensor_tensor(out=ohs[ab][:, :, :].rearrange("p (h r) a -> p h r a", h=4),
                                                                        in0=ohs[ab][:, :, :].rearrange("p (h r) a -> p h r a", h=4),
                                                                        in1=ti4[:, :, ab, :].unsqueeze(2).to_broadcast([128, 4, 16, 16]), op=ALU.mult))(ab),
                             reads=[ohs[ab], topif], writes=[ohs[ab]])
                    for ab in range(2):
                        P.op("dve", (lambda ab: lambda e: e.tensor_reduce(out=isel[:, ab, :], in_=ohs[ab][:], axis=AX.X, op=ALU.add))(ab), reads=[ohs[ab]], writes=[(isel, ab)])
                    P.op("dve", lambda e: e.scalar_tensor_tensor(out=ef[:], in0=isel[:, 0, :], scalar=128.0, in1=isel[:, 1, :], op0=ALU.mult, op1=ALU.add),
                         reads=[isel], writes=[ef])
                    csl = slice(ps_ * 64, ps_ * 64 + 64)
                    P.op("dve", (lambda i, csl: lambda e: e.tensor_copy(out=idxst[i][:, csl], in_=ef[:]))(i, csl), reads=[ef], writes=[(idxst[i], ps_)])
                    P.op("dve", lambda e: e.tensor_tensor(out=eg[:], in0=bestv[:], in1=bestv[:, :, 0:1].to_broadcast([128, 4, 16]), op=ALU.subtract),
                         reads=[bestv], writes=[eg])
                    P.op("act", lambda e: e.activation(out=eg[:], in_=eg[:], func=AF.Exp), reads=[eg], writes=[eg])
                    P.op("dve", lambda e: e.tensor_reduce(out=gs[:, 0, :], in_=eg[:], axis=AX.X, op=ALU.add), reads=[eg], writes=[gs])
                    P.op("dve", lambda e: e.reciprocal(out=gs[:, 1, :], in_=gs[:, 0, :]), reads=[gs], writes=[gs])
                    P.op("dve", (lambda i, csl: lambda e: e.tensor_tensor(out=gatest[i][:, csl].rearrange("p (h r) -> p h r", h=4), in0=eg[:],
                                                                          in1=gs[:, 1, :].unsqueeze(2).to_broadcast([128, 4, 16]), op=ALU.mult))(i, csl),
                         reads=[eg, gs], writes=[(gatest[i], ps_)])
                    if DEBUG and ps_ == 1:
                        P.dma("sp", (lambda i: lambda e: e.dma_start(out=dbg["dbg_idx"][i * 128:(i + 1) * 128, :], in_=idxst[i][:]))(i), reads=[idxst[i]])
                        P.dma("sp", (lambda i: lambda e: e.dma_start(out=dbg["dbg_gate"][i * 128:(i + 1) * 128, :], in_=gatest[i][:]))(i), reads=[gatest[i]])
                b1_front(0, scsL[0])
                for i in range(NT):
                    if i + 1 < NT:
                        b1_front(i + 1, scsL[(i + 1) % 2])
                    b1_topk(i, scsL[i % 2])
                    issue_cast()
                    issue_cast()
            P.barrier(skip_queue="pool")
            P.flush()
            maybe_stop(4)

        with ExitStack() as ph:
            NB = 20
            while castn[0] < NCAST:
                issue_cast()
            A2t = sb(ph, "A2t2", [128, D])
            B2t = sb(ph, "B2t2", [128, D])
            G2t = sb(ph, "G2t", [128, D])
            FWt = sb(ph, "FWt", [128, D])
            h2 = [sb(ph, f"h2c{j}", [128, D]) for j in range(2)]
            ring = [sb(ph, f"rb{j}", [128, 2 * D], BF16) for j in range(NB)]
            hraw = [sb(ph, f"hraw{j}", [128, 128]) for j in range(2)]
            hgel = [sb(ph, f"hgel{j}", [128, 128]) for j in range(2)]
            Dr = [sb(ph, f"Dr{j}", [128, 128], BF16) for j in range(8)]
            ot = [sb(ph, f"ot{j}", [128, D]) for j in range(2)]
            ssb = sb(ph, "ssb2", [128, 2])
            load_bc(A2t, 4)
            load_bc(B2t, 3)
            load_bc(G2t, 5)
            P.dma("sp", lambda e: e.dma_start(out=FWt[:], in_=bcast_row(h_fnw, D)), writes=[FWt])
            cu = cv = cd = 0
            YB = [(banks[6], banks[7]), (banks[4], banks[5])]
            NTB = NT if STAGE >= 5 else 0

            def b2_final(i):
                x1 = xstore[i]
                h2_ = h2[i % 2]
                o_ = ot[i % 2]
                Y0, Y1 = YB[i % 2]
                P.op("dve", (lambda o_: lambda e: e.tensor_tensor(out=o_[:, 0:512], in0=Y0[:, :], in1=G2t[:, 0:512], op=ALU.mult))(o_), reads=[Y0, G2t], writes=[(o_, 0)])
                P.op("dve", (lambda o_: lambda e: e.tensor_tensor(out=o_[:, 512:1024], in0=Y1[:, :], in1=G2t[:, 512:1024], op=ALU.mult))(o_), reads=[Y1, G2t], writes=[(o_, 1)])
                P.op("dve", (lambda o_, x1: lambda e: e.tensor_tensor(out=o_[:], in0=o_[:], in1=x1[:], op=ALU.add))(o_, x1), reads=[o_, x1], writes=[o_])
                P.op("act", (lambda o_, h2_: lambda e: e.activation(out=h2_[:], in_=o_[:], func=AF.Square, accum_out=ssb[:, 1:2]))(o_, h2_), reads=[o_], writes=[h2_, (ssb, "f")])
                P.op("act", lambda e: e.activation(out=ssb[:, 1:2], in_=ssb[:, 1:2], func=AF.Ln, scale=1.0 / D, bias=EPS), reads=[(ssb, "f")], writes=[(ssb, "f")])
                P.op("act", lambda e: e.activation(out=ssb[:, 1:2], in_=ssb[:, 1:2], func=AF.Exp, scale=-0.5), reads=[(ssb, "f")], writes=[(ssb, "f")])
                P.op("dve", (lambda o_: lambda e: e.scalar_tensor_tensor(out=o_[:], in0=o_[:], scalar=ssb[:, 1:2], in1=FWt[:], op0=ALU.mult, op1=ALU.mult))(o_),
                     reads=[o_, (ssb, "f"), FWt], writes=[o_])
                P.dma("sp", (lambda o_, i: lambda e: e.dma_start(out=d_out[i * 128:(i + 1) * 128, :], in_=o_[:]))(o_, i), reads=[o_])

            if NTB:
                norm2_h2(None, xstore[0], h2[0], A2t, B2t, ssb, add_eng="dve")
            for i in range(NTB):
                x1 = xstore[i]
                h2_ = h2[i % 2]
                hr, hgb, o_ = hraw[i % 2], hgel[i % 2], ot[i % 2]
                Y0, Y1 = YB[i % 2]
                for k4 in range(32):
                    if k4 == 2 and i > 0:
                        b2_final(i - 1)
                    if k4 == 16 and i + 1 < NTB:
                        norm2_h2(None, xstore[i + 1], h2[(i + 1) % 2], A2t, B2t, ssb, add_eng="dve")
                    bufs = []
                    for r in range(4):
                        k = k4 * 4 + r
                        rb = ring[cu % NB]; cu += 1
                        bufs.append(rb)
                        P.dma("pool", (lambda rb, i, k: lambda e: e.indirect_dma_start(out=rb[:], out_offset=None, in_=d_t16,
                                                                                       in_offset=bass.IndirectOffsetOnAxis(ap=idxst[i][:, k:k + 1], axis=0)))(rb, i, k),
                              reads=[idxst[i], t16b], writes=[rb])
                        P.op("dve", (lambda rb, k, hr, h2_: lambda e: e.scalar_tensor_tensor(out=rb[:, 0:D], in0=rb[:, 0:D], scalar=1.0, in1=h2_[:], op0=ALU.mult, op1=ALU.mult,
                                                                                             accum_out=hr[:, k:k + 1]))(rb, k, hr, h2_),
                             reads=[rb, h2_], writes=[(rb, "u"), (hr, k)])
                    ksl = slice(k4 * 4, k4 * 4 + 4)
                    P.op("act", (lambda hgb, hr, ksl: lambda e: e.activation(out=hgb[:, ksl], in_=hr[:, ksl], func=AF.Gelu))(hgb, hr, ksl),
                         reads=[(hr, k4 * 4 + r) for r in range(4)], writes=[(hgb, k4)])
                    P.op("dve", (lambda hgb, i, ksl: lambda e: e.tensor_tensor(out=hgb[:, ksl], in0=hgb[:, ksl], in1=gatest[i][:, ksl], op=ALU.mult))(hgb, i, ksl),
                         reads=[(hgb, k4), gatest[i]], writes=[(hgb, k4)])
                    for r in range(4):
                        k = k4 * 4 + r
                        rb = bufs[r]
                        dk_ = Dr[cd % 8]; cd += 1
                        P.op("act", (lambda dk_, hgb, k: lambda e: e.activation(out=dk_[:], in_=ident[:], func=AF.Copy, scale=hgb[:, k:k + 1]))(dk_, hgb, k),
                             reads=[ident, (hgb, k4)], writes=[dk_])
                        mm(Y0, Y0[:, :], dk_[:, :], rb[:, D:D + 512], k == 0, k == 127, [dk_, rb])
                        mm(Y1, Y1[:, :], dk_[:, :], rb[:, D + 512:2 * D], k == 0, k == 127, [dk_, rb])
            if NTB:
                b2_final(NTB - 1)
            P.wait_all_dma("sp")
            P.flush()
    except _Stop:
        pass
    top.close()
    return nc


def _host_inputs(inp):
    f = lambda a: np.ascontiguousarray(np.asarray(a, dtype=np.float32))
    x = f(inp["x"])
    c = f(inp["c"])
    ar = np.arange(128)
    q = ar[:, None]
    j = np.arange(256)[None, :]
    maskr = np.where((j > q) & (j <= q + 128), 0.0, -30000.0).astype(np.float32)
    mask0_first = maskr.copy()
    mask0_first[:, :128] = -30000.0
    common = {
        "w_ada": f(inp["w_ada"][0]), "b_ada": f(inp["b_ada"][0]).reshape(1, -1), "norm1_w": f(inp["norm1_w"][0]).reshape(1, -1),
        "w_in": f(inp["w_in"][0]), "sinks": f(inp["attn_sinks"][0]).reshape(1, -1), "gate_up": f(inp["gla_gate_up"][0]),
        "gate_bias": f(inp["gla_gate_bias"][0]).reshape(1, -1), "gla_norm_w": f(inp["gla_norm_w"][0]).reshape(1, -1),
        "w_out": f(inp["w_out"][0]), "norm2_w": f(inp["norm2_w"][0]).reshape(1, -1), "peer_wq": f(inp["peer_wq"][0]),
        "skT": f(np.transpose(np.asarray(inp["peer_subkeys"][0], dtype=np.float32).reshape(16, 128, 128), (2, 0, 1))),
        "peer_uv": np.ascontiguousarray(np.concatenate([np.asarray(inp["peer_u"][0], np.float32), np.asarray(inp["peer_v"][0], np.float32)], axis=1)), "final_norm_w": f(inp["final_norm_w"]).reshape(1, -1),
        "ident": np.eye(128, dtype=np.float32),
        "tri": (ar[:, None] <= ar[None, :]).astype(np.float32),
        "triu": (ar[:, None] > ar[None, :]).astype(np.float32),
        "iota16": np.tile(np.arange(16, dtype=np.float32)[None, :], (128, 1)),
        "maskr": maskr,
    }
    maps = []
    for core in range(NCORES):
        b, s = core // 4, core % 4
        xpre = np.zeros((NPRE * 128, D), np.float32)
        nvalid = s * TOK
        if nvalid:
            xpre[NPRE * 128 - nvalid:] = x[b, :nvalid]
        pv = np.zeros((128, NPRE), np.float32)
        pv[:, NPRE - nvalid // 128:] = 1.0 if nvalid else 0.0
        m = dict(common)
        m["x_own"] = np.ascontiguousarray(x[b, s * TOK:(s + 1) * TOK])
        m["x_pre"] = xpre
        m["pvalid"] = pv
        m["mask0"] = mask0_first if s == 0 else maskr
        m["c_t"] = np.ascontiguousarray(c[b].reshape(8, 128).T)
        maps.append(m)
    return maps


def kernel(**inputs):
    maps = _host_inputs(inputs)
    nc = build_program()
    res = run_bass_kernel_spmd(nc, maps[:RUN_CORES], core_ids=list(range(RUN_CORES)))
    out = np.zeros((2, 8192, D), np.float32)
    for core in range(RUN_CORES):
        b, s = core // 4, core % 4
        out[b, s * TOK:(s + 1) * TOK] = np.asarray(res.results[core]["out"], dtype=np.float32)
    if DEBUG:
        _dbg_out["res"] = res.results
    return out
```

```python
from contextlib import ExitStack
import numpy as np
import concourse.bass as bass
import concourse.mybir as mybir
from concourse.bass_utils import run_bass_kernel_spmd

F32 = mybir.dt.float32
F32R = mybir.dt.float32r
I32 = mybir.dt.int32
BF16 = mybir.dt.bfloat16
U32 = mybir.dt.uint32
AF = mybir.ActivationFunctionType
ALU = mybir.AluOpType
AX = mybir.AxisListType

NCORES = 8
TOK = 2048
NT = TOK // 128
NPRE = 48
D = 1024
EPS = 1e-6
DEBUG = False
STAGE = 99
SUB = 99
XSRC_PRE = False
NT_RUN = 16
NPRE_RUN = 48
RUN_CORES = NCORES


class _Stop(Exception):
    pass
_dbg_out = {}

STREAMS = ("pe", "act", "dve", "pool", "sp")
DMA_RING = {"sp": 8, "act": 4, "pool": 16}


class Buf:
    def __init__(self, name, t, psum=False):
        self.name = name
        self.t = t
        self.regs = {}
        self.psum = psum

    def __getitem__(self, idx):
        return self.t[idx]


class Prog:
    def __init__(self, nc):
        self.nc = nc
        self.ops = {s: [] for s in STREAMS}
        self.seq = {s: 0 for s in STREAMS}
        self.dcount = {q: 0 for q in DMA_RING}
        self.waited = {s: {} for s in STREAMS}
        self.last_dma_tok = {}

    def _need(self, stream, tok, hazard, waits):
        if tok is None:
            return
        semkey, val, pstream, kind = tok
        if kind == "c" and pstream == stream:
            if stream == "pe":
                return
        w = self.waited[stream]
        if w.get(semkey, -1) >= val:
            return
        w[semkey] = val
        waits.append((semkey, val))

    def _collect(self, stream, reads, writes):
        waits = []
        for b, k in reads:
            regs = b.regs
            if k is None:
                for e in regs.values():
                    self._need(stream, e["w"], "RAW", waits)
            elif k in regs:
                self._need(stream, regs[k]["w"], "RAW", waits)
            elif None in regs:
                self._need(stream, regs[None]["w"], "RAW", waits)
        for b, k in writes:
            regs = b.regs
            if k is None:
                for e in regs.values():
                    self._need(stream, e["w"], "WAW", waits)
                    for t in e["r"].values():
                        self._need(stream, t, "WAR", waits)
            else:
                for kk in (k, None):
                    if kk in regs:
                        e = regs[kk]
                        self._need(stream, e["w"], "WAW", waits)
                        for t in e["r"].values():
                            self._need(stream, t, "WAR", waits)
        return waits

    def _record(self, tok, reads, writes):
        for b, k in reads:
            regs = b.regs
            if k is None:
                e = regs.setdefault(None, {"w": None, "r": {}})
            else:
                if k not in regs:
                    regs[k] = {"w": regs[None]["w"] if None in regs else None, "r": {}}
                e = regs[k]
            e["r"][tok[0]] = tok
        for b, k in writes:
            if k is None:
                b.regs = {None: {"w": tok, "r": {}}}
            else:
                b.regs[k] = {"w": tok, "r": {}}

    @staticmethod
    def _norm(lst):
        return [x if isinstance(x, tuple) else (x, None) for x in (lst or [])]

    def op(self, stream, emit, reads=None, writes=None):
        reads = self._norm(reads)
        writes = self._norm(writes)
        writes = writes + [(b, None) for b, k in reads if b.psum]
        reads = [(b, k) for b, k in reads if not b.psum]
        waits = self._collect(stream, reads, writes)
        self.seq[stream] += 1
        tok = (("c", stream), self.seq[stream], stream, "c")
        self._record(tok, reads, writes)
        self.ops[stream].append((waits, emit, (("c", stream), 1)))
        return tok

    def dma(self, queue, emit, reads=None, writes=None):
        reads = self._norm(reads)
        writes = self._norm(writes)
        waits = self._collect(queue, reads, writes)
        i = self.dcount[queue]
        self.dcount[queue] += 1
        K = DMA_RING[queue]
        slot = i % K
        semkey = ("d", queue, slot)
        if i >= K:
            self._need(queue, (semkey, 16 * (i // K), queue, "d"), "WAW", waits)
        val = 16 * (i // K + 1)
        tok = (semkey, val, queue, "d")
        self.last_dma_tok[semkey] = val
        self._record(tok, reads, writes)
        self.ops[queue].append((waits, emit, (semkey, 16)))
        return tok

    def barrier(self, skip_queue=None):
        for s in STREAMS:
            waits = []
            for s2 in STREAMS:
                if s2 != s and self.seq[s2] > 0:
                    self._need(s, (("c", s2), self.seq[s2], s2, "c"), "RAW", waits)
            for semkey, val in self.last_dma_tok.items():
                if skip_queue is not None and semkey[1] == skip_queue:
                    continue
                self._need(s, (semkey, val, semkey[1], "d"), "RAW", waits)
            if waits:
                self.ops[s].append((waits, None, None))

    def wait_all_dma(self, stream="sp"):
        waits = []
        for semkey, val in self.last_dma_tok.items():
            self._need(stream, (semkey, val, semkey[1], "d"), "RAW", waits)
        if waits:
            self.ops[stream].append((waits, None, None))

    def setup(self, stack):
        nc = self.nc
        self.sems = {}
        for s in STREAMS:
            self.sems[("c", s)] = stack.enter_context(nc.semaphore(f"c_{s}"))
        for q, K in DMA_RING.items():
            for j in range(K):
                self.sems[("d", q, j)] = stack.enter_context(nc.semaphore(f"d_{q}_{j}"))

    def flush(self):
        nc = self.nc
        sems = self.sems
        ops = self.ops
        self.ops = {s: [] for s in STREAMS}

        def run(stream):
            def f(eng):
                for waits, emit, inc in ops[stream]:
                    for semkey, val in waits:
                        eng.wait_ge(sems[semkey], val)
                    if emit is not None:
                        emit(eng).then_inc(sems[inc[0]], inc[1])
            return f

        with nc.Block() as block:
            block.tensor(run("pe"))
            block.scalar(run("act"))
            block.vector(run("dve"))
            block.gpsimd(run("pool"))
            block.sync(run("sp"))


def build_program():
    nc = bass.Bass("TRN2", target_bir_lowering=False)

    def din(name, shape, dt=F32):
        return nc.dram_tensor(name, shape, dt, kind="ExternalInput")

    d_xown = din("x_own", [TOK, D]).ap()
    d_xpre = din("x_pre", [NPRE * 128, D]).ap()
    d_pvalid = din("pvalid", [128, NPRE]).ap()
    d_mask0 = din("mask0", [128, 256]).ap()
    d_maskr = din("maskr", [128, 256]).ap()
    d_ct = din("c_t", [128, 8]).ap()
    h_wada = din("w_ada", [D, 6 * D])
    h_bada = din("b_ada", [1, 6 * D])
    h_n1w = din("norm1_w", [1, D])
    h_win = din("w_in", [D, 2320])
    h_sinks = din("sinks", [1, 8])
    d_gup = din("gate_up", [16, 256]).ap()
    d_gbias = din("gate_bias", [1, 256]).ap()
    h_gnw = din("gla_norm_w", [1, 128])
    h_wout = din("w_out", [D, D])
    h_n2w = din("norm2_w", [1, D])
    h_wq = din("peer_wq", [D, 2048])
    d_skT = din("skT", [128, 16, 128]).ap()
    d_puv = din("peer_uv", [16384, 2 * D]).ap()
    d_t16 = nc.dram_tensor("tab16", [16384, 2 * D], BF16, kind="Internal").ap()
    h_fnw = din("final_norm_w", [1, D])
    d_ident = din("ident", [128, 128]).ap()
    d_tri = din("tri", [128, 128]).ap()
    d_triu = din("triu", [128, 128]).ap()
    d_iota = din("iota16", [128, 16]).ap()
    d_out = nc.dram_tensor("out", [TOK, D], F32, kind="ExternalOutput").ap()
    d_bc = nc.dram_tensor("bc_scratch", [6, 128, D], F32, kind="Internal").ap()
    dbg = {}
    if DEBUG:
        for nm, shp, dt in (("dbg_mix", [TOK, D], F32), ("dbg_x1", [TOK, D], F32),
                            ("dbg_idx", [TOK, 128], I32), ("dbg_gate", [TOK, 128], F32),
                            ("dbg_bc", [6, 128, D], F32), ("dbg_hid", [TOK, 128], F32), ("dbg_hraw", [TOK, 128], F32), ("dbg_y", [TOK, D], F32)):
            dbg[nm] = nc.dram_tensor(nm, shp, dt, kind="ExternalOutput").ap()

    def bcast_row(handle, n, off=0):
        return bass.AP(handle, off, [[0, 128], [1, n]])

    wada_r = h_wada.ap().rearrange("(kc p) n -> p kc n", p=128)
    win_r = h_win.ap().rearrange("(kc p) n -> p kc n", p=128)
    wout_r = h_wout.ap().rearrange("(kc p) n -> p kc n", p=128)
    wq_r = h_wq.ap().rearrange("(kc p) n -> p kc n", p=128)

    P = Prog(nc)
    top = ExitStack()
    P.setup(top)

    def maybe_stop(k):
        return

    try:

        def sb(st, name, shape, dt=F32):
            return Buf(name, st.enter_context(nc.sbuf_tensor("s_" + name, shape, dt)))

        banks = [Buf(f"ps{j}", top.enter_context(nc.psum_tensor(f"ps{j}", [128, 512], F32)), psum=True) for j in range(8)]
        pcnt = [0]

        def psum():
            b = banks[pcnt[0] % 6]
            pcnt[0] += 1
            return b

        ident = sb(top, "ident", [128, 128])
        tri = sb(top, "tri", [128, 128])
        triu = sb(top, "triu", [128, 128])
        iota16 = sb(top, "iota16", [128, 16])
        ones = sb(top, "ones", [128, 128])
        for t_, d_ in ((ident, d_ident), (tri, d_tri), (triu, d_triu), (iota16, d_iota)):
            P.dma("sp", (lambda t_, d_: lambda e: e.dma_start(out=t_[:], in_=d_))(t_, d_), writes=[t_])
        P.op("dve", lambda e: e.memset(ones[:], 1.0), writes=[ones])

        t16b = Buf("tab16", None)
        NCAST = 64
        castn = [0]

        def issue_cast():
            j = castn[0]
            if j >= NCAST:
                return
            castn[0] += 1
            rows = 16384 // NCAST
            P.dma("pool", lambda e: e.dma_start(out=d_t16[j * rows:(j + 1) * rows, :], in_=d_puv[j * rows:(j + 1) * rows, :]), writes=[(t16b, j)])


        def mm(out_b, out_ap, lhsT, rhs, start, stop, reads):
            return P.op("pe", lambda e: e.matmul(out_ap, lhsT=lhsT, rhs=rhs, start=start, stop=stop),
                        reads=reads, writes=[out_b])

        def tr(out_b, out_ap, in_ap, reads):
            return P.op("pe", lambda e: e.transpose(out_ap, in_ap, ident[:]), reads=reads + [ident], writes=[out_b])

        def rstd_from_ss(ss_b, ss_ap, n, keys=None):
            P.op("act", lambda e: e.activation(out=ss_ap, in_=ss_ap, func=AF.Ln, scale=1.0 / n, bias=EPS), reads=[ss_b], writes=[ss_b])
            P.op("act", lambda e: e.activation(out=ss_ap, in_=ss_ap, func=AF.Exp, scale=-0.5), reads=[ss_b], writes=[ss_b])

        def load_round(dst, src_r, n, tag, perm_aq=False):
            with ExitStack() as stg:
                half = n // 2
                stage = [sb(stg, f"wst_{tag}{j}", [128, half]) for j in range(2)]
                q = 0
                for kc in range(8):
                    for hf in range(2):
                        st_ = stage[q % 2]
                        P.dma("sp", (lambda st_, kc, hf: lambda e: e.dma_start(out=st_[:], in_=src_r[:, kc, hf * half:(hf + 1) * half]))(st_, kc, hf), writes=[st_])
                        if perm_aq and hf == 0:
                            P.op("act", (lambda st_, kc: lambda e: e.copy(
                                out=dst[:, kc, 0:512].rearrange("p (c two d) -> p two c d", c=4, two=2, d=64),
                                in_=st_[:, 0:512].rearrange("p (two c d) -> p two c d", two=2, c=4, d=64)))(st_, kc), reads=[st_], writes=[(dst, kc)])
                            P.op("dve", (lambda st_, kc: lambda e: e.tensor_copy(out=dst[:, kc, 512:half], in_=st_[:, 512:half]))(st_, kc),
                                 reads=[st_], writes=[(dst, kc)])
                        elif q % 2 == 0:
                            P.op("act", (lambda st_, kc, hf: lambda e: e.copy(out=dst[:, kc, hf * half:(hf + 1) * half], in_=st_[:]))(st_, kc, hf),
                                 reads=[st_], writes=[(dst, kc)])
                        else:
                            P.op("dve", (lambda st_, kc, hf: lambda e: e.tensor_copy(out=dst[:, kc, hf * half:(hf + 1) * half], in_=st_[:]))(st_, kc, hf),
                                 reads=[st_], writes=[(dst, kc)])
                        q += 1
                P.barrier(skip_queue="pool")
                P.flush()

        with ExitStack() as ph:
            wada = [sb(ph, f"wada{j}", [128, 8, 512]) for j in range(4)]
            stage = [sb(ph, f"stg{j}", [128, 512]) for j in range(2)]
            n1w = sb(ph, "n1w", [128, D])
            n2w = sb(ph, "n2w", [128, D])
            bada = sb(ph, "bada", [1, 6 * D])
            ct = sb(ph, "ct", [128, 8])
            silc = sb(ph, "silc", [128, 8])
            silc_bc = sb(ph, "silc_bc", [128, 8, 128], F32R)
            wadar = [sb(ph, f"wadar{j}", [128, 8, 512], F32R) for j in range(2)]
            P.dma("sp", lambda e: e.dma_start(out=n1w[:], in_=bcast_row(h_n1w, D)), writes=[n1w])
            P.dma("sp", lambda e: e.dma_start(out=n2w[:], in_=bcast_row(h_n2w, D)), writes=[n2w])
            P.dma("sp", lambda e: e.dma_start(out=bada[:], in_=h_bada.ap()), writes=[bada])
            P.dma("sp", lambda e: e.dma_start(out=ct[:], in_=d_ct), writes=[ct])
            P.op("act", lambda e: e.activation(out=silc[:], in_=ct[:], func=AF.Silu), reads=[ct], writes=[silc])
            for kc in range(8):
                P.op("act", (lambda kc: lambda e: e.copy(out=silc_bc[:, kc, :], in_=silc[:, kc:kc + 1].to_broadcast([128, 128])))(kc),
                     reads=[silc], writes=[(silc_bc, kc)])
            for n in range(12):
                wb = wada[n % 4]
                wr = wadar[n % 2]
                P.dma("sp", (lambda wb, n: lambda e: e.dma_start(out=wb[:], in_=wada_r[:, :, n * 512:(n + 1) * 512]))(wb, n), writes=[wb])
                for kc in range(8):
                    if kc % 2 == 0:
                        P.op("act", (lambda wr, wb, kc: lambda e: e.copy(out=wr[:, kc, :], in_=wb[:, kc, :]))(wr, wb, kc), reads=[wb], writes=[(wr, kc)])
                    else:
                        P.op("dve", (lambda wr, wb, kc: lambda e: e.tensor_copy(out=wr[:, kc, :], in_=wb[:, kc, :]))(wr, wb, kc), reads=[wb], writes=[(wr, kc)])
                bk = psum()
                for kc in range(8):
                    mm(bk, bk[:, :], silc_bc[:, kc, :], wr[:, kc, :], kc == 0, False, [(silc_bc, kc), (wr, kc)])
                mm(bk, bk[:, :], ones[0:1, :], bada[0:1, n * 512:(n + 1) * 512], False, True, [ones, bada])
                sec, half = n // 2, n % 2
                sg = stage[n % 2]
                if sec in (1, 4):
                    nw = n1w if sec == 1 else n2w
                    P.op("dve", (lambda sg, bk, nw, half: lambda e: e.scalar_tensor_tensor(
                        out=sg[:], in0=bk[:, :], scalar=1.0, in1=nw[:, half * 512:(half + 1) * 512], op0=ALU.add, op1=ALU.mult))(sg, bk, nw, half),
                        reads=[bk, nw], writes=[sg])
                else:
                    P.op("act", (lambda sg, bk: lambda e: e.copy(out=sg[:], in_=bk[:, :]))(sg, bk), reads=[bk], writes=[sg])
                P.dma("pool", (lambda sg, sec, half: lambda e: e.dma_start(out=d_bc[sec, :, half * 512:(half + 1) * 512], in_=sg[:]))(sg, sec, half), reads=[sg])
            P.barrier()
            P.flush()
            maybe_stop(0)
        bcbuf = Buf("bc_dram", None)

        def load_bc(t, sec):
            P.dma("sp", lambda e: e.dma_start(out=t[:], in_=d_bc[sec]), writes=[t])

        if DEBUG:
            with ExitStack() as ph:
                tt = sb(ph, "dbgt", [128, D])
                for sec in range(6):
                    load_bc(tt, sec)
                    P.dma("sp", (lambda sec: lambda e: e.dma_start(out=dbg["dbg_bc"][sec], in_=tt[:]))(sec), reads=[tt])
                P.barrier(skip_queue="pool")
                P.flush()

        xstore = [sb(top, f"xs{i}", [128, D]) for i in range(NT)]

        with ExitStack() as ph:
            w_in = sb(ph, "w_in", [128, 8, 2320], F32R)
            load_round(w_in, win_r, 2320, "a", perm_aq=True)
            A1t = sb(ph, "A1t", [128, D])
            B1t = sb(ph, "B1t", [128, D])
            gup = sb(ph, "gup", [128, 256])
            gbias = sb(ph, "gbias", [1, 256])
            sink_bc = sb(ph, "sink_bc", [128, 8])
            gnw_bc = sb(ph, "gnw_bc", [128, 128])
            pvalid = sb(ph, "pvalid", [128, NPRE])
            maskr = sb(ph, "maskr", [128, 256])
            mask0 = sb(ph, "mask0", [128, 256])
            xt0 = sb(ph, "xt0", [128, D])
            h = sb(ph, "h", [128, D])
            hT = sb(ph, "hT", [128, 8, 128], F32R)
            hTb = sb(ph, "hTb", [128, 8, 128], F32R)
            decay2 = sb(ph, "decay2", [128, 2])
            ss1b = sb(ph, "ss1b", [128, 1])
            aqTp = sb(ph, "aqTp", [128, 8, 128], F32R)
            akT = [sb(ph, f"akT{j}", [128, 128], F32R) for j in range(2)]
            av = [sb(ph, f"av{j}", [128, 128], F32R) for j in range(2)]
            gqTp = sb(ph, "gqTp", [128, 4, 128])
            raw = sb(ph, "raw", [128, 4, 128])
            glrTL = [sb(ph, f"glrT{j}", [128, 128]) for j in range(2)]
            k_tmL = [sb(ph, f"k_tm{j}", [128, 256]) for j in range(2)]
            v_tmL = [sb(ph, f"v_tm{j}", [128, 512]) for j in range(2)]
            sg_ = sb(ph, "sgg", [128, 512])
            e1 = sb(ph, "e1", [128, 256])
            sp_ = sb(ph, "sp", [128, 256])
            eq = sb(ph, "eq", [128, 2, 128])
            ek = sb(ph, "ek", [128, 2, 128])
            kt = sb(ph, "kt", [128, 2, 128])
            er = sb(ph, "er", [128, 256])
            khat = sb(ph, "khat", [128, 256])
            attTm = sb(ph, "attTm", [128, 4, 128])
            S_sb = sb(ph, "S_sb", [128, 2, 128])
            decay = sb(ph, "decay", [128, 2])
            sc = sb(ph, "sc", [128, 4, 256])
            PT = sb(ph, "PT", [128, 8, 128], F32R)
            st8 = sb(ph, "st8", [128, 5, 8])
            ss1 = sb(ph, "ss1", [128, 1])
            ss4 = sb(ph, "ss4", [128, 4])
            gtmp = sb(ph, "gtmp", [128, 512])
            xt = [xt0, xstore[15]]
            h_alt = xstore[14]
            e1L = [e1, Buf("e1v", gtmp.t[:, 0:256])]
            spL = [sp_, Buf("spv", gtmp.t[:, 256:512])]
            erL = [er, Buf("erv", attTm.t[:, 0:2, :].rearrange("p a b -> p (a b)"))]
            khatL = [khat, Buf("khatv", attTm.t[:, 2:4, :].rearrange("p a b -> p (a b)"))]
            k_tm3 = k_tmL + [Buf("ktm3v", raw.t[:, 0:2, :].rearrange("p a b -> p (a b)"))]
            v_tm3 = v_tmL + [Buf("vtm3v", sg_.t[:, :])]

            load_bc(A1t, 1)
            load_bc(B1t, 0)
            P.op("dve", lambda e: e.memset(gup[:], 0.0), writes=[gup])
            P.dma("sp", lambda e: e.dma_start(out=gup[112:128, :], in_=d_gup), writes=[gup])
            P.dma("sp", lambda e: e.dma_start(out=gbias[:], in_=d_gbias), writes=[gbias])
            P.dma("sp", lambda e: e.dma_start(out=sink_bc[:], in_=bcast_row(h_sinks, 8)), writes=[sink_bc])
            P.dma("sp", lambda e: e.dma_start(out=gnw_bc[:], in_=bcast_row(h_gnw, 128)), writes=[gnw_bc])
            P.dma("sp", lambda e: e.dma_start(out=pvalid[:], in_=d_pvalid), writes=[pvalid])
            P.dma("sp", lambda e: e.dma_start(out=maskr[:], in_=d_maskr), writes=[maskr])
            P.dma("sp", lambda e: e.dma_start(out=mask0[:], in_=d_mask0), writes=[mask0])
            w_in_r = w_in.t[:]
            w_in_f = w_in.t[:].bitcast(F32)
            zsrc = xstore[13]
            P.op("dve", lambda e: e.memset(zsrc[:], 0.0), writes=[zsrc])
            P.op("act", lambda e: e.copy(out=aqTp[:], in_=zsrc[:].rearrange("p (a b) -> p a b", a=8)), reads=[zsrc], writes=[aqTp])
            P.op("dve", lambda e: e.memset(gqTp[:], 0.0), writes=[gqTp])
            P.op("dve", lambda e: e.memset(S_sb[:], 0.0), writes=[S_sb])
            P.op("act", lambda e: e.copy(out=akT[1][:], in_=zsrc[:, 0:128]), reads=[zsrc], writes=[akT[1]])
            P.op("act", lambda e: e.copy(out=av[1][:], in_=zsrc[:, 0:128]), reads=[zsrc], writes=[av[1]])

            cnt = [0]

            def tile_front(xsrc, own, pj):
                x_ = xt0
                cnt[0] += 1
                P.dma("sp", lambda e: e.dma_start(out=x_[:], in_=xsrc), writes=[x_])
                P.op("act", lambda e: e.activation(out=h[:], in_=x_[:], func=AF.Square, accum_out=ss1[:, 0:1]), reads=[x_], writes=[h, ss1])
                rstd_from_ss(ss1, ss1[:, 0:1], D)
                P.op("dve", lambda e: e.scalar_tensor_tensor(out=h[:], in0=x_[:], scalar=ss1[:, 0:1], in1=A1t[:], op0=ALU.mult, op1=ALU.mult),
                     reads=[x_, ss1, A1t], writes=[h])
                P.op("dve", lambda e: e.tensor_tensor(out=h[:], in0=h[:], in1=B1t[:], op=ALU.add), reads=[h, B1t], writes=[h])
                ba, bb = psum(), psum()
                for j in range(8):
                    bk = ba if j < 4 else bb
                    tr(bk, bk[:, (j % 4) * 128:(j % 4 + 1) * 128], h[:, j * 128:(j + 1) * 128], [h])
                P.op("act", lambda e: e.copy(out=hT[:, 0:4, :], in_=ba[:, :].rearrange("p (a b) -> p a b", a=4)), reads=[ba], writes=[(hT, 0)])
                P.op("dve", lambda e: e.tensor_copy(out=hT[:, 4:8, :], in_=bb[:, :].rearrange("p (a b) -> p a b", a=4)), reads=[bb], writes=[(hT, 1)])
                hTk = lambda kc: (hT, 0 if kc < 4 else 1)
                last_pre = (not own) and pj == NPRE - 1
                cur = (cnt[0] - 1) % 2 if own else 1
                return x_, last_pre

            def proj_tm(cols, bk, ncol, hT=hT):
                for kc in range(8):
                    mm(bk, bk[:, 0:ncol], hT[:, kc, :], w_in_r[:, kc, cols[0]:cols[1]], kc == 0, kc == 7,
                       [(hT, 0 if kc < 4 else 1), (w_in, kc)])

            def proj_fm(lhs_fn, bk, col0, m=128, f32=False, hT=hT):
                for kc in range(8):
                    lhsT = lhs_fn(kc)
                    rhs = hT[:, kc, :]
                    if f32:
                        rhs = rhs.bitcast(F32)
                    mm(bk, bk[0:m, col0:col0 + 128], lhsT, rhs, kc == 0, kc == 7, [(hT, 0 if kc < 4 else 1), (w_in, kc)])

            def gla_common(own, pj, bs):
                glrT, k_tm = glrTL[bs], k_tmL[bs]
                bz = psum()
                mm(bz, bz[:, 0:256], glrT[:, :], gup[:, :], True, False, [glrT, gup])
                mm(bz, bz[:, 0:256], ones[0:1, :], gbias[0:1, :], False, True, [ones, gbias])
                P.op("act", lambda e: e.activation(out=e1[:], in_=bz[:, 0:256], func=AF.Exp, scale=-1.0), reads=[bz], writes=[e1])
                P.op("act", lambda e: e.activation(out=sp_[:], in_=e1[:], func=AF.Ln, bias=1.0), reads=[e1], writes=[sp_])
                br = psum()
                mm(br, br[:, 0:256], triu[:, :], sp_[:, :], True, True, [triu, sp_])
                bt = psum()
                if own:
                    for hc in range(2):
                        mm(bt, bt[:, hc * 128:(hc + 1) * 128], sp_[:, hc * 128:(hc + 1) * 128], tri[:, :], True, True, [sp_, tri])
                for hc in range(2):
                    mm(bt, bt[:, 256 + 2 * hc:258 + 2 * hc], sp_[:, hc * 128:(hc + 1) * 128], ones[:, 0:2], True, True, [sp_, ones])
                P.op("act", lambda e: e.activation(out=er[:], in_=br[:, 0:256], func=AF.Exp, scale=-1.0 / 16), reads=[br], writes=[er])
                P.op("dve", lambda e: e.tensor_tensor(out=khat[:], in0=k_tm[:], in1=er[:], op=ALU.mult), reads=[k_tm, er], writes=[khat])
                P.op("act", lambda e: e.activation(out=decay[:], in_=bt[:, 256:260].rearrange("p (a b) -> p a b", a=2)[:, :, 0],
                                                   func=AF.Exp, scale=-1.0 / 16), reads=[bt], writes=[decay])
                if own:
                    btv = bt[:, 0:256].rearrange("p (a b) -> p a b", a=2)
                    P.op("act", lambda e: e.activation(out=eq[:], in_=btv, func=AF.Exp, scale=-1.0 / 16), reads=[bt], writes=[eq])
                    P.op("act", lambda e: e.activation(out=ek[:], in_=btv, func=AF.Exp, scale=1.0 / 16), reads=[bt], writes=[ek])
                    P.op("dve", lambda e: e.scalar_tensor_tensor(out=gqTp[0:64, 0:4:2, :], in0=eq[0:64, :, :], scalar=0.125, in1=raw[0:64, 0:2, :],
                                                                 op0=ALU.mult, op1=ALU.mult), reads=[eq, raw], writes=[(gqTp, 0)])
                    P.op("dve", lambda e: e.scalar_tensor_tensor(out=gqTp[64:128, 1:4:2, :], in0=eq[64:128, :, :], scalar=0.125, in1=raw[64:128, 0:2, :],
                                                                 op0=ALU.mult, op1=ALU.mult), reads=[eq, raw], writes=[(gqTp, 1)])
                    P.op("dve", lambda e: e.tensor_tensor(out=kt[:], in0=ek[:], in1=raw[:, 2:4, :], op=ALU.mult), reads=[ek, raw], writes=[kt])

            def state_update(bs):
                v_tm = v_tmL[bs]
                bd = psum()
                for hc in range(2):
                    mm(bd, bd[:, hc * 256:(hc + 1) * 256], khat[:, hc * 128:(hc + 1) * 128], v_tm[:, hc * 256:(hc + 1) * 256], True, True, [khat, v_tm])
                for hh in range(4):
                    p0, c = (hh % 2) * 64, hh // 2
                    P.op("dve", (lambda p0, c, hh: lambda e: e.scalar_tensor_tensor(
                        out=S_sb[p0:p0 + 64, c, :], in0=S_sb[p0:p0 + 64, c, :], scalar=decay[p0:p0 + 64, c:c + 1],
                        in1=bd[p0:p0 + 64, c * 256 + (hh % 2) * 128:c * 256 + (hh % 2) * 128 + 128], op0=ALU.mult, op1=ALU.add))(p0, c, hh),
                        reads=[(S_sb, hh), decay, bd], writes=[(S_sb, hh)])

            HH, HT, SS1, DEC = [h, h_alt], [hT, hTb], [ss1, ss1b], [decay, decay2]

            def pS1a(pj):
                b_ = pj % 2
                x_, h_, hT_, ss_ = xt[b_], HH[b_], HT[b_], SS1[b_]
                P.dma("sp", lambda e: e.dma_start(out=x_[:], in_=d_xpre[pj * 128:(pj + 1) * 128, :]), writes=[x_])
                P.op("act", lambda e: e.activation(out=h_[:], in_=x_[:], func=AF.Square, accum_out=ss_[:, 0:1]), reads=[x_], writes=[h_, ss_])
                rstd_from_ss(ss_, ss_[:, 0:1], D)
                P.op("dve", lambda e: e.scalar_tensor_tensor(out=h_[:], in0=x_[:], scalar=ss_[:, 0:1], in1=A1t[:], op0=ALU.mult, op1=ALU.mult),
                     reads=[x_, ss_, A1t], writes=[h_])
                P.op("dve", lambda e: e.tensor_tensor(out=h_[:], in0=h_[:], in1=B1t[:], op=ALU.add), reads=[h_, B1t], writes=[h_])

            def pS1b(pj):
                b_ = pj % 2
                h_, hT_ = HH[b_], HT[b_]
                ba, bb = banks[6], banks[7]
                for j in range(8):
                    bk = ba if j < 4 else bb
                    tr(bk, bk[:, (j % 4) * 128:(j % 4 + 1) * 128], h_[:, j * 128:(j + 1) * 128], [h_])
                P.op("act", lambda e: e.copy(out=hT_[:, 0:4, :], in_=ba[:, :].rearrange("p (a b) -> p a b", a=4)), reads=[ba], writes=[(hT_, 0)])
                P.op("dve", lambda e: e.tensor_copy(out=hT_[:, 4:8, :], in_=bb[:, :].rearrange("p (a b) -> p a b", a=4)), reads=[bb], writes=[(hT_, 1)])

            def pS2(pj):
                b_ = pj % 2
                hT_ = HT[b_]
                glrT, k_tm, v_tm = glrTL[b_], k_tm3[pj % 3], v_tm3[pj % 3]
                b2 = banks[3]
                proj_tm((1024, 1536), b2, 512, hT=hT_)
                b3 = banks[4]
                proj_tm((1536, 1792), b3, 256, hT=hT_)
                P.op("act", lambda e: e.copy(out=k_tm[:], in_=b2[:, 0:256]), reads=[b2], writes=[k_tm])
                P.op("act", lambda e: e.activation(out=v_tm[:, 0:256], in_=b2[:, 256:512], func=AF.Copy, scale=pvalid[:, pj:pj + 1]),
                     reads=[b2, pvalid], writes=[(v_tm, 0)])
                P.op("act", lambda e: e.activation(out=v_tm[:, 256:512], in_=b3[:, 0:256], func=AF.Copy, scale=pvalid[:, pj:pj + 1]),
                     reads=[b3, pvalid], writes=[(v_tm, 1)])
                bf = banks[5]
                proj_fm(lambda kc: w_in_r[:, kc, 2192:2320], bf, 0, hT=hT_)
                P.op("act", lambda e: e.copy(out=glrT[:], in_=bf[:, 0:128]), reads=[bf], writes=[glrT])
                if pj == NPRE - 1:
                    b1 = banks[3]
                    proj_tm((640, 768), b1, 128, hT=hT_)
                    P.op("act", lambda e: e.copy(out=av[1][:], in_=b1[:, 0:128]), reads=[b1], writes=[av[1]])
                    bg = banks[4]
                    proj_fm(lambda kc: w_in_r[:, kc, 512:640], bg, 0, hT=hT_)
                    P.op("act", lambda e: e.copy(out=akT[1][:], in_=bg[:, 0:128]), reads=[bg], writes=[akT[1]])

            def pS3(pj):
                b_ = pj % 2
                glrT = glrTL[b_]
                e1_, spb, er_, dec_ = e1L[b_], spL[b_], erL[b_], DEC[b_]
                bz = banks[1]
                mm(bz, bz[:, 0:256], glrT[:, :], gup[:, :], True, False, [glrT, gup])
                mm(bz, bz[:, 0:256], ones[0:1, :], gbias[0:1, :], False, True, [ones, gbias])
                P.op("act", lambda e: e.activation(out=e1_[:], in_=bz[:, 0:256], func=AF.Exp, scale=-1.0), reads=[bz], writes=[e1_])
                P.op("act", lambda e: e.activation(out=spb[:], in_=e1_[:], func=AF.Ln, bias=1.0), reads=[e1_], writes=[spb])
                br = banks[2]
                mm(br, br[:, 0:256], triu[:, :], spb[:], True, True, [triu, spb])
                for hc in range(2):
                    mm(br, br[:, 256 + 2 * hc:258 + 2 * hc], spb[:, hc * 128:(hc + 1) * 128], ones[:, 0:2], True, True, [spb, ones])
                P.op("act", lambda e: e.activation(out=er_[:], in_=br[:, 0:256], func=AF.Exp, scale=-1.0 / 16), reads=[br], writes=[er_])
                P.op("act", lambda e: e.activation(out=dec_[:], in_=br[:, 256:260].rearrange("p (a b) -> p a b", a=2)[:, :, 0],
                                                   func=AF.Exp, scale=-1.0 / 16), reads=[br], writes=[dec_])

            def pS4(pj):
                b_ = pj % 2
                k_tm, v_tm = k_tm3[pj % 3], v_tm3[pj % 3]
                er_, kh_, dec_ = erL[b_], khatL[b_], DEC[b_]
                P.op("dve", lambda e: e.tensor_tensor(out=kh_[:], in0=k_tm[:], in1=er_[:], op=ALU.mult), reads=[k_tm, er_], writes=[kh_])
                bd = banks[0]
                for hc in range(2):
                    mm(bd, bd[:, hc * 256:(hc + 1) * 256], kh_[:, hc * 128:(hc + 1) * 128], v_tm[:, hc * 256:(hc + 1) * 256], True, True, [kh_, v_tm])
                for hh in range(4):
                    p0, c = (hh % 2) * 64, hh // 2
                    P.op("dve", (lambda p0, c, hh: lambda e: e.scalar_tensor_tensor(
                        out=S_sb[p0:p0 + 64, c, :], in0=S_sb[p0:p0 + 64, c, :], scalar=dec_[p0:p0 + 64, c:c + 1],
                        in1=bd[p0:p0 + 64, c * 256 + (hh % 2) * 128:c * 256 + (hh % 2) * 128 + 128], op0=ALU.mult, op1=ALU.add))(p0, c, hh),
                        reads=[(S_sb, hh), dec_, bd], writes=[(S_sb, hh)])

            pjs = list(range(NPRE - NPRE_RUN, NPRE)) if STAGE >= 1 else []
            stages = [(pS1a, 0), (pS2, 1), (pS1b, 0), (pS3, 2), (pS4, 3)]
            for t_ in range(len(pjs) + 3):
                for fn_, lag in stages:
                    jj = t_ - lag
                    if 0 <= jj < len(pjs):
                        fn_(pjs[jj])
            P.barrier(skip_queue="pool")
            if STAGE == 1:
                P.barrier(skip_queue="pool")
                maybe_stop(1)
            for i in range(NT_RUN if STAGE >= 2 else 0):
                cur, prv = i % 2, (i + 1) % 2
                glrT, k_tm, v_tm = glrTL[i % 2], k_tmL[i % 2], v_tmL[i % 2]
                x_, _ = tile_front((d_xpre if XSRC_PRE else d_xown)[i * 128:(i + 1) * 128, :], True, None)
                mix = xstore[i]
                if SUB < -3:
                    continue
                b1 = psum(); proj_tm((640, 768), b1, 128)
                b2 = psum(); proj_tm((1024, 1536), b2, 512)
                b3 = psum(); proj_tm((1536, 2048), b3, 512)
                b4 = psum(); proj_tm((2048, 2304), b4, 256)
                P.op("act", (lambda cur, b1: lambda e: e.copy(out=av[cur][:], in_=b1[:, 0:128]))(cur, b1), reads=[b1], writes=[av[cur]])
                P.op("act", (lambda b2, k_tm: lambda e: e.copy(out=k_tm[:], in_=b2[:, 0:256]))(b2, k_tm), reads=[b2], writes=[k_tm])
                P.op("dve", (lambda b2, v_tm: lambda e: e.tensor_copy(out=v_tm[:, 0:256], in_=b2[:, 256:512]))(b2, v_tm), reads=[b2], writes=[(v_tm, 0)])
                P.op("dve", (lambda b3, v_tm: lambda e: e.tensor_copy(out=v_tm[:, 256:512], in_=b3[:, 0:256]))(b3, v_tm), reads=[b3], writes=[(v_tm, 1)])
                P.op("act", (lambda b3: lambda e: e.activation(out=sg_[:, 0:256], in_=b3[:, 256:512], func=AF.Silu))(b3), reads=[b3], writes=[(sg_, 0)])
                P.op("act", (lambda b4: lambda e: e.activation(out=sg_[:, 256:512], in_=b4[:, 0:256], func=AF.Silu))(b4), reads=[b4], writes=[(sg_, 1)])
                if SUB < -2:
                    continue
                f1 = psum()
                for c in range(4):
                    proj_fm((lambda c: lambda kc: w_in_r[:, kc, c * 128:(c + 1) * 128])(c), f1, c * 128)
                f2 = psum()
                proj_fm(lambda kc: w_in_r[:, kc, 512:640], f2, 0)
                proj_fm(lambda kc: w_in_r[:, kc, 768:896], f2, 128)
                proj_fm(lambda kc: w_in_r[:, kc, 896:1024], f2, 256)
                f3 = psum()
                proj_fm(lambda kc: w_in_r[:, kc, 1024:1152], f3, 0)
                proj_fm(lambda kc: w_in_r[:, kc, 1152:1280], f3, 128)
                f4 = psum()
                proj_fm(lambda kc: w_in_r[:, kc, 2192:2320], f4, 0)
                if SUB < -1:
                    continue
                P.op("act", (lambda f1: lambda e: e.activation(out=aqTp[0:64, 0:4, :], in_=f1[0:64, :].rearrange("p (a b) -> p a b", a=4), func=AF.Copy, scale=0.125))(f1),
                     reads=[f1], writes=[(aqTp, 0)])
                P.op("act", (lambda f1: lambda e: e.activation(out=aqTp[64:128, 4:8, :], in_=f1[64:128, :].rearrange("p (a b) -> p a b", a=4), func=AF.Copy, scale=0.125))(f1),
                     reads=[f1], writes=[(aqTp, 1)])
                P.op("dve", (lambda cur, f2: lambda e: e.tensor_copy(out=akT[cur][:], in_=f2[:, 0:128]))(cur, f2), reads=[f2], writes=[akT[cur]])
                P.op("dve", (lambda f2: lambda e: e.tensor_copy(out=raw[:, 0:2, :], in_=f2[:, 128:384].rearrange("p (a b) -> p a b", a=2)))(f2), reads=[f2], writes=[(raw, 0)])
                P.op("act", (lambda f3: lambda e: e.copy(out=raw[:, 2:4, :], in_=f3[:, 0:256].rearrange("p (a b) -> p a b", a=2)))(f3), reads=[f3], writes=[(raw, 1)])
                P.op("dve", (lambda f4, glrT: lambda e: e.tensor_copy(out=glrT[:], in_=f4[:, 0:128]))(f4, glrT), reads=[f4], writes=[glrT])

                msk = mask0 if i == 0 else maskr
                batt = banks[6]
                mxv, nmx, rsum, es, rden = (st8[:, j, :] for j in range(5))
                def G1():
                    gla_common(True, None, i % 2)

                def G2():
                    bat = psum()
                    for hh in range(4):
                        mm(bat, bat[:, hh * 128:(hh + 1) * 128], kt[:, hh // 2, :], gqTp[:, hh, :], True, True, [kt, gqTp])
                    P.op("dve", (lambda bat: lambda e: e.tensor_tensor(out=attTm[:], in0=bat[:, :].rearrange("p (a b) -> p a b", a=4),
                                                                      in1=tri[:, :].unsqueeze(1).to_broadcast([128, 4, 128]), op=ALU.mult))(bat),
                         reads=[bat, tri], writes=[attTm])

                def G3():
                    bo = banks[7]
                    for hh in range(4):
                        mm(bo, bo[:, hh * 128:(hh + 1) * 128], gqTp[:, hh, :], S_sb[:, hh // 2, :], True, False, [gqTp, S_sb])
                        mm(bo, bo[:, hh * 128:(hh + 1) * 128], attTm[:, hh, :], v_tm[:, hh * 128:(hh + 1) * 128], False, True, [attTm, v_tm])
                    state_update(i % 2)

                def G4():
                    bo = banks[7]
                    for hh in range(4):
                        P.op("act", (lambda hh, bo: lambda e: e.activation(out=gtmp[:, hh * 128:(hh + 1) * 128], in_=bo[:, hh * 128:(hh + 1) * 128],
                                                                          func=AF.Square, accum_out=ss4[:, hh:hh + 1]))(hh, bo), reads=[bo], writes=[(gtmp, hh), (ss4, hh)])
                    rstd_from_ss(ss4, ss4[:, :], 128)
                    for hh in range(4):
                        P.op("dve", (lambda hh, bo: lambda e: e.scalar_tensor_tensor(out=gtmp[:, hh * 128:(hh + 1) * 128], in0=bo[:, hh * 128:(hh + 1) * 128],
                                                                                    scalar=ss4[:, hh:hh + 1], in1=gnw_bc[:], op0=ALU.mult, op1=ALU.mult))(hh, bo),
                             reads=[bo, ss4, gnw_bc], writes=[(gtmp, hh)])
                    P.op("pool", (lambda mix: lambda e: e.tensor_tensor(out=mix[:, 512:1024], in0=gtmp[:], in1=sg_[:], op=ALU.mult))(mix),
                         reads=[gtmp, sg_], writes=[(mix, 1)])


                att_state = {}

                def A1(hg):
                    sbk = [psum(), psum()]
                    for j in range(4):
                        hh = hg * 4 + j
                        bk = sbk[j // 2]
                        c0 = (j % 2) * 256
                        mm(bk, bk[:, c0:c0 + 128], aqTp[:, hh, :], akT[prv][:, :], True, True, [aqTp, akT[prv]])
                        mm(bk, bk[:, c0 + 128:c0 + 256], aqTp[:, hh, :], akT[cur][:, :], True, True, [aqTp, akT[cur]])
                    for jj in range(2):
                        P.op("dve", (lambda jj, bk, msk: lambda e: e.tensor_tensor(out=sc[:, 2 * jj:2 * jj + 2, :], in0=bk[:, :].rearrange("p (a b) -> p a b", a=2),
                                                                                   in1=msk[:, :].unsqueeze(1).to_broadcast([128, 2, 256]), op=ALU.add))(jj, sbk[jj], msk),
                             reads=[sbk[jj], msk], writes=[(sc, jj)])
                    att_state['sbk'] = sbk

                def A2(hg):
                    sbk = att_state['sbk']
                    hs = slice(hg * 4, hg * 4 + 4)
                    P.op("dve", (lambda hs: lambda e: e.tensor_reduce(out=mxv[:, hs], in_=sc[:], axis=AX.X, op=ALU.max))(hs), reads=[sc], writes=[(st8, "mx")])
                    P.op("dve", (lambda hs: lambda e: e.tensor_tensor(out=mxv[:, hs], in0=mxv[:, hs], in1=sink_bc[:, hs], op=ALU.max))(hs),
                         reads=[(st8, "mx"), sink_bc], writes=[(st8, "mx")])
                    P.op("dve", (lambda hs: lambda e: e.tensor_scalar(out=nmx[:, hs], in0=mxv[:, hs], scalar1=-1.0, scalar2=None, op0=ALU.mult))(hs),
                         reads=[(st8, "mx")], writes=[(st8, "nmx")])
                    for j in range(4):
                        hh = hg * 4 + j
                        P.op("act", (lambda j, hh: lambda e: e.activation(out=sc[:, j, :], in_=sc[:, j, :], func=AF.Exp, bias=nmx[:, hh:hh + 1],
                                                                          accum_out=rsum[:, hh:hh + 1]))(j, hh),
                             reads=[(sc, j // 2), (st8, "nmx")], writes=[(sc, j // 2), (st8, ("rs", hh))])

                def A3(hg):
                    tb = [psum(), psum()]
                    for j in range(4):
                        for blk in range(2):
                            q = j * 2 + blk
                            bk = tb[q // 4]
                            tr(bk, bk[:, (q % 4) * 128:(q % 4 + 1) * 128], sc[:, j, blk * 128:(blk + 1) * 128], [(sc, j // 2)])
                    P.op("act", (lambda bk: lambda e: e.copy(out=PT[:, 0:4, :], in_=bk[:, :].rearrange("p (a b) -> p a b", a=4)))(tb[0]), reads=[tb[0]], writes=[(PT, 0)])
                    P.op("dve", (lambda bk: lambda e: e.tensor_copy(out=PT[:, 4:8, :], in_=bk[:, :].rearrange("p (a b) -> p a b", a=4)))(tb[1]), reads=[tb[1]], writes=[(PT, 1)])
                    att_state['tb'] = tb

                def A4(hg):
                    tb = att_state['tb']
                    for j in range(4):
                        hh = hg * 4 + j
                        for blk in range(2):
                            q = j * 2 + blk
                            avb = av[prv] if blk == 0 else av[cur]
                            mm(batt, batt[:, hh * 64:(hh + 1) * 64], PT[:, q, :], avb[:, hg * 64:(hg + 1) * 64], blk == 0, blk == 1, [(PT, q // 4), avb])

                def ATTF():
                    P.op("dve", lambda e: e.tensor_tensor(out=es[:, :], in0=sink_bc[:], in1=mxv[:, :], op=ALU.subtract), reads=[sink_bc, (st8, "mx")], writes=[(st8, "es")])
                    P.op("act", lambda e: e.activation(out=es[:, :], in_=es[:, :], func=AF.Exp), reads=[(st8, "es")], writes=[(st8, "es")])
                    P.op("dve", lambda e: e.tensor_tensor(out=rden[:, :], in0=rsum[:, :], in1=es[:, :], op=ALU.add),
                         reads=[(st8, "es")] + [(st8, ("rs", hh)) for hh in range(8)], writes=[(st8, "rden")])
                    P.op("dve", lambda e: e.reciprocal(out=rden[:, :], in_=rden[:, :]), reads=[(st8, "rden")], writes=[(st8, "rden")])
                    P.op("dve", (lambda mix, batt: lambda e: e.tensor_tensor(out=mix[:, 0:512].rearrange("p (a b) -> p a b", a=8),
                                                                            in0=batt[:, :].rearrange("p (a b) -> p a b", a=8),
                                                                            in1=rden[:, :].unsqueeze(2).to_broadcast([128, 8, 64]), op=ALU.mult))(mix, batt),
                         reads=[batt, (st8, "rden")], writes=[(mix, 0)])

                if SUB < 1:
                    continue
                A1(0); G1(); A2(0); G2(); A3(0); G3(); A4(0); A1(1); G4(); A2(1); A3(1); A4(1); ATTF()
                if DEBUG:
                    P.dma("sp", (lambda mix, i: lambda e: e.dma_start(out=dbg["dbg_mix"][i * 128:(i + 1) * 128, :], in_=mix[:]))(mix, i), reads=[mix])
            P.barrier(skip_queue="pool")
            P.flush()
            maybe_stop(2)

        with ExitStack() as ph:
            w_out = sb(ph, "w_out", [128, 8, D], F32R)
            if STAGE >= 3:
                load_round(w_out, wout_r, D, "b")
            G1t = sb(ph, "G1t", [128, D])
            xt = [sb(ph, f"xta{j}", [128, D]) for j in range(2)]
            mT = [sb(ph, f"mT{j}", [128, 8, 128], F32R) for j in range(2)]
            tmp = sb(ph, "tmpa", [128, D])
            load_bc(G1t, 2)
            w_out_r = w_out.t[:]
            for i in range(NT if STAGE >= 3 else 0):
                mix = xstore[i]
                x_ = xt[i % 2]
                m_ = mT[i % 2]
                P.dma("sp", (lambda x_, i: lambda e: e.dma_start(out=x_[:], in_=d_xown[i * 128:(i + 1) * 128, :]))(x_, i), writes=[x_])
                ba, bb = psum(), psum()
                for j in range(8):
                    bk = ba if j < 4 else bb
                    tr(bk, bk[:, (j % 4) * 128:(j % 4 + 1) * 128], mix[:, j * 128:(j + 1) * 128], [mix])
                P.op("act", (lambda m_, ba: lambda e: e.copy(out=m_[:, 0:4, :], in_=ba[:, :].rearrange("p (a b) -> p a b", a=4)))(m_, ba), reads=[ba], writes=[(m_, 0)])
                P.op("dve", (lambda m_, bb: lambda e: e.tensor_copy(out=m_[:, 4:8, :], in_=bb[:, :].rearrange("p (a b) -> p a b", a=4)))(m_, bb), reads=[bb], writes=[(m_, 1)])
                for half in range(2):
                    bk = psum()
                    for kc in range(8):
                        mm(bk, bk[:, :], m_[:, kc, :], w_out_r[:, kc, half * 512:(half + 1) * 512], kc == 0, kc == 7, [(m_, 0 if kc < 4 else 1), (w_out, kc)])
                    hsl = slice(half * 512, (half + 1) * 512)
                    P.op("dve", (lambda bk, hsl: lambda e: e.tensor_tensor(out=tmp[:, hsl], in0=bk[:, :], in1=G1t[:, hsl], op=ALU.mult))(bk, hsl),
                         reads=[bk, G1t], writes=[(tmp, half)])
                    P.op("pool", (lambda mix, x_, hsl: lambda e: e.tensor_tensor(out=mix[:, hsl], in0=tmp[:, hsl], in1=x_[:, hsl], op=ALU.add))(mix, x_, hsl),
                         reads=[(tmp, half), x_], writes=[mix])
                if DEBUG:
                    P.dma("sp", (lambda mix, i: lambda e: e.dma_start(out=dbg["dbg_x1"][i * 128:(i + 1) * 128, :], in_=mix[:]))(mix, i), reads=[mix])
            P.barrier(skip_queue="pool")
            P.flush()
            maybe_stop(3)

        idxst = [sb(top, f"idx{i}", [128, 128], I32) for i in range(NT)]
        gatest = [sb(top, f"gate{i}", [128, 128]) for i in range(NT)]

        def norm2_h2(ph_bufs, x1, h2, A2t, B2t, ss, add_eng="pool"):
            P.op("act", lambda e: e.activation(out=h2[:], in_=x1[:], func=AF.Square, accum_out=ss[:, 0:1]), reads=[x1], writes=[h2, ss])
            rstd_from_ss(ss, ss[:, 0:1], D)
            P.op("dve", lambda e: e.scalar_tensor_tensor(out=h2[:], in0=x1[:], scalar=ss[:, 0:1], in1=A2t[:], op0=ALU.mult, op1=ALU.mult),
                 reads=[x1, ss, A2t], writes=[h2])
            P.op(add_eng, lambda e: e.tensor_tensor(out=h2[:], in0=h2[:], in1=B2t[:], op=ALU.add), reads=[h2, B2t], writes=[h2])

        with ExitStack() as ph:
            wqh = sb(ph, "wqh", [128, 8, 1024], F32R)
            skT = sb(ph, "skT", [128, 16, 128])
            A2t = sb(ph, "A2t", [128, D])
            B2t = sb(ph, "B2t", [128, D])
            h2L = [sb(ph, f"h2b{j}", [128, D]) for j in range(2)]
            h2T = sb(ph, "h2T", [128, 8, 128], F32R)
            qT = sb(ph, "qT", [128, 8, 128])
            scsL = [sb(ph, f"scs{j}", [128, 8, 128]) for j in range(2)]
            sc2 = sb(ph, "sc2", [128, 8, 128])
            topv = sb(ph, "topv", [128, 8, 16])
            topi = sb(ph, "topi", [128, 8, 16], U32)
            topif = sb(ph, "topif", [128, 8, 16])
            cand = sb(ph, "cand", [128, 4, 256])
            cand2 = sb(ph, "cand2", [128, 4, 256])
            bestv = sb(ph, "bestv", [128, 4, 16])
            pos = sb(ph, "pos", [128, 4, 16], U32)
            pab = sb(ph, "pab", [128, 2, 64], U32)
            pabf = sb(ph, "pabf", [128, 2, 64])
            ohs = [sb(ph, f"oh{j}", [128, 64, 16]) for j in range(2)]
            isel = sb(ph, "isel", [128, 2, 64])
            ef = sb(ph, "ef", [128, 64])
            gs = sb(ph, "gs", [128, 3, 4])
            eg = sb(ph, "eg", [128, 4, 16])
            ssb = sb(ph, "ssb", [128, 1])
            P.dma("sp", lambda e: e.dma_start(out=skT[:], in_=d_skT), writes=[skT])
            load_bc(A2t, 4)
            load_bc(B2t, 3)
            for ps_ in range(2 if STAGE >= 4 else 0):
                load_round(wqh, wq_r[:, :, ps_ * 1024:(ps_ + 1) * 1024], 1024, f"q{ps_}")
                wqh_r = wqh.t[:]
                def b1_norm(i):
                    norm2_h2(None, xstore[i], h2L[i % 2], A2t, B2t, ssb)

                def b1_front(i, scs):
                    h2 = h2L[i % 2]
                    ba, bb = psum(), psum()
                    for j in range(8):
                        bk = ba if j < 4 else bb
                        tr(bk, bk[:, (j % 4) * 128:(j % 4 + 1) * 128], h2[:, j * 128:(j + 1) * 128], [h2])
                    P.op("act", (lambda ba: lambda e: e.copy(out=h2T[:, 0:4, :], in_=ba[:, :].rearrange("p (a b) -> p a b", a=4)))(ba), reads=[ba], writes=[(h2T, 0)])
                    P.op("act", (lambda bb: lambda e: e.copy(out=h2T[:, 4:8, :], in_=bb[:, :].rearrange("p (a b) -> p a b", a=4)))(bb), reads=[bb], writes=[(h2T, 1)])
                    qb = [psum(), psum()]
                    for g in range(8):
                        bk = qb[g // 4]
                        for kc in range(8):
                            mm(bk, bk[:, (g % 4) * 128:(g % 4 + 1) * 128], wqh_r[:, kc, g * 128:(g + 1) * 128], h2T[:, kc, :], kc == 0, kc == 7,
                               [(wqh, kc), (h2T, 0 if kc < 4 else 1)])
                    P.op("act", (lambda bk: lambda e: e.copy(out=qT[:, 0:4, :], in_=bk[:, :].rearrange("p (a b) -> p a b", a=4)))(qb[0]), reads=[qb[0]], writes=[(qT, 0)])
                    P.op("act", (lambda bk: lambda e: e.copy(out=qT[:, 4:8, :], in_=bk[:, :].rearrange("p (a b) -> p a b", a=4)))(qb[1]), reads=[qb[1]], writes=[(qT, 1)])
                    sbk = [psum(), psum()]
                    for g in range(8):
                        bk = sbk[g // 4]
                        mm(bk, bk[:, (g % 4) * 128:(g % 4 + 1) * 128], qT[:, g, :], skT[:, ps_ * 8 + g, :], True, True, [(qT, g // 4), skT])
                    P.op("act", (lambda bk: lambda e: e.copy(out=scs[:, 0:4, :], in_=bk[:, :].rearrange("p (a b) -> p a b", a=4)))(sbk[0]), reads=[sbk[0]], writes=[(scs, 0)])
                    P.op("act", (lambda bk: lambda e: e.copy(out=scs[:, 4:8, :], in_=bk[:, :].rearrange("p (a b) -> p a b", a=4)))(sbk[1]), reads=[sbk[1]], writes=[(scs, 1)])
                def b1_topk(i, scs):
                    for g in range(8):
                        P.op("dve", (lambda g: lambda e: e.max(out=topv[:, g, 0:8], in_=scs[:, g, :]))(g), reads=[(scs, g // 4)], writes=[(topv, (g, 0))])
                    for g in range(8):
                        P.op("dve", (lambda g: lambda e: e.match_replace(out=sc2[:, g, :], in_to_replace=topv[:, g, 0:8], in_values=scs[:, g, :], imm_value=-1e30))(g),
                             reads=[(scs, g // 4), (topv, (g, 0))], writes=[(sc2, g)])
                    for g in range(8):
                        P.op("dve", (lambda g: lambda e: e.max(out=topv[:, g, 8:16], in_=sc2[:, g, :]))(g), reads=[(sc2, g)], writes=[(topv, (g, 1))])
                    for g in range(8):
                        P.op("dve", (lambda g: lambda e: e.max_index(out=topi[:, g, 0:8], in_max=topv[:, g, 0:8], in_values=scs[:, g, :]))(g),
                             reads=[(scs, g // 4), (topv, (g, 0))], writes=[(topi, (g, 0))])
                    for g in range(8):
                        P.op("dve", (lambda g: lambda e: e.max_index(out=topi[:, g, 8:16], in_max=topv[:, g, 8:16], in_values=scs[:, g, :]))(g),
                             reads=[(scs, g // 4), (topv, (g, 1))], writes=[(topi, (g, 1))])
                    P.op("dve", lambda e: e.tensor_copy(out=topif[:], in_=topi[:]), reads=[topi], writes=[topif])
                    tv4 = topv[:, :, :].rearrange("p (h two) a -> p h two a", two=2)
                    ti4 = topif[:, :, :].rearrange("p (h two) a -> p h two a", two=2)
                    P.op("dve", lambda e: e.tensor_tensor(out=cand[:, :, :].rearrange("p h (a b) -> p h a b", a=16),
                                                          in0=tv4[:, :, 0, :].unsqueeze(3).to_broadcast([128, 4, 16, 16]),
                                                          in1=tv4[:, :, 1, :].unsqueeze(2).to_broadcast([128, 4, 16, 16]), op=ALU.add),
                         reads=[topv], writes=[cand])
                    for hh in range(4):
                        P.op("dve", (lambda hh: lambda e: e.max(out=bestv[:, hh, 0:8], in_=cand[:, hh, :]))(hh), reads=[cand], writes=[(bestv, (hh, 0))])
                    for hh in range(4):
                        P.op("dve", (lambda hh: lambda e: e.match_replace(out=cand2[:, hh, :], in_to_replace=bestv[:, hh, 0:8], in_values=cand[:, hh, :], imm_value=-1e30))(hh),
                             reads=[cand, (bestv, (hh, 0))], writes=[(cand2, hh)])
                    for hh in range(4):
                        P.op("dve", (lambda hh: lambda e: e.max(out=bestv[:, hh, 8:16], in_=cand2[:, hh, :]))(hh), reads=[(cand2, hh)], writes=[(bestv, (hh, 1))])
                    for hh in range(4):
                        P.op("dve", (lambda hh: lambda e: e.max_index(out=pos[:, hh, 0:8], in_max=bestv[:, hh, 0:8], in_values=cand[:, hh, :]))(hh),
                             reads=[cand, (bestv, (hh, 0))], writes=[(pos, (hh, 0))])
                    for hh in range(4):
                        P.op("dve", (lambda hh: lambda e: e.max_index(out=pos[:, hh, 8:16], in_max=bestv[:, hh, 8:16], in_values=cand[:, hh, :]))(hh),
                             reads=[cand, (bestv, (hh, 1))], writes=[(pos, (hh, 1))])
                    posf = pos[:, :, :].rearrange("p h r -> p (h r)")
                    P.op("dve", lambda e: e.tensor_single_scalar(out=pab[:, 0, :], in_=posf, scalar=4, op=ALU.logical_shift_right), reads=[pos], writes=[(pab, 0)])
                    P.op("dve", lambda e: e.tensor_single_scalar(out=pab[:, 1, :], in_=posf, scalar=15, op=ALU.bitwise_and), reads=[pos], writes=[(pab, 1)])
                    P.op("dve", lambda e: e.tensor_copy(out=pabf[:, 0, :], in_=pab[:, 0, :]), reads=[(pab, 0)], writes=[(pabf, 0)])
                    P.op("dve", lambda e: e.tensor_copy(out=pabf[:, 1, :], in_=pab[:, 1, :]), reads=[(pab, 1)], writes=[(pabf, 1)])
                    for ab in range(2):
                        eng = "dve"
                        P.op(eng, (lambda ab: lambda e: e.tensor_tensor(out=ohs[ab][:], in0=pabf[:, ab, :].unsqueeze(2).to_broadcast([128, 64, 16]),
                                                                        in1=iota16[:, :].unsqueeze(1).to_broadcast([128, 64, 16]), op=ALU.is_equal))(ab),
                             reads=[(pabf, ab), iota16], writes=[ohs[ab]])
                    for ab in range(2):
                        eng = "dve" if ab == 0 else "pool"
                        P.op(eng, (lambda ab: lambda e: e.tensor_tensor(out=ohs[ab][:, :, :].rearrange("p (h r) a -> p h r a", h=4),
                                                                        in0=ohs[ab][:, :, :].rearrange("p (h r) a -> p h r a", h=4),
                                                                        in1=ti4[:, :, ab, :].unsqueeze(2).to_broadcast([128, 4, 16, 16]), op=ALU.mult))(ab),
                             reads=[ohs[ab], topif], writes=[ohs[ab]])
                    for ab in range(2):
                        P.op("dve", (lambda ab: lambda e: e.tensor_reduce(out=isel[:, ab, :], in_=ohs[ab][:], axis=AX.X, op=ALU.add))(ab), reads=[ohs[ab]], writes=[(isel, ab)])
                    P.op("dve", lambda e: e.scalar_tensor_tensor(out=ef[:], in0=isel[:, 0, :], scalar=128.0, in1=isel[:, 1, :], op0=ALU.mult, op1=ALU.add),
                         reads=[isel], writes=[ef])
                    csl = slice(ps_ * 64, ps_ * 64 + 64)
                    P.op("dve", (lambda i, csl: lambda e: e.tensor_copy(out=idxst[i][:, csl], in_=ef[:]))(i, csl), reads=[ef], writes=[(idxst[i], ps_)])
                    P.op("dve", lambda e: e.tensor_tensor(out=eg[:], in0=bestv[:], in1=bestv[:, :, 0:1].to_broadcast([128, 4, 16]), op=ALU.subtract),
                         reads=[bestv], writes=[eg])
                    P.op("act", lambda e: e.activation(out=eg[:], in_=eg[:], func=AF.Exp), reads=[eg], writes=[eg])
                    P.op("dve", lambda e: e.tensor_reduce(out=gs[:, 0, :], in_=eg[:], axis=AX.X, op=ALU.add), reads=[eg], writes=[gs])
                    P.op("dve", lambda e: e.reciprocal(out=gs[:, 1, :], in_=gs[:, 0, :]), reads=[gs], writes=[gs])
                    P.op("dve", (lambda i, csl: lambda e: e.tensor_tensor(out=gatest[i][:, csl].rearrange("p (h r) -> p h r", h=4), in0=eg[:],
                                                                          in1=gs[:, 1, :].unsqueeze(2).to_broadcast([128, 4, 16]), op=ALU.mult))(i, csl),
                         reads=[eg, gs], writes=[(gatest[i], ps_)])
                    if DEBUG and ps_ == 1:
                        P.dma("sp", (lambda i: lambda e: e.dma_start(out=dbg["dbg_idx"][i * 128:(i + 1) * 128, :], in_=idxst[i][:]))(i), reads=[idxst[i]])
                        P.dma("sp", (lambda i: lambda e: e.dma_start(out=dbg["dbg_gate"][i * 128:(i + 1) * 128, :], in_=gatest[i][:]))(i), reads=[gatest[i]])
                b1_norm(0)
                b1_front(0, scsL[0])
                b1_norm(1)
                for i in range(NT):
                    if i + 2 < NT:
                        b1_norm(i + 2)
                    if i + 1 < NT:
                        b1_front(i + 1, scsL[(i + 1) % 2])
                    b1_topk(i, scsL[i % 2])
                    issue_cast()
                    issue_cast()
            P.barrier(skip_queue="pool")
            P.flush()
            maybe_stop(4)

        with ExitStack() as ph:
            NB = 20
            while castn[0] < NCAST:
                issue_cast()
            A2t = sb(ph, "A2t2", [128, D])
            B2t = sb(ph, "B2t2", [128, D])
            G2t = sb(ph, "G2t", [128, D])
            FWt = sb(ph, "FWt", [128, D])
            h2 = [sb(ph, f"h2c{j}", [128, D]) for j in range(2)]
            ring = [sb(ph, f"rb{j}", [128, 2 * D], BF16) for j in range(NB)]
            hraw = [sb(ph, f"hraw{j}", [128, 128]) for j in range(2)]
            hgel = [sb(ph, f"hgel{j}", [128, 128]) for j in range(2)]
            Dr = [sb(ph, f"Dr{j}", [128, 128], BF16) for j in range(8)]
            ot = [sb(ph, f"ot{j}", [128, D]) for j in range(2)]
            ssb = sb(ph, "ssb2", [128, 2])
            load_bc(A2t, 4)
            load_bc(B2t, 3)
            load_bc(G2t, 5)
            P.dma("sp", lambda e: e.dma_start(out=FWt[:], in_=bcast_row(h_fnw, D)), writes=[FWt])
            cu = cv = cd = 0
            YB = [(banks[6], banks[7]), (banks[4], banks[5])]
            NTB = NT if STAGE >= 5 else 0

            def b2_final(i):
                x1 = xstore[i]
                h2_ = h2[i % 2]
                o_ = ot[i % 2]
                Y0, Y1 = YB[i % 2]
                P.op("dve", (lambda o_: lambda e: e.tensor_tensor(out=o_[:, 0:512], in0=Y0[:, :], in1=G2t[:, 0:512], op=ALU.mult))(o_), reads=[Y0, G2t], writes=[(o_, 0)])
                P.op("dve", (lambda o_: lambda e: e.tensor_tensor(out=o_[:, 512:1024], in0=Y1[:, :], in1=G2t[:, 512:1024], op=ALU.mult))(o_), reads=[Y1, G2t], writes=[(o_, 1)])
                P.op("dve", (lambda o_, x1: lambda e: e.tensor_tensor(out=o_[:], in0=o_[:], in1=x1[:], op=ALU.add))(o_, x1), reads=[o_, x1], writes=[o_])
                P.op("act", (lambda o_, h2_: lambda e: e.activation(out=h2_[:], in_=o_[:], func=AF.Square, accum_out=ssb[:, 1:2]))(o_, h2_), reads=[o_], writes=[h2_, (ssb, "f")])
                P.op("act", lambda e: e.activation(out=ssb[:, 1:2], in_=ssb[:, 1:2], func=AF.Ln, scale=1.0 / D, bias=EPS), reads=[(ssb, "f")], writes=[(ssb, "f")])
                P.op("act", lambda e: e.activation(out=ssb[:, 1:2], in_=ssb[:, 1:2], func=AF.Exp, scale=-0.5), reads=[(ssb, "f")], writes=[(ssb, "f")])
                P.op("dve", (lambda o_: lambda e: e.scalar_tensor_tensor(out=o_[:], in0=o_[:], scalar=ssb[:, 1:2], in1=FWt[:], op0=ALU.mult, op1=ALU.mult))(o_),
                     reads=[o_, (ssb, "f"), FWt], writes=[o_])
                P.dma("sp", (lambda o_, i: lambda e: e.dma_start(out=d_out[i * 128:(i + 1) * 128, :], in_=o_[:]))(o_, i), reads=[o_])

            if NTB:
                norm2_h2(None, xstore[0], h2[0], A2t, B2t, ssb, add_eng="dve")
            for i in range(NTB):
                x1 = xstore[i]
                h2_ = h2[i % 2]
                hr, hgb, o_ = hraw[i % 2], hgel[i % 2], ot[i % 2]
                Y0, Y1 = YB[i % 2]
                for k4 in range(32):
                    if k4 == 2 and i > 0:
                        b2_final(i - 1)
                    if k4 == 16 and i + 1 < NTB:
                        norm2_h2(None, xstore[i + 1], h2[(i + 1) % 2], A2t, B2t, ssb, add_eng="dve")
                    bufs = []
                    for r in range(4):
                        k = k4 * 4 + r
                        rb = ring[cu % NB]; cu += 1
                        bufs.append(rb)
                        P.dma("pool", (lambda rb, i, k: lambda e: e.indirect_dma_start(out=rb[:], out_offset=None, in_=d_t16,
                                                                                       in_offset=bass.IndirectOffsetOnAxis(ap=idxst[i][:, k:k + 1], axis=0)))(rb, i, k),
                              reads=[idxst[i], t16b], writes=[rb])
                        P.op("dve", (lambda rb, k, hr, h2_: lambda e: e.scalar_tensor_tensor(out=rb[:, 0:D], in0=rb[:, 0:D], scalar=1.0, in1=h2_[:], op0=ALU.mult, op1=ALU.mult,
                                                                                             accum_out=hr[:, k:k + 1]))(rb, k, hr, h2_),
                             reads=[rb, h2_], writes=[(rb, "u"), (hr, k)])
                    ksl = slice(k4 * 4, k4 * 4 + 4)
                    P.op("act", (lambda hgb, hr, ksl: lambda e: e.activation(out=hgb[:, ksl], in_=hr[:, ksl], func=AF.Gelu))(hgb, hr, ksl),
                         reads=[(hr, k4 * 4 + r) for r in range(4)], writes=[(hgb, k4)])
                    P.op("dve", (lambda hgb, i, ksl: lambda e: e.tensor_tensor(out=hgb[:, ksl], in0=hgb[:, ksl], in1=gatest[i][:, ksl], op=ALU.mult))(hgb, i, ksl),
                         reads=[(hgb, k4), gatest[i]], writes=[(hgb, k4)])
                    for r in range(4):
                        k = k4 * 4 + r
                        rb = bufs[r]
                        dk_ = Dr[cd % 8]; cd += 1
                        P.op("act", (lambda dk_, hgb, k: lambda e: e.activation(out=dk_[:], in_=ident[:], func=AF.Copy, scale=hgb[:, k:k + 1]))(dk_, hgb, k),
                             reads=[ident, (hgb, k4)], writes=[dk_])
                        mm(Y0, Y0[:, :], dk_[:, :], rb[:, D:D + 512], k == 0, k == 127, [dk_, rb])
                        mm(Y1, Y1[:, :], dk_[:, :], rb[:, D + 512:2 * D], k == 0, k == 127, [dk_, rb])
            if NTB:
                b2_final(NTB - 1)
            P.wait_all_dma("sp")
            P.flush()
    except _Stop:
        pass
    top.close()
    return nc


def _host_inputs(inp):
    f = lambda a: np.ascontiguousarray(np.asarray(a, dtype=np.float32))
    x = f(inp["x"])
    c = f(inp["c"])
    ar = np.arange(128)
    q = ar[:, None]
    j = np.arange(256)[None, :]
    maskr = np.where((j > q) & (j <= q + 128), 0.0, -30000.0).astype(np.float32)
    mask0_first = maskr.copy()
    mask0_first[:, :128] = -30000.0
    common = {
        "w_ada": f(inp["w_ada"][0]), "b_ada": f(inp["b_ada"][0]).reshape(1, -1), "norm1_w": f(inp["norm1_w"][0]).reshape(1, -1),
        "w_in": f(inp["w_in"][0]), "sinks": f(inp["attn_sinks"][0]).reshape(1, -1), "gate_up": f(inp["gla_gate_up"][0]),
        "gate_bias": f(inp["gla_gate_bias"][0]).reshape(1, -1), "gla_norm_w": f(inp["gla_norm_w"][0]).reshape(1, -1),
        "w_out": f(inp["w_out"][0]), "norm2_w": f(inp["norm2_w"][0]).reshape(1, -1), "peer_wq": f(inp["peer_wq"][0]),
        "skT": f(np.transpose(np.asarray(inp["peer_subkeys"][0], dtype=np.float32).reshape(16, 128, 128), (2, 0, 1))),
        "peer_uv": np.ascontiguousarray(np.concatenate([np.asarray(inp["peer_u"][0], np.float32), np.asarray(inp["peer_v"][0], np.float32)], axis=1)), "final_norm_w": f(inp["final_norm_w"]).reshape(1, -1),
        "ident": np.eye(128, dtype=np.float32),
        "tri": (ar[:, None] <= ar[None, :]).astype(np.float32),
        "triu": (ar[:, None] > ar[None, :]).astype(np.float32),
        "iota16": np.tile(np.arange(16, dtype=np.float32)[None, :], (128, 1)),
        "maskr": maskr,
    }
    maps = []
    for core in range(NCORES):
        b, s = core // 4, core % 4
        xpre = np.zeros((NPRE * 128, D), np.float32)
        nvalid = s * TOK
        if nvalid:
            xpre[NPRE * 128 - nvalid:] = x[b, :nvalid]
        pv = np.zeros((128, NPRE), np.float32)
        pv[:, NPRE - nvalid // 128:] = 1.0 if nvalid else 0.0
        m = dict(common)
        m["x_own"] = np.ascontiguousarray(x[b, s * TOK:(s + 1) * TOK])
        m["x_pre"] = xpre
        m["pvalid"] = pv
        m["mask0"] = mask0_first if s == 0 else maskr
        m["c_t"] = np.ascontiguousarray(c[b].reshape(8, 128).T)
        maps.append(m)
    return maps


def kernel(**inputs):
    maps = _host_inputs(inputs)
    nc = build_program()
    res = run_bass_kernel_spmd(nc, maps[:RUN_CORES], core_ids=list(range(RUN_CORES)))
    out = np.zeros((2, 8192, D), np.float32)
    for core in range(RUN_CORES):
        b, s = core // 4, core % 4
        out[b, s * TOK:(s + 1) * TOK] = np.asarray(res.results[core]["out"], dtype=np.float32)
    if DEBUG:
        _dbg_out["res"] = res.results
    return out
```

```python
from contextlib import ExitStack
import numpy as np
import concourse.bass as bass
import concourse.mybir as mybir
from concourse.bass_utils import run_bass_kernel_spmd

F32 = mybir.dt.float32
F32R = mybir.dt.float32r
I32 = mybir.dt.int32
BF16 = mybir.dt.bfloat16
U32 = mybir.dt.uint32
AF = mybir.ActivationFunctionType
ALU = mybir.AluOpType
AX = mybir.AxisListType

NCORES = 8
TOK = 2048
NT = TOK // 128
NPRE = 48
D = 1024
EPS = 1e-6
DEBUG = False
STAGE = 99
SUB = 99
XSRC_PRE = False
NT_RUN = 16
NPRE_RUN = 48
RUN_CORES = NCORES


class _Stop(Exception):
    pass
_dbg_out = {}

STREAMS = ("pe", "act", "dve", "pool", "sp")
DMA_RING = {"sp": 8, "act": 4, "pool": 16}


class Buf:
    def __init__(self, name, t, psum=False):
        self.name = name
        self.t = t
        self.regs = {}
        self.psum = psum

    def __getitem__(self, idx):
        return self.t[idx]


class Prog:
    def __init__(self, nc):
        self.nc = nc
        self.ops = {s: [] for s in STREAMS}
        self.seq = {s: 0 for s in STREAMS}
        self.dcount = {q: 0 for q in DMA_RING}
        self.waited = {s: {} for s in STREAMS}
        self.last_dma_tok = {}

    def _need(self, stream, tok, hazard, waits):
        if tok is None:
            return
        semkey, val, pstream, kind = tok
        if kind == "c" and pstream == stream:
            if stream == "pe":
                return
        w = self.waited[stream]
        if w.get(semkey, -1) >= val:
            return
        w[semkey] = val
        waits.append((semkey, val))

    def _collect(self, stream, reads, writes):
        waits = []
        for b, k in reads:
            regs = b.regs
            if k is None:
                for e in regs.values():
                    self._need(stream, e["w"], "RAW", waits)
            elif k in regs:
                self._need(stream, regs[k]["w"], "RAW", waits)
            elif None in regs:
                self._need(stream, regs[None]["w"], "RAW", waits)
        for b, k in writes:
            regs = b.regs
            if k is None:
                for e in regs.values():
                    self._need(stream, e["w"], "WAW", waits)
                    for t in e["r"].values():
                        self._need(stream, t, "WAR", waits)
            else:
                for kk in (k, None):
                    if kk in regs:
                        e = regs[kk]
                        self._need(stream, e["w"], "WAW", waits)
                        for t in e["r"].values():
                            self._need(stream, t, "WAR", waits)
        return waits

    def _record(self, tok, reads, writes):
        for b, k in reads:
            regs = b.regs
            if k is None:
                e = regs.setdefault(None, {"w": None, "r": {}})
            else:
                if k not in regs:
                    regs[k] = {"w": regs[None]["w"] if None in regs else None, "r": {}}
                e = regs[k]
            e["r"][tok[0]] = tok
        for b, k in writes:
            if k is None:
                b.regs = {None: {"w": tok, "r": {}}}
            else:
                b.regs[k] = {"w": tok, "r": {}}

    @staticmethod
    def _norm(lst):
        return [x if isinstance(x, tuple) else (x, None) for x in (lst or [])]

    def op(self, stream, emit, reads=None, writes=None):
        reads = self._norm(reads)
        writes = self._norm(writes)
        writes = writes + [(b, None) for b, k in reads if b.psum]
        reads = [(b, k) for b, k in reads if not b.psum]
        waits = self._collect(stream, reads, writes)
        self.seq[stream] += 1
        tok = (("c", stream), self.seq[stream], stream, "c")
        self._record(tok, reads, writes)
        self.ops[stream].append((waits, emit, (("c", stream), 1)))
        return tok

    def dma(self, queue, emit, reads=None, writes=None):
        reads = self._norm(reads)
        writes = self._norm(writes)
        waits = self._collect(queue, reads, writes)
        i = self.dcount[queue]
        self.dcount[queue] += 1
        K = DMA_RING[queue]
        slot = i % K
        semkey = ("d", queue, slot)
        if i >= K:
            self._need(queue, (semkey, 16 * (i // K), queue, "d"), "WAW", waits)
        val = 16 * (i // K + 1)
        tok = (semkey, val, queue, "d")
        self.last_dma_tok[semkey] = val
        self._record(tok, reads, writes)
        self.ops[queue].append((waits, emit, (semkey, 16)))
        return tok

    def barrier(self, skip_queue=None):
        for s in STREAMS:
            waits = []
            for s2 in STREAMS:
                if s2 != s and self.seq[s2] > 0:
                    self._need(s, (("c", s2), self.seq[s2], s2, "c"), "RAW", waits)
            for semkey, val in self.last_dma_tok.items():
                if skip_queue is not None and semkey[1] == skip_queue:
                    continue
                self._need(s, (semkey, val, semkey[1], "d"), "RAW", waits)
            if waits:
                self.ops[s].append((waits, None, None))

    def wait_all_dma(self, stream="sp"):
        waits = []
        for semkey, val in self.last_dma_tok.items():
            self._need(stream, (semkey, val, semkey[1], "d"), "RAW", waits)
        if waits:
            self.ops[stream].append((waits, None, None))

    def setup(self, stack):
        nc = self.nc
        self.sems = {}
        for s in STREAMS:
            self.sems[("c", s)] = stack.enter_context(nc.semaphore(f"c_{s}"))
        for q, K in DMA_RING.items():
            for j in range(K):
                self.sems[("d", q, j)] = stack.enter_context(nc.semaphore(f"d_{q}_{j}"))

    def flush(self):
        nc = self.nc
        sems = self.sems
        ops = self.ops
        self.ops = {s: [] for s in STREAMS}

        def run(stream):
            def f(eng):
                for waits, emit, inc in ops[stream]:
                    for semkey, val in waits:
                        eng.wait_ge(sems[semkey], val)
                    if emit is not None:
                        emit(eng).then_inc(sems[inc[0]], inc[1])
            return f

        with nc.Block() as block:
            block.tensor(run("pe"))
            block.scalar(run("act"))
            block.vector(run("dve"))
            block.gpsimd(run("pool"))
            block.sync(run("sp"))


def build_program():
    nc = bass.Bass("TRN2", target_bir_lowering=False)

    def din(name, shape, dt=F32):
        return nc.dram_tensor(name, shape, dt, kind="ExternalInput")

    d_xown = din("x_own", [TOK, D]).ap()
    d_xpre = din("x_pre", [NPRE * 128, D]).ap()
    d_pvalid = din("pvalid", [128, NPRE]).ap()
    d_mask0 = din("mask0", [128, 256]).ap()
    d_maskr = din("maskr", [128, 256]).ap()
    d_ct = din("c_t", [128, 8]).ap()
    h_wada = din("w_ada", [D, 6 * D])
    h_bada = din("b_ada", [1, 6 * D])
    h_n1w = din("norm1_w", [1, D])
    h_win = din("w_in", [D, 2320])
    h_sinks = din("sinks", [1, 8])
    d_gup = din("gate_up", [16, 256]).ap()
    d_gbias = din("gate_bias", [1, 256]).ap()
    h_gnw = din("gla_norm_w", [1, 128])
    h_wout = din("w_out", [D, D])
    h_n2w = din("norm2_w", [1, D])
    h_wq = din("peer_wq", [D, 2048])
    d_skT = din("skT", [128, 16, 128]).ap()
    d_puv = din("peer_uv", [16384, 2 * D]).ap()
    d_t16 = nc.dram_tensor("tab16", [16384, 2 * D], BF16, kind="Internal").ap()
    h_fnw = din("final_norm_w", [1, D])
    d_ident = din("ident", [128, 128]).ap()
    d_tri = din("tri", [128, 128]).ap()
    d_triu = din("triu", [128, 128]).ap()
    d_iota = din("iota16", [128, 16]).ap()
    d_out = nc.dram_tensor("out", [TOK, D], F32, kind="ExternalOutput").ap()
    d_bc = nc.dram_tensor("bc_scratch", [6, 128, D], F32, kind="Internal").ap()
    dbg = {}
    if DEBUG:
        for nm, shp, dt in (("dbg_mix", [TOK, D], F32), ("dbg_x1", [TOK, D], F32),
                            ("dbg_idx", [TOK, 128], I32), ("dbg_gate", [TOK, 128], F32),
                            ("dbg_bc", [6, 128, D], F32), ("dbg_hid", [TOK, 128], F32), ("dbg_hraw", [TOK, 128], F32), ("dbg_y", [TOK, D], F32)):
            dbg[nm] = nc.dram_tensor(nm, shp, dt, kind="ExternalOutput").ap()

    def bcast_row(handle, n, off=0):
        return bass.AP(handle, off, [[0, 128], [1, n]])

    wada_r = h_wada.ap().rearrange("(kc p) n -> p kc n", p=128)
    win_r = h_win.ap().rearrange("(kc p) n -> p kc n", p=128)
    wout_r = h_wout.ap().rearrange("(kc p) n -> p kc n", p=128)
    wq_r = h_wq.ap().rearrange("(kc p) n -> p kc n", p=128)

    P = Prog(nc)
    top = ExitStack()
    P.setup(top)

    def maybe_stop(k):
        return

    try:

        def sb(st, name, shape, dt=F32):
            return Buf(name, st.enter_context(nc.sbuf_tensor("s_" + name, shape, dt)))

        banks = [Buf(f"ps{j}", top.enter_context(nc.psum_tensor(f"ps{j}", [128, 512], F32)), psum=True) for j in range(8)]
        pcnt = [0]

        def psum():
            b = banks[pcnt[0] % 6]
            pcnt[0] += 1
            return b

        ident = sb(top, "ident", [128, 128])
        tri = sb(top, "tri", [128, 128])
        triu = sb(top, "triu", [128, 128])
        iota16 = sb(top, "iota16", [128, 16])
        ones = sb(top, "ones", [128, 128])
        for t_, d_ in ((ident, d_ident), (tri, d_tri), (triu, d_triu), (iota16, d_iota)):
            P.dma("sp", (lambda t_, d_: lambda e: e.dma_start(out=t_[:], in_=d_))(t_, d_), writes=[t_])
        P.op("dve", lambda e: e.memset(ones[:], 1.0), writes=[ones])

        t16b = Buf("tab16", None)
        NCAST = 64
        castn = [0]

        def issue_cast():
            j = castn[0]
            if j >= NCAST:
                return
            castn[0] += 1
            rows = 16384 // NCAST
            P.dma("pool", lambda e: e.dma_start(out=d_t16[j * rows:(j + 1) * rows, :], in_=d_puv[j * rows:(j + 1) * rows, :]), writes=[(t16b, j)])


        def mm(out_b, out_ap, lhsT, rhs, start, stop, reads):
            return P.op("pe", lambda e: e.matmul(out_ap, lhsT=lhsT, rhs=rhs, start=start, stop=stop),
                        reads=reads, writes=[out_b])

        def tr(out_b, out_ap, in_ap, reads):
            return P.op("pe", lambda e: e.transpose(out_ap, in_ap, ident[:]), reads=reads + [ident], writes=[out_b])

        def rstd_from_ss(ss_b, ss_ap, n, keys=None):
            P.op("act", lambda e: e.activation(out=ss_ap, in_=ss_ap, func=AF.Ln, scale=1.0 / n, bias=EPS), reads=[ss_b], writes=[ss_b])
            P.op("act", lambda e: e.activation(out=ss_ap, in_=ss_ap, func=AF.Exp, scale=-0.5), reads=[ss_b], writes=[ss_b])

        def load_round(dst, src_r, n, tag, perm_aq=False):
            with ExitStack() as stg:
                half = n // 2
                stage = [sb(stg, f"wst_{tag}{j}", [128, half]) for j in range(2)]
                q = 0
                for kc in range(8):
                    for hf in range(2):
                        st_ = stage[q % 2]
                        P.dma("sp", (lambda st_, kc, hf: lambda e: e.dma_start(out=st_[:], in_=src_r[:, kc, hf * half:(hf + 1) * half]))(st_, kc, hf), writes=[st_])
                        if perm_aq and hf == 0:
                            P.op("act", (lambda st_, kc: lambda e: e.copy(
                                out=dst[:, kc, 0:512].rearrange("p (c two d) -> p two c d", c=4, two=2, d=64),
                                in_=st_[:, 0:512].rearrange("p (two c d) -> p two c d", two=2, c=4, d=64)))(st_, kc), reads=[st_], writes=[(dst, kc)])
                            P.op("dve", (lambda st_, kc: lambda e: e.tensor_copy(out=dst[:, kc, 512:half], in_=st_[:, 512:half]))(st_, kc),
                                 reads=[st_], writes=[(dst, kc)])
                        elif q % 2 == 0:
                            P.op("act", (lambda st_, kc, hf: lambda e: e.copy(out=dst[:, kc, hf * half:(hf + 1) * half], in_=st_[:]))(st_, kc, hf),
                                 reads=[st_], writes=[(dst, kc)])
                        else:
                            P.op("dve", (lambda st_, kc, hf: lambda e: e.tensor_copy(out=dst[:, kc, hf * half:(hf + 1) * half], in_=st_[:]))(st_, kc, hf),
                                 reads=[st_], writes=[(dst, kc)])
                        q += 1
                P.barrier(skip_queue="pool")
                P.flush()

        with ExitStack() as ph:
            wada = [sb(ph, f"wada{j}", [128, 8, 512]) for j in range(4)]
            stage = [sb(ph, f"stg{j}", [128, 512]) for j in range(2)]
            n1w = sb(ph, "n1w", [128, D])
            n2w = sb(ph, "n2w", [128, D])
            bada = sb(ph, "bada", [1, 6 * D])
            ct = sb(ph, "ct", [128, 8])
            silc = sb(ph, "silc", [128, 8])
            silc_bc = sb(ph, "silc_bc", [128, 8, 128], F32R)
            wadar = [sb(ph, f"wadar{j}", [128, 8, 512], F32R) for j in range(2)]
            P.dma("sp", lambda e: e.dma_start(out=n1w[:], in_=bcast_row(h_n1w, D)), writes=[n1w])
            P.dma("sp", lambda e: e.dma_start(out=n2w[:], in_=bcast_row(h_n2w, D)), writes=[n2w])
            P.dma("sp", lambda e: e.dma_start(out=bada[:], in_=h_bada.ap()), writes=[bada])
            P.dma("sp", lambda e: e.dma_start(out=ct[:], in_=d_ct), writes=[ct])
            P.op("act", lambda e: e.activation(out=silc[:], in_=ct[:], func=AF.Silu), reads=[ct], writes=[silc])
            for kc in range(8):
                P.op("act", (lambda kc: lambda e: e.copy(out=silc_bc[:, kc, :], in_=silc[:, kc:kc + 1].to_broadcast([128, 128])))(kc),
                     reads=[silc], writes=[(silc_bc, kc)])
            for n in range(12):
                wb = wada[n % 4]
                wr = wadar[n % 2]
                P.dma("sp", (lambda wb, n: lambda e: e.dma_start(out=wb[:], in_=wada_r[:, :, n * 512:(n + 1) * 512]))(wb, n), writes=[wb])
                for kc in range(8):
                    if kc % 2 == 0:
                        P.op("act", (lambda wr, wb, kc: lambda e: e.copy(out=wr[:, kc, :], in_=wb[:, kc, :]))(wr, wb, kc), reads=[wb], writes=[(wr, kc)])
                    else:
                        P.op("dve", (lambda wr, wb, kc: lambda e: e.tensor_copy(out=wr[:, kc, :], in_=wb[:, kc, :]))(wr, wb, kc), reads=[wb], writes=[(wr, kc)])
                bk = psum()
                for kc in range(8):
                    mm(bk, bk[:, :], silc_bc[:, kc, :], wr[:, kc, :], kc == 0, False, [(silc_bc, kc), (wr, kc)])
                mm(bk, bk[:, :], ones[0:1, :], bada[0:1, n * 512:(n + 1) * 512], False, True, [ones, bada])
                sec, half = n // 2, n % 2
                sg = stage[n % 2]
                if sec in (1, 4):
                    nw = n1w if sec == 1 else n2w
                    P.op("dve", (lambda sg, bk, nw, half: lambda e: e.scalar_tensor_tensor(
                        out=sg[:], in0=bk[:, :], scalar=1.0, in1=nw[:, half * 512:(half + 1) * 512], op0=ALU.add, op1=ALU.mult))(sg, bk, nw, half),
                        reads=[bk, nw], writes=[sg])
                else:
                    P.op("act", (lambda sg, bk: lambda e: e.copy(out=sg[:], in_=bk[:, :]))(sg, bk), reads=[bk], writes=[sg])
                P.dma("pool", (lambda sg, sec, half: lambda e: e.dma_start(out=d_bc[sec, :, half * 512:(half + 1) * 512], in_=sg[:]))(sg, sec, half), reads=[sg])
            P.barrier()
            P.flush()
            maybe_stop(0)
        bcbuf = Buf("bc_dram", None)

        def load_bc(t, sec):
            P.dma("sp", lambda e: e.dma_start(out=t[:], in_=d_bc[sec]), writes=[t])

        if DEBUG:
            with ExitStack() as ph:
                tt = sb(ph, "dbgt", [128, D])
                for sec in range(6):
                    load_bc(tt, sec)
                    P.dma("sp", (lambda sec: lambda e: e.dma_start(out=dbg["dbg_bc"][sec], in_=tt[:]))(sec), reads=[tt])
                P.barrier(skip_queue="pool")
                P.flush()

        xstore = [sb(top, f"xs{i}", [128, D]) for i in range(NT)]

        with ExitStack() as ph:
            w_in = sb(ph, "w_in", [128, 8, 2320], F32R)
            load_round(w_in, win_r, 2320, "a", perm_aq=True)
            A1t = sb(ph, "A1t", [128, D])
            B1t = sb(ph, "B1t", [128, D])
            gup = sb(ph, "gup", [128, 256])
            gbias = sb(ph, "gbias", [1, 256])
            sink_bc = sb(ph, "sink_bc", [128, 8])
            gnw_bc = sb(ph, "gnw_bc", [128, 128])
            pvalid = sb(ph, "pvalid", [128, NPRE])
            maskr = sb(ph, "maskr", [128, 256])
            mask0 = sb(ph, "mask0", [128, 256])
            xt0 = sb(ph, "xt0", [128, D])
            h = sb(ph, "h", [128, D])
            hT = sb(ph, "hT", [128, 8, 128], F32R)
            hTb = sb(ph, "hTb", [128, 8, 128], F32R)
            decay2 = sb(ph, "decay2", [128, 2])
            ss1b = sb(ph, "ss1b", [128, 1])
            aqTp = sb(ph, "aqTp", [128, 8, 128], F32R)
            akT = [sb(ph, f"akT{j}", [128, 128], F32R) for j in range(2)]
            av = [sb(ph, f"av{j}", [128, 128], F32R) for j in range(2)]
            gqTp = sb(ph, "gqTp", [128, 4, 128])
            raw = sb(ph, "raw", [128, 4, 128])
            glrTL = [sb(ph, f"glrT{j}", [128, 128]) for j in range(2)]
            k_tmL = [sb(ph, f"k_tm{j}", [128, 256]) for j in range(2)]
            v_tmL = [sb(ph, f"v_tm{j}", [128, 512]) for j in range(2)]
            sg_ = sb(ph, "sgg", [128, 512])
            e1 = sb(ph, "e1", [128, 256])
            sp_ = sb(ph, "sp", [128, 256])
            eq = sb(ph, "eq", [128, 2, 128])
            ek = sb(ph, "ek", [128, 2, 128])
            kt = sb(ph, "kt", [128, 2, 128])
            er = sb(ph, "er", [128, 256])
            khat = sb(ph, "khat", [128, 256])
            attTm = sb(ph, "attTm", [128, 4, 128])
            S_sb = sb(ph, "S_sb", [128, 2, 128])
            decay = sb(ph, "decay", [128, 2])
            sc = sb(ph, "sc", [128, 4, 256])
            PT = sb(ph, "PT", [128, 8, 128], F32R)
            st8 = sb(ph, "st8", [128, 5, 8])
            ss1 = sb(ph, "ss1", [128, 1])
            ss4 = sb(ph, "ss4", [128, 4])
            gtmp = sb(ph, "gtmp", [128, 512])
            xt = [xt0, xstore[15]]
            h_alt = xstore[14]
            e1L = [e1, Buf("e1v", gtmp.t[:, 0:256])]
            spL = [sp_, Buf("spv", gtmp.t[:, 256:512])]
            erL = [er, Buf("erv", attTm.t[:, 0:2, :].rearrange("p a b -> p (a b)"))]
            khatL = [khat, Buf("khatv", attTm.t[:, 2:4, :].rearrange("p a b -> p (a b)"))]
            k_tm3 = k_tmL + [Buf("ktm3v", raw.t[:, 0:2, :].rearrange("p a b -> p (a b)"))]
            v_tm3 = v_tmL + [Buf("vtm3v", sg_.t[:, :])]

            load_bc(A1t, 1)
            load_bc(B1t, 0)
            P.op("dve", lambda e: e.memset(gup[:], 0.0), writes=[gup])
            P.dma("sp", lambda e: e.dma_start(out=gup[112:128, :], in_=d_gup), writes=[gup])
            P.dma("sp", lambda e: e.dma_start(out=gbias[:], in_=d_gbias), writes=[gbias])
            P.dma("sp", lambda e: e.dma_start(out=sink_bc[:], in_=bcast_row(h_sinks, 8)), writes=[sink_bc])
            P.dma("sp", lambda e: e.dma_start(out=gnw_bc[:], in_=bcast_row(h_gnw, 128)), writes=[gnw_bc])
            P.dma("sp", lambda e: e.dma_start(out=pvalid[:], in_=d_pvalid), writes=[pvalid])
            P.dma("sp", lambda e: e.dma_start(out=maskr[:], in_=d_maskr), writes=[maskr])
            P.dma("sp", lambda e: e.dma_start(out=mask0[:], in_=d_mask0), writes=[mask0])
            w_in_r = w_in.t[:]
            w_in_f = w_in.t[:].bitcast(F32)
            zsrc = xstore[13]
            P.op("dve", lambda e: e.memset(zsrc[:], 0.0), writes=[zsrc])
            P.op("act", lambda e: e.copy(out=aqTp[:], in_=zsrc[:].rearrange("p (a b) -> p a b", a=8)), reads=[zsrc], writes=[aqTp])
            P.op("dve", lambda e: e.memset(gqTp[:], 0.0), writes=[gqTp])
            P.op("dve", lambda e: e.memset(S_sb[:], 0.0), writes=[S_sb])
            P.op("act", lambda e: e.copy(out=akT[1][:], in_=zsrc[:, 0:128]), reads=[zsrc], writes=[akT[1]])
            P.op("act", lambda e: e.copy(out=av[1][:], in_=zsrc[:, 0:128]), reads=[zsrc], writes=[av[1]])

            cnt = [0]

            def tile_front(xsrc, own, pj):
                x_ = xt0
                cnt[0] += 1
                P.dma("sp", lambda e: e.dma_start(out=x_[:], in_=xsrc), writes=[x_])
                P.op("act", lambda e: e.activation(out=h[:], in_=x_[:], func=AF.Square, accum_out=ss1[:, 0:1]), reads=[x_], writes=[h, ss1])
                rstd_from_ss(ss1, ss1[:, 0:1], D)
                P.op("dve", lambda e: e.scalar_tensor_tensor(out=h[:], in0=x_[:], scalar=ss1[:, 0:1], in1=A1t[:], op0=ALU.mult, op1=ALU.mult),
                     reads=[x_, ss1, A1t], writes=[h])
                P.op("dve", lambda e: e.tensor_tensor(out=h[:], in0=h[:], in1=B1t[:], op=ALU.add), reads=[h, B1t], writes=[h])
                ba, bb = psum(), psum()
                for j in range(8):
                    bk = ba if j < 4 else bb
                    tr(bk, bk[:, (j % 4) * 128:(j % 4 + 1) * 128], h[:, j * 128:(j + 1) * 128], [h])
                P.op("act", lambda e: e.copy(out=hT[:, 0:4, :], in_=ba[:, :].rearrange("p (a b) -> p a b", a=4)), reads=[ba], writes=[(hT, 0)])
                P.op("dve", lambda e: e.tensor_copy(out=hT[:, 4:8, :], in_=bb[:, :].rearrange("p (a b) -> p a b", a=4)), reads=[bb], writes=[(hT, 1)])
                hTk = lambda kc: (hT, 0 if kc < 4 else 1)
                last_pre = (not own) and pj == NPRE - 1
                cur = (cnt[0] - 1) % 2 if own else 1
                return x_, last_pre

            def proj_tm(cols, bk, ncol, hT=hT):
                for kc in range(8):
                    mm(bk, bk[:, 0:ncol], hT[:, kc, :], w_in_r[:, kc, cols[0]:cols[1]], kc == 0, kc == 7,
                       [(hT, 0 if kc < 4 else 1), (w_in, kc)])

            def proj_fm(lhs_fn, bk, col0, m=128, f32=False, hT=hT):
                for kc in range(8):
                    lhsT = lhs_fn(kc)
                    rhs = hT[:, kc, :]
                    if f32:
                        rhs = rhs.bitcast(F32)
                    mm(bk, bk[0:m, col0:col0 + 128], lhsT, rhs, kc == 0, kc == 7, [(hT, 0 if kc < 4 else 1), (w_in, kc)])

            def gla_common(own, pj, bs):
                glrT, k_tm = glrTL[bs], k_tmL[bs]
                bz = psum()
                mm(bz, bz[:, 0:256], glrT[:, :], gup[:, :], True, False, [glrT, gup])
                mm(bz, bz[:, 0:256], ones[0:1, :], gbias[0:1, :], False, True, [ones, gbias])
                P.op("act", lambda e: e.activation(out=e1[:], in_=bz[:, 0:256], func=AF.Exp, scale=-1.0), reads=[bz], writes=[e1])
                P.op("act", lambda e: e.activation(out=sp_[:], in_=e1[:], func=AF.Ln, bias=1.0), reads=[e1], writes=[sp_])
                br = psum()
                mm(br, br[:, 0:256], triu[:, :], sp_[:, :], True, True, [triu, sp_])
                bt = psum()
                if own:
                    for hc in range(2):
                        mm(bt, bt[:, hc * 128:(hc + 1) * 128], sp_[:, hc * 128:(hc + 1) * 128], tri[:, :], True, True, [sp_, tri])
                for hc in range(2):
                    mm(bt, bt[:, 256 + 2 * hc:258 + 2 * hc], sp_[:, hc * 128:(hc + 1) * 128], ones[:, 0:2], True, True, [sp_, ones])
                P.op("act", lambda e: e.activation(out=er[:], in_=br[:, 0:256], func=AF.Exp, scale=-1.0 / 16), reads=[br], writes=[er])
                P.op("dve", lambda e: e.tensor_tensor(out=khat[:], in0=k_tm[:], in1=er[:], op=ALU.mult), reads=[k_tm, er], writes=[khat])
                P.op("act", lambda e: e.activation(out=decay[:], in_=bt[:, 256:260].rearrange("p (a b) -> p a b", a=2)[:, :, 0],
                                                   func=AF.Exp, scale=-1.0 / 16), reads=[bt], writes=[decay])
                if own:
                    btv = bt[:, 0:256].rearrange("p (a b) -> p a b", a=2)
                    P.op("act", lambda e: e.activation(out=eq[:], in_=btv, func=AF.Exp, scale=-1.0 / 16), reads=[bt], writes=[eq])
                    P.op("act", lambda e: e.activation(out=ek[:], in_=btv, func=AF.Exp, scale=1.0 / 16), reads=[bt], writes=[ek])
                    P.op("dve", lambda e: e.scalar_tensor_tensor(out=gqTp[0:64, 0:4:2, :], in0=eq[0:64, :, :], scalar=0.125, in1=raw[0:64, 0:2, :],
                                                                 op0=ALU.mult, op1=ALU.mult), reads=[eq, raw], writes=[(gqTp, 0)])
                    P.op("dve", lambda e: e.scalar_tensor_tensor(out=gqTp[64:128, 1:4:2, :], in0=eq[64:128, :, :], scalar=0.125, in1=raw[64:128, 0:2, :],
                                                                 op0=ALU.mult, op1=ALU.mult), reads=[eq, raw], writes=[(gqTp, 1)])
                    P.op("dve", lambda e: e.tensor_tensor(out=kt[:], in0=ek[:], in1=raw[:, 2:4, :], op=ALU.mult), reads=[ek, raw], writes=[kt])

            def state_update(bs):
                v_tm = v_tmL[bs]
                bd = psum()
                for hc in range(2):
                    mm(bd, bd[:, hc * 256:(hc + 1) * 256], khat[:, hc * 128:(hc + 1) * 128], v_tm[:, hc * 256:(hc + 1) * 256], True, True, [khat, v_tm])
                for hh in range(4):
                    p0, c = (hh % 2) * 64, hh // 2
                    P.op("dve", (lambda p0, c, hh: lambda e: e.scalar_tensor_tensor(
                        out=S_sb[p0:p0 + 64, c, :], in0=S_sb[p0:p0 + 64, c, :], scalar=decay[p0:p0 + 64, c:c + 1],
                        in1=bd[p0:p0 + 64, c * 256 + (hh % 2) * 128:c * 256 + (hh % 2) * 128 + 128], op0=ALU.mult, op1=ALU.add))(p0, c, hh),
                        reads=[(S_sb, hh), decay, bd], writes=[(S_sb, hh)])

            HH, HT, SS1, DEC = [h, h_alt], [hT, hTb], [ss1, ss1b], [decay, decay2]

            def pS1a(pj):
                b_ = pj % 2
                x_, h_, hT_, ss_ = xt[b_], HH[b_], HT[b_], SS1[b_]
                P.dma("sp", lambda e: e.dma_start(out=x_[:], in_=d_xpre[pj * 128:(pj + 1) * 128, :]), writes=[x_])
                P.op("act", lambda e: e.activation(out=h_[:], in_=x_[:], func=AF.Square, accum_out=ss_[:, 0:1]), reads=[x_], writes=[h_, ss_])
                rstd_from_ss(ss_, ss_[:, 0:1], D)
                P.op("dve", lambda e: e.scalar_tensor_tensor(out=h_[:], in0=x_[:], scalar=ss_[:, 0:1], in1=A1t[:], op0=ALU.mult, op1=ALU.mult),
                     reads=[x_, ss_, A1t], writes=[h_])
                P.op("dve", lambda e: e.tensor_tensor(out=h_[:], in0=h_[:], in1=B1t[:], op=ALU.add), reads=[h_, B1t], writes=[h_])

            def pS1b(pj):
                b_ = pj % 2
                h_, hT_ = HH[b_], HT[b_]
                ba, bb = banks[6], banks[7]
                for j in range(8):
                    bk = ba if j < 4 else bb
                    tr(bk, bk[:, (j % 4) * 128:(j % 4 + 1) * 128], h_[:, j * 128:(j + 1) * 128], [h_])
                P.op("act", lambda e: e.copy(out=hT_[:, 0:4, :], in_=ba[:, :].rearrange("p (a b) -> p a b", a=4)), reads=[ba], writes=[(hT_, 0)])
                P.op("dve", lambda e: e.tensor_copy(out=hT_[:, 4:8, :], in_=bb[:, :].rearrange("p (a b) -> p a b", a=4)), reads=[bb], writes=[(hT_, 1)])

            def pS2(pj):
                b_ = pj % 2
                hT_ = HT[b_]
                glrT, k_tm, v_tm = glrTL[b_], k_tm3[pj % 3], v_tm3[pj % 3]
                b2 = banks[3]
                proj_tm((1024, 1536), b2, 512, hT=hT_)
                b3 = banks[4]
                proj_tm((1536, 1792), b3, 256, hT=hT_)
                P.op("act", lambda e: e.copy(out=k_tm[:], in_=b2[:, 0:256]), reads=[b2], writes=[k_tm])
                P.op("act", lambda e: e.activation(out=v_tm[:, 0:256], in_=b2[:, 256:512], func=AF.Copy, scale=pvalid[:, pj:pj + 1]),
                     reads=[b2, pvalid], writes=[(v_tm, 0)])
                P.op("act", lambda e: e.activation(out=v_tm[:, 256:512], in_=b3[:, 0:256], func=AF.Copy, scale=pvalid[:, pj:pj + 1]),
                     reads=[b3, pvalid], writes=[(v_tm, 1)])
                bf = banks[5]
                proj_fm(lambda kc: w_in_r[:, kc, 2192:2320], bf, 0, hT=hT_)
                P.op("act", lambda e: e.copy(out=glrT[:], in_=bf[:, 0:128]), reads=[bf], writes=[glrT])
                if pj == NPRE - 1:
                    b1 = banks[3]
                    proj_tm((640, 768), b1, 128, hT=hT_)
                    P.op("act", lambda e: e.copy(out=av[1][:], in_=b1[:, 0:128]), reads=[b1], writes=[av[1]])
                    bg = banks[4]
                    proj_fm(lambda kc: w_in_r[:, kc, 512:640], bg, 0, hT=hT_)
                    P.op("act", lambda e: e.copy(out=akT[1][:], in_=bg[:, 0:128]), reads=[bg], writes=[akT[1]])

            def pS3(pj):
                b_ = pj % 2
                glrT = glrTL[b_]
                e1_, spb, er_, dec_ = e1L[b_], spL[b_], erL[b_], DEC[b_]
                bz = banks[1]
                mm(bz, bz[:, 0:256], glrT[:, :], gup[:, :], True, False, [glrT, gup])
                mm(bz, bz[:, 0:256], ones[0:1, :], gbias[0:1, :], False, True, [ones, gbias])
                P.op("act", lambda e: e.activation(out=e1_[:], in_=bz[:, 0:256], func=AF.Exp, scale=-1.0), reads=[bz], writes=[e1_])
                P.op("act", lambda e: e.activation(out=spb[:], in_=e1_[:], func=AF.Ln, bias=1.0), reads=[e1_], writes=[spb])
                br = banks[2]
                mm(br, br[:, 0:256], triu[:, :], spb[:], True, True, [triu, spb])
                for hc in range(2):
                    mm(br, br[:, 256 + 2 * hc:258 + 2 * hc], spb[:, hc * 128:(hc + 1) * 128], ones[:, 0:2], True, True, [spb, ones])
                P.op("act", lambda e: e.activation(out=er_[:], in_=br[:, 0:256], func=AF.Exp, scale=-1.0 / 16), reads=[br], writes=[er_])
                P.op("act", lambda e: e.activation(out=dec_[:], in_=br[:, 256:260].rearrange("p (a b) -> p a b", a=2)[:, :, 0],
                                                   func=AF.Exp, scale=-1.0 / 16), reads=[br], writes=[dec_])

            def pS4(pj):
                b_ = pj % 2
                k_tm, v_tm = k_tm3[pj % 3], v_tm3[pj % 3]
                er_, kh_, dec_ = erL[b_], khatL[b_], DEC[b_]
                P.op("dve", lambda e: e.tensor_tensor(out=kh_[:], in0=k_tm[:], in1=er_[:], op=ALU.mult), reads=[k_tm, er_], writes=[kh_])
                bd = banks[0]
                for hc in range(2):
                    mm(bd, bd[:, hc * 256:(hc + 1) * 256], kh_[:, hc * 128:(hc + 1) * 128], v_tm[:, hc * 256:(hc + 1) * 256], True, True, [kh_, v_tm])
                for hh in range(4):
                    p0, c = (hh % 2) * 64, hh // 2
                    P.op("dve", (lambda p0, c, hh: lambda e: e.scalar_tensor_tensor(
                        out=S_sb[p0:p0 + 64, c, :], in0=S_sb[p0:p0 + 64, c, :], scalar=dec_[p0:p0 + 64, c:c + 1],
                        in1=bd[p0:p0 + 64, c * 256 + (hh % 2) * 128:c * 256 + (hh % 2) * 128 + 128], op0=ALU.mult, op1=ALU.add))(p0, c, hh),
                        reads=[(S_sb, hh), dec_, bd], writes=[(S_sb, hh)])

            pjs = list(range(NPRE - NPRE_RUN, NPRE)) if STAGE >= 1 else []
            stages = [(pS1a, 0), (pS2, 1), (pS1b, 0), (pS3, 2), (pS4, 3)]
            for t_ in range(len(pjs) + 3):
                for fn_, lag in stages:
                    jj = t_ - lag
                    if 0 <= jj < len(pjs):
                        fn_(pjs[jj])
            P.barrier(skip_queue="pool")
            if STAGE == 1:
                P.barrier(skip_queue="pool")
                maybe_stop(1)
            for i in range(NT_RUN if STAGE >= 2 else 0):
                cur, prv = i % 2, (i + 1) % 2
                glrT, k_tm, v_tm = glrTL[i % 2], k_tmL[i % 2], v_tmL[i % 2]
                x_, _ = tile_front((d_xpre if XSRC_PRE else d_xown)[i * 128:(i + 1) * 128, :], True, None)
                mix = xstore[i]
                if SUB < -3:
                    continue
                b1 = psum(); proj_tm((640, 768), b1, 128)
                b2 = psum(); proj_tm((1024, 1536), b2, 512)
                b3 = psum(); proj_tm((1536, 2048), b3, 512)
                b4 = psum(); proj_tm((2048, 2304), b4, 256)
                P.op("act", (lambda cur, b1: lambda e: e.copy(out=av[cur][:], in_=b1[:, 0:128]))(cur, b1), reads=[b1], writes=[av[cur]])
                P.op("act", (lambda b2, k_tm: lambda e: e.copy(out=k_tm[:], in_=b2[:, 0:256]))(b2, k_tm), reads=[b2], writes=[k_tm])
                P.op("dve", (lambda b2, v_tm: lambda e: e.tensor_copy(out=v_tm[:, 0:256], in_=b2[:, 256:512]))(b2, v_tm), reads=[b2], writes=[(v_tm, 0)])
                P.op("dve", (lambda b3, v_tm: lambda e: e.tensor_copy(out=v_tm[:, 256:512], in_=b3[:, 0:256]))(b3, v_tm), reads=[b3], writes=[(v_tm, 1)])
                P.op("act", (lambda b3: lambda e: e.activation(out=sg_[:, 0:256], in_=b3[:, 256:512], func=AF.Silu))(b3), reads=[b3], writes=[(sg_, 0)])
                P.op("act", (lambda b4: lambda e: e.activation(out=sg_[:, 256:512], in_=b4[:, 0:256], func=AF.Silu))(b4), reads=[b4], writes=[(sg_, 1)])
                if SUB < -2:
                    continue
                f1 = psum()
                for c in range(4):
                    proj_fm((lambda c: lambda kc: w_in_r[:, kc, c * 128:(c + 1) * 128])(c), f1, c * 128)
                f2 = psum()
                proj_fm(lambda kc: w_in_r[:, kc, 512:640], f2, 0)
                proj_fm(lambda kc: w_in_r[:, kc, 768:896], f2, 128)
                proj_fm(lambda kc: w_in_r[:, kc, 896:1024], f2, 256)
                f3 = psum()
                proj_fm(lambda kc: w_in_r[:, kc, 1024:1152], f3, 0)
                proj_fm(lambda kc: w_in_r[:, kc, 1152:1280], f3, 128)
                f4 = psum()
                proj_fm(lambda kc: w_in_r[:, kc, 2192:2320], f4, 0)
                if SUB < -1:
                    continue
                P.op("act", (lambda f1: lambda e: e.activation(out=aqTp[0:64, 0:4, :], in_=f1[0:64, :].rearrange("p (a b) -> p a b", a=4), func=AF.Copy, scale=0.125))(f1),
                     reads=[f1], writes=[(aqTp, 0)])
                P.op("act", (lambda f1: lambda e: e.activation(out=aqTp[64:128, 4:8, :], in_=f1[64:128, :].rearrange("p (a b) -> p a b", a=4), func=AF.Copy, scale=0.125))(f1),
                     reads=[f1], writes=[(aqTp, 1)])
                P.op("dve", (lambda cur, f2: lambda e: e.tensor_copy(out=akT[cur][:], in_=f2[:, 0:128]))(cur, f2), reads=[f2], writes=[akT[cur]])
                P.op("dve", (lambda f2: lambda e: e.tensor_copy(out=raw[:, 0:2, :], in_=f2[:, 128:384].rearrange("p (a b) -> p a b", a=2)))(f2), reads=[f2], writes=[(raw, 0)])
                P.op("act", (lambda f3: lambda e: e.copy(out=raw[:, 2:4, :], in_=f3[:, 0:256].rearrange("p (a b) -> p a b", a=2)))(f3), reads=[f3], writes=[(raw, 1)])
                P.op("dve", (lambda f4, glrT: lambda e: e.tensor_copy(out=glrT[:], in_=f4[:, 0:128]))(f4, glrT), reads=[f4], writes=[glrT])

                msk = mask0 if i == 0 else maskr
                batt = banks[6]
                mxv, nmx, rsum, es, rden = (st8[:, j, :] for j in range(5))
                def G1():
                    gla_common(True, None, i % 2)

                def G2():
                    bat = psum()
                    for hh in range(4):
                        mm(bat, bat[:, hh * 128:(hh + 1) * 128], kt[:, hh // 2, :], gqTp[:, hh, :], True, True, [kt, gqTp])
                    P.op("dve", (lambda bat: lambda e: e.tensor_tensor(out=attTm[:], in0=bat[:, :].rearrange("p (a b) -> p a b", a=4),
                                                                      in1=tri[:, :].unsqueeze(1).to_broadcast([128, 4, 128]), op=ALU.mult))(bat),
                         reads=[bat, tri], writes=[attTm])

                def G3():
                    bo = banks[7]
                    for hh in range(4):
                        mm(bo, bo[:, hh * 128:(hh + 1) * 128], gqTp[:, hh, :], S_sb[:, hh // 2, :], True, False, [gqTp, S_sb])
                        mm(bo, bo[:, hh * 128:(hh + 1) * 128], attTm[:, hh, :], v_tm[:, hh * 128:(hh + 1) * 128], False, True, [attTm, v_tm])
                    state_update(i % 2)

                def G4():
                    bo = banks[7]
                    for hh in range(4):
                        P.op("act", (lambda hh, bo: lambda e: e.activation(out=gtmp[:, hh * 128:(hh + 1) * 128], in_=bo[:, hh * 128:(hh + 1) * 128],
                                                                          func=AF.Square, accum_out=ss4[:, hh:hh + 1]))(hh, bo), reads=[bo], writes=[(gtmp, hh), (ss4, hh)])
                    rstd_from_ss(ss4, ss4[:, :], 128)
                    for hh in range(4):
                        P.op("dve", (lambda hh, bo: lambda e: e.scalar_tensor_tensor(out=gtmp[:, hh * 128:(hh + 1) * 128], in0=bo[:, hh * 128:(hh + 1) * 128],
                                                                                    scalar=ss4[:, hh:hh + 1], in1=gnw_bc[:], op0=ALU.mult, op1=ALU.mult))(hh, bo),
                             reads=[bo, ss4, gnw_bc], writes=[(gtmp, hh)])
                    P.op("pool", (lambda mix: lambda e: e.tensor_tensor(out=mix[:, 512:1024], in0=gtmp[:], in1=sg_[:], op=ALU.mult))(mix),
                         reads=[gtmp, sg_], writes=[(mix, 1)])


                att_state = {}

                def A1(hg):
                    sbk = [psum(), psum()]
                    for j in range(4):
                        hh = hg * 4 + j
                        bk = sbk[j // 2]
                        c0 = (j % 2) * 256
                        mm(bk, bk[:, c0:c0 + 128], aqTp[:, hh, :], akT[prv][:, :], True, True, [aqTp, akT[prv]])
                        mm(bk, bk[:, c0 + 128:c0 + 256], aqTp[:, hh, :], akT[cur][:, :], True, True, [aqTp, akT[cur]])
                    for jj in range(2):
                        P.op("dve", (lambda jj, bk, msk: lambda e: e.tensor_tensor(out=sc[:, 2 * jj:2 * jj + 2, :], in0=bk[:, :].rearrange("p (a b) -> p a b", a=2),
                                                                                   in1=msk[:, :].unsqueeze(1).to_broadcast([128, 2, 256]), op=ALU.add))(jj, sbk[jj], msk),
                             reads=[sbk[jj], msk], writes=[(sc, jj)])
                    att_state['sbk'] = sbk

                def A2(hg):
                    sbk = att_state['sbk']
                    hs = slice(hg * 4, hg * 4 + 4)
                    P.op("dve", (lambda hs: lambda e: e.tensor_reduce(out=mxv[:, hs], in_=sc[:], axis=AX.X, op=ALU.max))(hs), reads=[sc], writes=[(st8, "mx")])
                    P.op("dve", (lambda hs: lambda e: e.tensor_tensor(out=mxv[:, hs], in0=mxv[:, hs], in1=sink_bc[:, hs], op=ALU.max))(hs),
                         reads=[(st8, "mx"), sink_bc], writes=[(st8, "mx")])
                    P.op("dve", (lambda hs: lambda e: e.tensor_scalar(out=nmx[:, hs], in0=mxv[:, hs], scalar1=-1.0, scalar2=None, op0=ALU.mult))(hs),
                         reads=[(st8, "mx")], writes=[(st8, "nmx")])
                    for j in range(4):
                        hh = hg * 4 + j
                        P.op("act", (lambda j, hh: lambda e: e.activation(out=sc[:, j, :], in_=sc[:, j, :], func=AF.Exp, bias=nmx[:, hh:hh + 1],
                                                                          accum_out=rsum[:, hh:hh + 1]))(j, hh),
                             reads=[(sc, j // 2), (st8, "nmx")], writes=[(sc, j // 2), (st8, ("rs", hh))])

                def A3(hg):
                    tb = [psum(), psum()]
                    for j in range(4):
                        for blk in range(2):
                            q = j * 2 + blk
                            bk = tb[q // 4]
                            tr(bk, bk[:, (q % 4) * 128:(q % 4 + 1) * 128], sc[:, j, blk * 128:(blk + 1) * 128], [(sc, j // 2)])
                    P.op("act", (lambda bk: lambda e: e.copy(out=PT[:, 0:4, :], in_=bk[:, :].rearrange("p (a b) -> p a b", a=4)))(tb[0]), reads=[tb[0]], writes=[(PT, 0)])
                    P.op("dve", (lambda bk: lambda e: e.tensor_copy(out=PT[:, 4:8, :], in_=bk[:, :].rearrange("p (a b) -> p a b", a=4)))(tb[1]), reads=[tb[1]], writes=[(PT, 1)])
                    att_state['tb'] = tb

                def A4(hg):
                    tb = att_state['tb']
                    for j in range(4):
                        hh = hg * 4 + j
                        for blk in range(2):
                            q = j * 2 + blk
                            avb = av[prv] if blk == 0 else av[cur]
                            mm(batt, batt[:, hh * 64:(hh + 1) * 64], PT[:, q, :], avb[:, hg * 64:(hg + 1) * 64], blk == 0, blk == 1, [(PT, q // 4), avb])

                def ATTF():
                    P.op("dve", lambda e: e.tensor_tensor(out=es[:, :], in0=sink_bc[:], in1=mxv[:, :], op=ALU.subtract), reads=[sink_bc, (st8, "mx")], writes=[(st8, "es")])
                    P.op("act", lambda e: e.activation(out=es[:, :], in_=es[:, :], func=AF.Exp), reads=[(st8, "es")], writes=[(st8, "es")])
                    P.op("dve", lambda e: e.tensor_tensor(out=rden[:, :], in0=rsum[:, :], in1=es[:, :], op=ALU.add),
                         reads=[(st8, "es")] + [(st8, ("rs", hh)) for hh in range(8)], writes=[(st8, "rden")])
                    P.op("dve", lambda e: e.reciprocal(out=rden[:, :], in_=rden[:, :]), reads=[(st8, "rden")], writes=[(st8, "rden")])
                    P.op("dve", (lambda mix, batt: lambda e: e.tensor_tensor(out=mix[:, 0:512].rearrange("p (a b) -> p a b", a=8),
                                                                            in0=batt[:, :].rearrange("p (a b) -> p a b", a=8),
                                                                            in1=rden[:, :].unsqueeze(2).to_broadcast([128, 8, 64]), op=ALU.mult))(mix, batt),
                         reads=[batt, (st8, "rden")], writes=[(mix, 0)])

                if SUB < 1:
                    continue
                A1(0); G1(); A2(0); G2(); A3(0); G3(); A4(0); A1(1); G4(); A2(1); A3(1); A4(1); ATTF()
                if DEBUG:
                    P.dma("sp", (lambda mix, i: lambda e: e.dma_start(out=dbg["dbg_mix"][i * 128:(i + 1) * 128, :], in_=mix[:]))(mix, i), reads=[mix])
            P.barrier(skip_queue="pool")
            P.flush()
            maybe_stop(2)

        with ExitStack() as ph:
            w_out = sb(ph, "w_out", [128, 8, D], F32R)
            if STAGE >= 3:
                load_round(w_out, wout_r, D, "b")
            G1t = sb(ph, "G1t", [128, D])
            xt = [sb(ph, f"xta{j}", [128, D]) for j in range(2)]
            mT = [sb(ph, f"mT{j}", [128, 8, 128], F32R) for j in range(2)]
            tmp = sb(ph, "tmpa", [128, D])
            load_bc(G1t, 2)
            w_out_r = w_out.t[:]
            for i in range(NT if STAGE >= 3 else 0):
                mix = xstore[i]
                x_ = xt[i % 2]
                m_ = mT[i % 2]
                P.dma("sp", (lambda x_, i: lambda e: e.dma_start(out=x_[:], in_=d_xown[i * 128:(i + 1) * 128, :]))(x_, i), writes=[x_])
                ba, bb = psum(), psum()
                for j in range(8):
                    bk = ba if j < 4 else bb
                    tr(bk, bk[:, (j % 4) * 128:(j % 4 + 1) * 128], mix[:, j * 128:(j + 1) * 128], [mix])
                P.op("act", (lambda m_, ba: lambda e: e.copy(out=m_[:, 0:4, :], in_=ba[:, :].rearrange("p (a b) -> p a b", a=4)))(m_, ba), reads=[ba], writes=[(m_, 0)])
                P.op("dve", (lambda m_, bb: lambda e: e.tensor_copy(out=m_[:, 4:8, :], in_=bb[:, :].rearrange("p (a b) -> p a b", a=4)))(m_, bb), reads=[bb], writes=[(m_, 1)])
                for half in range(2):
                    bk = psum()
                    for kc in range(8):
                        mm(bk, bk[:, :], m_[:, kc, :], w_out_r[:, kc, half * 512:(half + 1) * 512], kc == 0, kc == 7, [(m_, 0 if kc < 4 else 1), (w_out, kc)])
                    hsl = slice(half * 512, (half + 1) * 512)
                    P.op("dve", (lambda bk, hsl: lambda e: e.tensor_tensor(out=tmp[:, hsl], in0=bk[:, :], in1=G1t[:, hsl], op=ALU.mult))(bk, hsl),
                         reads=[bk, G1t], writes=[(tmp, half)])
                    P.op("pool", (lambda mix, x_, hsl: lambda e: e.tensor_tensor(out=mix[:, hsl], in0=tmp[:, hsl], in1=x_[:, hsl], op=ALU.add))(mix, x_, hsl),
                         reads=[(tmp, half), x_], writes=[mix])
                if DEBUG:
                    P.dma("sp", (lambda mix, i: lambda e: e.dma_start(out=dbg["dbg_x1"][i * 128:(i + 1) * 128, :], in_=mix[:]))(mix, i), reads=[mix])
            P.barrier(skip_queue="pool")
            P.flush()
            maybe_stop(3)

        idxst = [sb(top, f"idx{i}", [128, 128], I32) for i in range(NT)]
        gatest = [sb(top, f"gate{i}", [128, 128]) for i in range(NT)]

        def norm2_h2(ph_bufs, x1, h2, A2t, B2t, ss, add_eng="pool"):
            P.op("act", lambda e: e.activation(out=h2[:], in_=x1[:], func=AF.Square, accum_out=ss[:, 0:1]), reads=[x1], writes=[h2, ss])
            rstd_from_ss(ss, ss[:, 0:1], D)
            P.op("dve", lambda e: e.scalar_tensor_tensor(out=h2[:], in0=x1[:], scalar=ss[:, 0:1], in1=A2t[:], op0=ALU.mult, op1=ALU.mult),
                 reads=[x1, ss, A2t], writes=[h2])
            P.op(add_eng, lambda e: e.tensor_tensor(out=h2[:], in0=h2[:], in1=B2t[:], op=ALU.add), reads=[h2, B2t], writes=[h2])

        with ExitStack() as ph:
            wqhL = [sb(ph, f"wqh{j}", [128, 8, 1024], F32R) for j in range(2)]
            wqst = [sb(ph, f"wqst{j}", [128, 256]) for j in range(2)]
            wqsn = [0, 0]

            def wq_stage_chunk(half=1):
                q = wqsn[half]
                if q >= 32:
                    return
                wqsn[half] += 1
                kc, hf = q // 4, q % 4
                st_ = wqst[q % 2]
                c0 = half * 1024 + hf * 256
                P.dma("sp", lambda e: e.dma_start(out=st_[:], in_=wq_r[:, kc, c0:c0 + 256]), writes=[st_])
                if q % 2 == 0:
                    P.op("act", lambda e: e.copy(out=wqhL[half][:, kc, hf * 256:(hf + 1) * 256], in_=st_[:]), reads=[st_], writes=[(wqhL[half], kc)])
                else:
                    P.op("dve", lambda e: e.tensor_copy(out=wqhL[half][:, kc, hf * 256:(hf + 1) * 256], in_=st_[:]), reads=[st_], writes=[(wqhL[half], kc)])
            skT = sb(ph, "skT", [128, 16, 128])
            A2t = sb(ph, "A2t", [128, D])
            B2t = sb(ph, "B2t", [128, D])
            h2L = [sb(ph, f"h2b{j}", [128, D]) for j in range(2)]
            h2T = sb(ph, "h2T", [128, 8, 128], F32R)
            qT = sb(ph, "qT", [128, 8, 128])
            scsL = [sb(ph, f"scs{j}", [128, 8, 128]) for j in range(2)]
            sc2 = sb(ph, "sc2", [128, 8, 128])
            topv = sb(ph, "topv", [128, 8, 16])
            topi = sb(ph, "topi", [128, 8, 16], U32)
            topif = sb(ph, "topif", [128, 8, 16])
            cand = sb(ph, "cand", [128, 4, 256])
            cand2 = Buf("cand2v", sc2.t[:].rearrange("p (h two) n -> p h (two n)", two=2))
            bestv = sb(ph, "bestv", [128, 4, 16])
            pos = sb(ph, "pos", [128, 4, 16], U32)
            pab = sb(ph, "pab", [128, 2, 64], U32)
            pabf = sb(ph, "pabf", [128, 2, 64])
            ohs = [Buf("oh0v", cand.t[:].rearrange("p h (r a) -> p (h r) a", a=16)), sb(ph, "oh1", [128, 64, 16])]
            isel = sb(ph, "isel", [128, 2, 64])
            ef = sb(ph, "ef", [128, 64])
            gs = sb(ph, "gs", [128, 3, 4])
            eg = sb(ph, "eg", [128, 4, 16])
            ssb = sb(ph, "ssb", [128, 1])
            P.dma("sp", lambda e: e.dma_start(out=skT[:], in_=d_skT), writes=[skT])
            load_bc(A2t, 4)
            load_bc(B2t, 3)
            for ps_ in range(2 if STAGE >= 4 else 0):
                wqh = wqhL[ps_]
                while wqsn[ps_] < 32:
                    wq_stage_chunk(ps_)
                wqh_r = wqh.t[:]
                def b1_norm(i):
                    norm2_h2(None, xstore[i], h2L[i % 2], A2t, B2t, ssb)

                def b1_front(i, scs):
                    h2 = h2L[i % 2]
                    ba, bb = psum(), psum()
                    for j in range(8):
                        bk = ba if j < 4 else bb
                        tr(bk, bk[:, (j % 4) * 128:(j % 4 + 1) * 128], h2[:, j * 128:(j + 1) * 128], [h2])
                    P.op("act", (lambda ba: lambda e: e.copy(out=h2T[:, 0:4, :], in_=ba[:, :].rearrange("p (a b) -> p a b", a=4)))(ba), reads=[ba], writes=[(h2T, 0)])
                    P.op("act", (lambda bb: lambda e: e.copy(out=h2T[:, 4:8, :], in_=bb[:, :].rearrange("p (a b) -> p a b", a=4)))(bb), reads=[bb], writes=[(h2T, 1)])
                    qb = [psum(), psum()]
                    for g in range(8):
                        bk = qb[g // 4]
                        for kc in range(8):
                            mm(bk, bk[:, (g % 4) * 128:(g % 4 + 1) * 128], wqh_r[:, kc, g * 128:(g + 1) * 128], h2T[:, kc, :], kc == 0, kc == 7,
                               [(wqh, kc), (h2T, 0 if kc < 4 else 1)])
                    P.op("act", (lambda bk: lambda e: e.copy(out=qT[:, 0:4, :], in_=bk[:, :].rearrange("p (a b) -> p a b", a=4)))(qb[0]), reads=[qb[0]], writes=[(qT, 0)])
                    P.op("act", (lambda bk: lambda e: e.copy(out=qT[:, 4:8, :], in_=bk[:, :].rearrange("p (a b) -> p a b", a=4)))(qb[1]), reads=[qb[1]], writes=[(qT, 1)])
                    sbk = [psum(), psum()]
                    for g in range(8):
                        bk = sbk[g // 4]
                        mm(bk, bk[:, (g % 4) * 128:(g % 4 + 1) * 128], qT[:, g, :], skT[:, ps_ * 8 + g, :], True, True, [(qT, g // 4), skT])
                    P.op("act", (lambda bk: lambda e: e.copy(out=scs[:, 0:4, :], in_=bk[:, :].rearrange("p (a b) -> p a b", a=4)))(sbk[0]), reads=[sbk[0]], writes=[(scs, 0)])
                    P.op("act", (lambda bk: lambda e: e.copy(out=scs[:, 4:8, :], in_=bk[:, :].rearrange("p (a b) -> p a b", a=4)))(sbk[1]), reads=[sbk[1]], writes=[(scs, 1)])
                def b1_topk(i, scs):
                    for g in range(8):
                        P.op("dve", (lambda g: lambda e: e.max(out=topv[:, g, 0:8], in_=scs[:, g, :]))(g), reads=[(scs, g // 4)], writes=[(topv, (g, 0))])
                    for g in range(8):
                        P.op("dve", (lambda g: lambda e: e.match_replace(out=sc2[:, g, :], in_to_replace=topv[:, g, 0:8], in_values=scs[:, g, :], imm_value=-1e30))(g),
                             reads=[(scs, g // 4), (topv, (g, 0))], writes=[(sc2, g)])
                    for g in range(8):
                        P.op("dve", (lambda g: lambda e: e.max(out=topv[:, g, 8:16], in_=sc2[:, g, :]))(g), reads=[(sc2, g)], writes=[(topv, (g, 1))])
                    for g in range(8):
                        P.op("dve", (lambda g: lambda e: e.max_index(out=topi[:, g, 0:8], in_max=topv[:, g, 0:8], in_values=scs[:, g, :]))(g),
                             reads=[(scs, g // 4), (topv, (g, 0))], writes=[(topi, (g, 0))])
                    for g in range(8):
                        P.op("dve", (lambda g: lambda e: e.max_index(out=topi[:, g, 8:16], in_max=topv[:, g, 8:16], in_values=scs[:, g, :]))(g),
                             reads=[(scs, g // 4), (topv, (g, 1))], writes=[(topi, (g, 1))])
                    P.op("dve", lambda e: e.tensor_copy(out=topif[:], in_=topi[:]), reads=[topi], writes=[topif])
                    tv4 = topv[:, :, :].rearrange("p (h two) a -> p h two a", two=2)
                    ti4 = topif[:, :, :].rearrange("p (h two) a -> p h two a", two=2)
                    P.op("dve", lambda e: e.tensor_tensor(out=cand[:, :, :].rearrange("p h (a b) -> p h a b", a=16),
                                                          in0=tv4[:, :, 0, :].unsqueeze(3).to_broadcast([128, 4, 16, 16]),
                                                          in1=tv4[:, :, 1, :].unsqueeze(2).to_broadcast([128, 4, 16, 16]), op=ALU.add),
                         reads=[topv], writes=[cand])
                    for hh in range(4):
                        P.op("dve", (lambda hh: lambda e: e.max(out=bestv[:, hh, 0:8], in_=cand[:, hh, :]))(hh), reads=[cand], writes=[(bestv, (hh, 0))])
                    for hh in range(4):
                        P.op("dve", (lambda hh: lambda e: e.match_replace(out=cand2[:, hh, :], in_to_replace=bestv[:, hh, 0:8], in_values=cand[:, hh, :], imm_value=-1e30))(hh),
                             reads=[cand, (bestv, (hh, 0))], writes=[(cand2, hh)])
                    for hh in range(4):
                        P.op("dve", (lambda hh: lambda e: e.max(out=bestv[:, hh, 8:16], in_=cand2[:, hh, :]))(hh), reads=[(cand2, hh)], writes=[(bestv, (hh, 1))])
                    for hh in range(4):
                        P.op("dve", (lambda hh: lambda e: e.max_index(out=pos[:, hh, 0:8], in_max=bestv[:, hh, 0:8], in_values=cand[:, hh, :]))(hh),
                             reads=[cand, (bestv, (hh, 0))], writes=[(pos, (hh, 0))])
                    for hh in range(4):
                        P.op("dve", (lambda hh: lambda e: e.max_index(out=pos[:, hh, 8:16], in_max=bestv[:, hh, 8:16], in_values=cand[:, hh, :]))(hh),
                             reads=[cand, (bestv, (hh, 1))], writes=[(pos, (hh, 1))])
                    posf = pos[:, :, :].rearrange("p h r -> p (h r)")
                    P.op("dve", lambda e: e.tensor_single_scalar(out=pab[:, 0, :], in_=posf, scalar=4, op=ALU.logical_shift_right), reads=[pos], writes=[(pab, 0)])
                    P.op("dve", lambda e: e.tensor_single_scalar(out=pab[:, 1, :], in_=posf, scalar=15, op=ALU.bitwise_and), reads=[pos], writes=[(pab, 1)])
                    P.op("dve", lambda e: e.tensor_copy(out=pabf[:, 0, :], in_=pab[:, 0, :]), reads=[(pab, 0)], writes=[(pabf, 0)])
                    P.op("dve", lambda e: e.tensor_copy(out=pabf[:, 1, :], in_=pab[:, 1, :]), reads=[(pab, 1)], writes=[(pabf, 1)])
                    for ab in range(2):
                        eng = "dve"
                        P.op(eng, (lambda ab: lambda e: e.tensor_tensor(out=ohs[ab][:], in0=pabf[:, ab, :].unsqueeze(2).to_broadcast([128, 64, 16]),
                                                                        in1=iota16[:, :].unsqueeze(1).to_broadcast([128, 64, 16]), op=ALU.is_equal))(ab),
                             reads=[(pabf, ab), iota16], writes=[ohs[ab]])
                    for ab in range(2):
                        eng = "dve" if ab == 0 else "pool"
                        P.op(eng, (lambda ab: lambda e: e.tensor_tensor(out=ohs[ab][:, :, :].rearrange("p (h r) a -> p h r a", h=4),
                                                                        in0=ohs[ab][:, :, :].rearrange("p (h r) a -> p h r a", h=4),
                                                                        in1=ti4[:, :, ab, :].unsqueeze(2).to_broadcast([128, 4, 16, 16]), op=ALU.mult))(ab),
                             reads=[ohs[ab], topif], writes=[ohs[ab]])
                    for ab in range(2):
                        P.op("dve", (lambda ab: lambda e: e.tensor_reduce(out=isel[:, ab, :], in_=ohs[ab][:], axis=AX.X, op=ALU.add))(ab), reads=[ohs[ab]], writes=[(isel, ab)])
                    P.op("dve", lambda e: e.scalar_tensor_tensor(out=ef[:], in0=isel[:, 0, :], scalar=128.0, in1=isel[:, 1, :], op0=ALU.mult, op1=ALU.add),
                         reads=[isel], writes=[ef])
                    csl = slice(ps_ * 64, ps_ * 64 + 64)
                    P.op("dve", (lambda i, csl: lambda e: e.tensor_copy(out=idxst[i][:, csl], in_=ef[:]))(i, csl), reads=[ef], writes=[(idxst[i], ps_)])
                    P.op("dve", lambda e: e.tensor_tensor(out=eg[:], in0=bestv[:], in1=bestv[:, :, 0:1].to_broadcast([128, 4, 16]), op=ALU.subtract),
                         reads=[bestv], writes=[eg])
                    P.op("act", lambda e: e.activation(out=eg[:], in_=eg[:], func=AF.Exp), reads=[eg], writes=[eg])
                    P.op("dve", lambda e: e.tensor_reduce(out=gs[:, 0, :], in_=eg[:], axis=AX.X, op=ALU.add), reads=[eg], writes=[gs])
                    P.op("dve", lambda e: e.reciprocal(out=gs[:, 1, :], in_=gs[:, 0, :]), reads=[gs], writes=[gs])
                    P.op("dve", (lambda i, csl: lambda e: e.tensor_tensor(out=gatest[i][:, csl].rearrange("p (h r) -> p h r", h=4), in0=eg[:],
                                                                          in1=gs[:, 1, :].unsqueeze(2).to_broadcast([128, 4, 16]), op=ALU.mult))(i, csl),
                         reads=[eg, gs], writes=[(gatest[i], ps_)])
                    if DEBUG and ps_ == 1:
                        P.dma("sp", (lambda i: lambda e: e.dma_start(out=dbg["dbg_idx"][i * 128:(i + 1) * 128, :], in_=idxst[i][:]))(i), reads=[idxst[i]])
                        P.dma("sp", (lambda i: lambda e: e.dma_start(out=dbg["dbg_gate"][i * 128:(i + 1) * 128, :], in_=gatest[i][:]))(i), reads=[gatest[i]])
                b1_norm(0)
                b1_front(0, scsL[0])
                b1_norm(1)
                for i in range(NT):
                    if i + 2 < NT:
                        b1_norm(i + 2)
                    if i + 1 < NT:
                        b1_front(i + 1, scsL[(i + 1) % 2])
                    b1_topk(i, scsL[i % 2])
                    issue_cast()
                    issue_cast()
                    if ps_ == 0:
                        wq_stage_chunk()
                        wq_stage_chunk()
            P.barrier(skip_queue="pool")
            P.flush()
            maybe_stop(4)

        with ExitStack() as ph:
            NB = 20
            while castn[0] < NCAST:
                issue_cast()
            A2t = sb(ph, "A2t2", [128, D])
            B2t = sb(ph, "B2t2", [128, D])
            G2t = sb(ph, "G2t", [128, D])
            FWt = sb(ph, "FWt", [128, D])
            h2 = [sb(ph, f"h2c{j}", [128, D]) for j in range(2)]
            ring = [sb(ph, f"rb{j}", [128, 2 * D], BF16) for j in range(NB)]
            hraw = [sb(ph, f"hraw{j}", [128, 128]) for j in range(2)]
            hgel = [sb(ph, f"hgel{j}", [128, 128]) for j in range(2)]
            Dr = [sb(ph, f"Dr{j}", [128, 128], BF16) for j in range(8)]
            ot = [sb(ph, f"ot{j}", [128, D]) for j in range(2)]
            ssb = sb(ph, "ssb2", [128, 2])
            load_bc(A2t, 4)
            load_bc(B2t, 3)
            load_bc(G2t, 5)
            P.dma("sp", lambda e: e.dma_start(out=FWt[:], in_=bcast_row(h_fnw, D)), writes=[FWt])
            cu = cv = cd = 0
            YB = [(banks[6], banks[7]), (banks[4], banks[5])]
            NTB = NT if STAGE >= 5 else 0

            def b2_final(i):
                x1 = xstore[i]
                h2_ = h2[i % 2]
                o_ = ot[i % 2]
                Y0, Y1 = YB[i % 2]
                P.op("dve", (lambda o_: lambda e: e.tensor_tensor(out=o_[:, 0:512], in0=Y0[:, :], in1=G2t[:, 0:512], op=ALU.mult))(o_), reads=[Y0, G2t], writes=[(o_, 0)])
                P.op("dve", (lambda o_: lambda e: e.tensor_tensor(out=o_[:, 512:1024], in0=Y1[:, :], in1=G2t[:, 512:1024], op=ALU.mult))(o_), reads=[Y1, G2t], writes=[(o_, 1)])
                P.op("dve", (lambda o_, x1: lambda e: e.tensor_tensor(out=o_[:], in0=o_[:], in1=x1[:], op=ALU.add))(o_, x1), reads=[o_, x1], writes=[o_])
                P.op("act", (lambda o_, h2_: lambda e: e.activation(out=h2_[:], in_=o_[:], func=AF.Square, accum_out=ssb[:, 1:2]))(o_, h2_), reads=[o_], writes=[h2_, (ssb, "f")])
                P.op("act", lambda e: e.activation(out=ssb[:, 1:2], in_=ssb[:, 1:2], func=AF.Ln, scale=1.0 / D, bias=EPS), reads=[(ssb, "f")], writes=[(ssb, "f")])
                P.op("act", lambda e: e.activation(out=ssb[:, 1:2], in_=ssb[:, 1:2], func=AF.Exp, scale=-0.5), reads=[(ssb, "f")], writes=[(ssb, "f")])
                P.op("dve", (lambda o_: lambda e: e.scalar_tensor_tensor(out=o_[:], in0=o_[:], scalar=ssb[:, 1:2], in1=FWt[:], op0=ALU.mult, op1=ALU.mult))(o_),
                     reads=[o_, (ssb, "f"), FWt], writes=[o_])
                P.dma("sp", (lambda o_, i: lambda e: e.dma_start(out=d_out[i * 128:(i + 1) * 128, :], in_=o_[:]))(o_, i), reads=[o_])

            if NTB:
                norm2_h2(None, xstore[0], h2[0], A2t, B2t, ssb, add_eng="dve")
            for i in range(NTB):
                x1 = xstore[i]
                h2_ = h2[i % 2]
                hr, hgb, o_ = hraw[i % 2], hgel[i % 2], ot[i % 2]
                Y0, Y1 = YB[i % 2]
                for k4 in range(32):
                    if k4 == 2 and i > 0:
                        b2_final(i - 1)
                    if k4 == 16 and i + 1 < NTB:
                        norm2_h2(None, xstore[i + 1], h2[(i + 1) % 2], A2t, B2t, ssb, add_eng="dve")
                    bufs = []
                    for r in range(4):
                        k = k4 * 4 + r
                        rb = ring[cu % NB]; cu += 1
                        bufs.append(rb)
                        P.dma("pool", (lambda rb, i, k: lambda e: e.indirect_dma_start(out=rb[:], out_offset=None, in_=d_t16,
                                                                                       in_offset=bass.IndirectOffsetOnAxis(ap=idxst[i][:, k:k + 1], axis=0)))(rb, i, k),
                              reads=[idxst[i], t16b], writes=[rb])
                        P.op("dve", (lambda rb, k, hr, h2_: lambda e: e.scalar_tensor_tensor(out=rb[:, 0:D], in0=rb[:, 0:D], scalar=1.0, in1=h2_[:], op0=ALU.mult, op1=ALU.mult,
                                                                                             accum_out=hr[:, k:k + 1]))(rb, k, hr, h2_),
                             reads=[rb, h2_], writes=[(rb, "u"), (hr, k)])
                    ksl = slice(k4 * 4, k4 * 4 + 4)
                    P.op("act", (lambda hgb, hr, ksl: lambda e: e.activation(out=hgb[:, ksl], in_=hr[:, ksl], func=AF.Gelu))(hgb, hr, ksl),
                         reads=[(hr, k4 * 4 + r) for r in range(4)], writes=[(hgb, k4)])
                    P.op("dve", (lambda hgb, i, ksl: lambda e: e.tensor_tensor(out=hgb[:, ksl], in0=hgb[:, ksl], in1=gatest[i][:, ksl], op=ALU.mult))(hgb, i, ksl),
                         reads=[(hgb, k4), gatest[i]], writes=[(hgb, k4)])
                    for r in range(4):
                        k = k4 * 4 + r
                        rb = bufs[r]
                        dk_ = Dr[cd % 8]; cd += 1
                        P.op("act", (lambda dk_, hgb, k: lambda e: e.activation(out=dk_[:], in_=ident[:], func=AF.Copy, scale=hgb[:, k:k + 1]))(dk_, hgb, k),
                             reads=[ident, (hgb, k4)], writes=[dk_])
                        mm(Y0, Y0[:, :], dk_[:, :], rb[:, D:D + 512], k == 0, k == 127, [dk_, rb])
                        mm(Y1, Y1[:, :], dk_[:, :], rb[:, D + 512:2 * D], k == 0, k == 127, [dk_, rb])
            if NTB:
                b2_final(NTB - 1)
            P.wait_all_dma("sp")
            P.flush()
    except _Stop:
        pass
    top.close()
    return nc


def _host_inputs(inp):
    f = lambda a: np.ascontiguousarray(np.asarray(a, dtype=np.float32))
    x = f(inp["x"])
    c = f(inp["c"])
    ar = np.arange(128)
    q = ar[:, None]
    j = np.arange(256)[None, :]
    maskr = np.where((j > q) & (j <= q + 128), 0.0, -30000.0).astype(np.float32)
    mask0_first = maskr.copy()
    mask0_first[:, :128] = -30000.0
    common = {
        "w_ada": f(inp["w_ada"][0]), "b_ada": f(inp["b_ada"][0]).reshape(1, -1), "norm1_w": f(inp["norm1_w"][0]).reshape(1, -1),
        "w_in": f(inp["w_in"][0]), "sinks": f(inp["attn_sinks"][0]).reshape(1, -1), "gate_up": f(inp["gla_gate_up"][0]),
        "gate_bias": f(inp["gla_gate_bias"][0]).reshape(1, -1), "gla_norm_w": f(inp["gla_norm_w"][0]).reshape(1, -1),
        "w_out": f(inp["w_out"][0]), "norm2_w": f(inp["norm2_w"][0]).reshape(1, -1), "peer_wq": f(inp["peer_wq"][0]),
        "skT": f(np.transpose(np.asarray(inp["peer_subkeys"][0], dtype=np.float32).reshape(16, 128, 128), (2, 0, 1))),
        "peer_uv": np.ascontiguousarray(np.concatenate([np.asarray(inp["peer_u"][0], np.float32), np.asarray(inp["peer_v"][0], np.float32)], axis=1)), "final_norm_w": f(inp["final_norm_w"]).reshape(1, -1),
        "ident": np.eye(128, dtype=np.float32),
        "tri": (ar[:, None] <= ar[None, :]).astype(np.float32),
        "triu": (ar[:, None] > ar[None, :]).astype(np.float32),
        "iota16": np.tile(np.arange(16, dtype=np.float32)[None, :], (128, 1)),
        "maskr": maskr,
    }
    maps = []
    for core in range(NCORES):
        b, s = core // 4, core % 4
        xpre = np.zeros((NPRE * 128, D), np.float32)
        nvalid = s * TOK
        if nvalid:
            xpre[NPRE * 128 - nvalid:] = x[b, :nvalid]
        pv = np.zeros((128, NPRE), np.float32)
        pv[:, NPRE - nvalid // 128:] = 1.0 if nvalid else 0.0
        m = dict(common)
        m["x_own"] = np.ascontiguousarray(x[b, s * TOK:(s + 1) * TOK])
        m["x_pre"] = xpre
        m["pvalid"] = pv
        m["mask0"] = mask0_first if s == 0 else maskr
        m["c_t"] = np.ascontiguousarray(c[b].reshape(8, 128).T)
        maps.append(m)
    return maps


def kernel(**inputs):
    maps = _host_inputs(inputs)
    nc = build_program()
    res = run_bass_kernel_spmd(nc, maps[:RUN_CORES], core_ids=list(range(RUN_CORES)))
    out = np.zeros((2, 8192, D), np.float32)
    for core in range(RUN_CORES):
        b, s = core // 4, core % 4
        out[b, s * TOK:(s + 1) * TOK] = np.asarray(res.results[core]["out"], dtype=np.float32)
    if DEBUG:
        _dbg_out["res"] = res.results
    return out
```
